# Optimizing a Trainium2 kernel written in Bass

```python
import math
import jax, jax.numpy as jnp
from jax import lax
import numpy as np

D_MODEL = 1024
BATCH = 8
SEQ = 2048
DEPTH = 1

D_MIX = D_MODEL
SSM_WIDTH = D_MIX // 2
SSM_GROUP = 16
SSM_GROUPS = SSM_WIDTH // SSM_GROUP
SSM_STATE = 64
MLA_HEADS = 4
QK_NOPE = 128
QK_ROPE = 64
QK_DIM = QK_NOPE + QK_ROPE
V_HEAD = 128
MLA_WIDTH = MLA_HEADS * V_HEAD
Q_LORA = 256
KV_LORA = 128
PLE_DIM = 256
ROPE_THETA = 10000.0
Q_BLOCK = 128
LN_EPS = 1e-5
RMS_EPS = 1e-6
DT_MIN = 1e-3
DT_MAX = 1e-1
MAX_POS_OFFSET = 4096
DEEPNORM_ALPHA = (2 * DEPTH) ** 0.25
DEEPNORM_BETA = (8 * DEPTH) ** -0.25

IN_SPLITS = (SSM_WIDTH, SSM_WIDTH, Q_LORA, KV_LORA, QK_ROPE, MLA_WIDTH)
D_IN = sum(IN_SPLITS)
IN_SPLIT_POINTS = tuple(int(v) for v in np.cumsum(IN_SPLITS)[:-1])

kernel_name = "hymba_s5_mla_deepnorm_ple"


def layer_norm(x, g, b):
    xf = x.astype(jnp.float32)
    mu = jnp.mean(xf, axis=-1, keepdims=True)
    var = jnp.mean(jnp.square(xf - mu), axis=-1, keepdims=True)
    return ((xf - mu) * lax.rsqrt(var + LN_EPS)).astype(x.dtype) * g + b


def rms_norm(x, g):
    xf = x.astype(jnp.float32)
    return (xf * lax.rsqrt(jnp.mean(xf * xf, axis=-1, keepdims=True) + RMS_EPS)).astype(x.dtype) * g


def rope_tables(positions):
    inv_freq = 1.0 / (ROPE_THETA ** (jnp.arange(0, QK_ROPE, 2, dtype=jnp.float32) / QK_ROPE))
    ang = positions.astype(jnp.float32)[..., None] * inv_freq
    return jnp.cos(ang), jnp.sin(ang)


def rotary(x, cos, sin):
    xf = x.astype(jnp.float32)
    x1, x2 = jnp.split(xf, 2, axis=-1)
    return jnp.concatenate([x1 * cos - x2 * sin, x2 * cos + x1 * sin], axis=-1).astype(x.dtype)


def s5_branch(u, a_re, a_im, log_dt, b_re, b_im, c_re, c_im, d_skip, w_glu, b_glu):
    bsz, seqlen, _ = u.shape
    f32 = jnp.float32
    uf = u.astype(f32)
    ug = uf.reshape(bsz, seqlen, SSM_GROUPS, SSM_GROUP)
    a_re = a_re.astype(f32)
    a_im = a_im.astype(f32)
    dt = jnp.exp(log_dt.astype(f32))[:, None]
    mag = jnp.exp(dt * a_re)
    ab_re = mag * jnp.cos(dt * a_im)
    ab_im = mag * jnp.sin(dt * a_im)
    den = a_re * a_re + a_im * a_im
    nr = ab_re - 1.0
    k_re = (nr * a_re + ab_im * a_im) / den
    k_im = (ab_im * a_re - nr * a_im) / den
    bu_re = jnp.einsum('gpc,blgc->lbgp', b_re.astype(f32), ug)
    bu_im = jnp.einsum('gpc,blgc->lbgp', b_im.astype(f32), ug)
    x_re = k_re * bu_re - k_im * bu_im
    x_im = k_re * bu_im + k_im * bu_re
    a_seq_re = jnp.broadcast_to(ab_re, (seqlen, 1, SSM_GROUPS, SSM_STATE))
    a_seq_im = jnp.broadcast_to(ab_im, (seqlen, 1, SSM_GROUPS, SSM_STATE))

    def combine(left, right):
        al_re, al_im, bl_re, bl_im = left
        ar_re, ar_im, br_re, br_im = right
        return (al_re * ar_re - al_im * ar_im,
                al_re * ar_im + al_im * ar_re,
                ar_re * bl_re - ar_im * bl_im + br_re,
                ar_re * bl_im + ar_im * bl_re + br_im)

    _, _, h_re, h_im = lax.associative_scan(combine, (a_seq_re, a_seq_im, x_re, x_im), axis=0)
    y = (jnp.einsum('gcp,lbgp->blgc', c_re.astype(f32), h_re)
         - jnp.einsum('gcp,lbgp->blgc', c_im.astype(f32), h_im))
    y = y.reshape(bsz, seqlen, SSM_WIDTH) + d_skip.astype(f32) * uf
    y = jax.nn.gelu(y)
    y = y * jax.nn.sigmoid(y @ w_glu.astype(f32) + b_glu.astype(f32))
    return y.astype(u.dtype)


def mla_branch(cq, ckv, kr, cos, sin, q_norm_g, w_uq, kv_norm_g, w_ukv):
    bsz, seqlen, _ = cq.shape
    q = (rms_norm(cq, q_norm_g) @ w_uq).reshape(bsz, seqlen, MLA_HEADS, QK_DIM)
    q = jnp.concatenate([q[..., :QK_NOPE],
                         rotary(q[..., QK_NOPE:], cos[:, :, None, :], sin[:, :, None, :])], axis=-1)
    kv = (rms_norm(ckv, kv_norm_g) @ w_ukv).reshape(bsz, seqlen, MLA_HEADS, QK_NOPE + V_HEAD)
    k_nope, v = kv[..., :QK_NOPE], kv[..., QK_NOPE:]
    k_rope = rotary(kr, cos, sin)
    k = jnp.concatenate([k_nope, jnp.broadcast_to(k_rope[:, :, None, :],
                                                  (bsz, seqlen, MLA_HEADS, QK_ROPE))], axis=-1)
    scale = QK_DIM ** -0.5
    n_blocks = seqlen // Q_BLOCK
    q_blocks = q.reshape(bsz, n_blocks, Q_BLOCK, MLA_HEADS, QK_DIM).transpose(1, 0, 2, 3, 4)
    key_pos = jnp.arange(seqlen)

    def attend(args):
        qb, start = args
        s = jnp.einsum('bqhd,bkhd->bhqk', qb, k, preferred_element_type=jnp.float32) * scale
        q_pos = start + jnp.arange(Q_BLOCK)
        causal = key_pos[None, :] <= q_pos[:, None]
        s = jnp.where(causal, s, -jnp.inf)
        pr = jax.nn.softmax(s, axis=-1).astype(v.dtype)
        return jnp.einsum('bhqk,bkhd->bqhd', pr, v)

    out = lax.map(attend, (q_blocks, jnp.arange(n_blocks) * Q_BLOCK))
    return out.transpose(1, 0, 2, 3, 4).reshape(bsz, seqlen, MLA_WIDTH)


def setup_inputs(seed: int = 0) -> dict:
    key = jax.random.key(seed)
    ks = jax.random.split(key, 26)
    f32 = jnp.float32
    nrm = lambda k, shape, s: jax.random.normal(k, shape, f32) * s
    x = jax.random.normal(ks[0], (BATCH, SEQ, D_MODEL), f32)
    p = jax.random.normal(ks[1], (DEPTH, BATCH, SEQ, PLE_DIM), f32)
    offsets = jax.random.randint(ks[2], (BATCH, 1), 0, MAX_POS_OFFSET, dtype=jnp.int32)
    positions = (jnp.arange(SEQ, dtype=jnp.int32)[None, :] + offsets).astype(jnp.int32)
    ln_emb_g = 1.0 + nrm(ks[3], (D_MODEL,), 0.05)
    ln_emb_b = nrm(ks[4], (D_MODEL,), 0.02)
    w_in = nrm(ks[5], (DEPTH, D_MODEL, D_IN), D_MODEL ** -0.5)
    a_re = -0.5 * jnp.exp(nrm(ks[6], (DEPTH, SSM_GROUPS, SSM_STATE), 0.05))
    a_im = math.pi * jnp.arange(SSM_STATE, dtype=f32) + nrm(ks[7], (DEPTH, SSM_GROUPS, SSM_STATE), 0.05)
    log_dt = jax.random.uniform(ks[8], (DEPTH, SSM_GROUPS), f32, math.log(DT_MIN), math.log(DT_MAX))
    b_re = nrm(ks[9], (DEPTH, SSM_GROUPS, SSM_STATE, SSM_GROUP), (2 * SSM_GROUP) ** -0.5)
    b_im = nrm(ks[10], (DEPTH, SSM_GROUPS, SSM_STATE, SSM_GROUP), (2 * SSM_GROUP) ** -0.5)
    c_re = nrm(ks[11], (DEPTH, SSM_GROUPS, SSM_GROUP, SSM_STATE), SSM_STATE ** -0.5)
    c_im = nrm(ks[12], (DEPTH, SSM_GROUPS, SSM_GROUP, SSM_STATE), SSM_STATE ** -0.5)
    d_skip = nrm(ks[13], (DEPTH, SSM_WIDTH), 1.0)
    w_glu = nrm(ks[14], (DEPTH, SSM_WIDTH, SSM_WIDTH), SSM_WIDTH ** -0.5)
    b_glu = nrm(ks[15], (DEPTH, SSM_WIDTH), 0.02)
    q_norm_g = 1.0 + nrm(ks[16], (DEPTH, Q_LORA), 0.05)
    w_uq = nrm(ks[17], (DEPTH, Q_LORA, MLA_HEADS * QK_DIM), Q_LORA ** -0.5)
    kv_norm_g = 1.0 + nrm(ks[18], (DEPTH, KV_LORA), 0.05)
    w_ukv = nrm(ks[19], (DEPTH, KV_LORA, MLA_HEADS * (QK_NOPE + V_HEAD)), KV_LORA ** -0.5)
    w_out = nrm(ks[20], (DEPTH, D_MIX, D_MODEL), D_MIX ** -0.5 * DEEPNORM_BETA)
    w_pg = nrm(ks[21], (DEPTH, D_MODEL, D_MODEL), D_MODEL ** -0.5)
    w_pp = nrm(ks[22], (DEPTH, PLE_DIM, D_MODEL), PLE_DIM ** -0.5 * DEEPNORM_BETA)
    ln_g = 1.0 + nrm(ks[23], (DEPTH, D_MODEL), 0.05)
    ln_b = nrm(ks[24], (DEPTH, D_MODEL), 0.02)
    return {"x": x, "p": p, "positions": positions, "ln_emb_g": ln_emb_g, "ln_emb_b": ln_emb_b,
            "w_in": w_in, "a_re": a_re, "a_im": a_im, "log_dt": log_dt,
            "b_re": b_re, "b_im": b_im, "c_re": c_re, "c_im": c_im, "d_skip": d_skip,
            "w_glu": w_glu, "b_glu": b_glu, "q_norm_g": q_norm_g, "w_uq": w_uq,
            "kv_norm_g": kv_norm_g, "w_ukv": w_ukv, "w_out": w_out, "w_pg": w_pg,
            "w_pp": w_pp, "ln_g": ln_g, "ln_b": ln_b}


def reference(x, p, positions, ln_emb_g, ln_emb_b, w_in, a_re, a_im, log_dt,
              b_re, b_im, c_re, c_im, d_skip, w_glu, b_glu, q_norm_g, w_uq,
              kv_norm_g, w_ukv, w_out, w_pg, w_pp, ln_g, ln_b):
    x = layer_norm(x, ln_emb_g, ln_emb_b)
    cos, sin = rope_tables(positions)
    for i in range(DEPTH):
        z = x @ w_in[i]
        xs, gs, cq, ckv, kr, gm = jnp.split(z, IN_SPLIT_POINTS, axis=-1)
        ys = s5_branch(xs, a_re[i], a_im[i], log_dt[i], b_re[i], b_im[i], c_re[i], c_im[i],
                       d_skip[i], w_glu[i], b_glu[i]) * jax.nn.silu(gs)
        ym = mla_branch(cq, ckv, kr, cos, sin, q_norm_g[i], w_uq[i], kv_norm_g[i],
                        w_ukv[i]) * jax.nn.silu(gm)
        mix = jnp.concatenate([ys, ym], axis=-1) @ w_out[i]
        u = DEEPNORM_ALPHA * x + mix
        ple = jax.nn.sigmoid(u @ w_pg[i]) * (p[i] @ w_pp[i])
        x = layer_norm(u + ple, ln_g[i], ln_b[i])
    return x
```

```python
import math
import contextlib
import numpy as np
import ml_dtypes
import concourse.bass as bass
import concourse.mybir as mybir
from concourse.bass_utils import run_bass_kernel_spmd

F32 = mybir.dt.float32
I32 = mybir.dt.int32
BF16 = mybir.dt.bfloat16
ALU = mybir.AluOpType
AF = mybir.ActivationFunctionType

L = 2048
D = 1024
NCORES = 8
TWO_PI = 2.0 * math.pi
C2PI = TWO_PI * (1.0 - 1e-6)
LN_EPS = 1e-5
RMS_EPS = 1e-6
ALPHA = 2.0 ** 0.25
SCALE = 192.0 ** -0.5
KV = [0.0] + [-float(s) for s in range(1, 8)] + [float(t) for t in range(0, 8)] + [8.0]
NCF = 576


class Buf:
    __slots__ = ("name", "w", "r", "dsem", "dcnt")

    def __init__(self, name):
        self.name = name
        self.w = None
        self.r = []
        self.dsem = None
        self.dcnt = 0


class Sched:
    def __init__(self, nc, es):
        self.nc = nc
        self.es = es
        self.E = {"pe": nc.tensor, "act": nc.scalar, "dve": nc.vector, "pool": nc.gpsimd, "sp": nc.sync}
        self.sem = {k: es.enter_context(nc.semaphore("prog_" + k)) for k in self.E}
        self.cnt = {k: 0 for k in self.E}
        self.seen = {k: {} for k in self.E}
        self.pe_pending = []
        self.nsem = 0
        self.rec = None
        self.snap = {}
        self.age = {}
        self.clock = 0

    def _wait(self, eng, tok):
        if tok is None:
            return
        sem, val = tok
        if eng == "pe" and sem is self.sem["pe"]:
            return
        key = sem.name
        if self.seen[eng].get(key, 0) >= val:
            return
        self.seen[eng][key] = val
        self.E[eng].wait_ge(sem, val)

    def _deps(self, eng, r, w):
        for b in r:
            self._wait(eng, b.w)
        for b in w:
            self._wait(eng, b.w)
            for t in b.r:
                self._wait(eng, t)

    def op(self, eng, fn, r=(), w=(), inc=True):
        if self.rec is not None:
            r = list(r); w = list(w)
            self.rec.append(lambda: self._op(eng, fn, r, w, inc))
            return None
        return self._op(eng, fn, r, w, inc)

    def _collect(self, eng, r, w):
        need = {}

        def add(tok):
            if tok is None:
                return
            sem, val = tok
            if eng == "pe" and sem is self.sem["pe"]:
                return
            if self.seen[eng].get(sem.name, 0) >= val:
                return
            if sem.name not in need or need[sem.name][1] < val:
                need[sem.name] = (sem, val)

        cand = [b.w for b in r]
        for b in w:
            cand.append(b.w)
            cand.extend(b.r)
        cand = [t for t in cand if t is not None]
        cand.sort(key=lambda t: -t[1])
        for tok in cand:
            before = len(need)
            had = need.get(tok[0].name)
            add(tok)
            if need.get(tok[0].name) is not had or len(need) != before:
                self.seen[eng][tok[0].name] = max(self.seen[eng].get(tok[0].name, 0), tok[1])
                snap = self.snap.get((tok[0].name, tok[1]))
                if snap:
                    se = self.seen[eng]
                    for k_, v_ in snap.items():
                        if se.get(k_, 0) < v_:
                            se[k_] = v_
        for name, (sem, val) in need.items():
            self.seen[eng][name] = max(self.seen[eng].get(name, 0), val)
        return sorted(need.values(), key=lambda t: self.age.get((t[0].name, t[1]), 0))

    def _op(self, eng, fn, r=(), w=(), inc=True):
        toks = self._collect(eng, r, w)
        for (sem, val) in toks[:-1]:
            self.E[eng].wait_ge(sem, val)
        ins = fn(self.E[eng])
        if toks:
            ins._wait_ge(toks[-1][0], toks[-1][1])
        if inc:
            self.cnt[eng] += 1
            ins.then_inc(self.sem[eng], 1)
            tok = (self.sem[eng], self.cnt[eng])
            self.snap[(tok[0].name, tok[1])] = dict(self.seen[eng])
            self.clock += 1
            self.age[(tok[0].name, tok[1])] = self.clock
            for b in r:
                b.r.append(tok)
            for b in w:
                b.w = tok
                b.r = []
            if eng == "pe":
                for b in self.pe_pending:
                    b.r.append(tok)
                self.pe_pending = []
        else:
            assert eng == "pe"
            self.pe_pending.extend(r)
            for b in w:
                b.r = []
        return ins

    def dma(self, q, out, in_, r=(), w=(), **kw):
        if self.rec is not None:
            r = list(r); w = list(w)
            self.rec.append(lambda: self._dma(q, out, in_, r, w, **kw))
            return None
        return self._dma(q, out, in_, r, w, **kw)

    def _dma(self, q, out, in_, r=(), w=(), **kw):
        dst = w[0]
        for b in r:
            self._wait(q, b.w)
        for b in w:
            if not (b.w is not None and b.dsem is not None and b.w[0] is b.dsem):
                self._wait(q, b.w)
            for t in b.r:
                self._wait(q, t)
        if dst.dsem is None:
            dst.dsem = self.es.enter_context(self.nc.semaphore("d%d_%s" % (self.nsem, dst.name)))
            dst.dcnt = [0]
            self.nsem += 1
        for b in w:
            if b.dsem is None:
                b.dsem = dst.dsem
                b.dcnt = dst.dcnt
        ins = self.E[q].dma_start(out=out, in_=in_, **kw)
        ins.then_inc(dst.dsem, 16)
        dst.dcnt[0] += 16
        tok = (dst.dsem, dst.dcnt[0])
        self.snap[(tok[0].name, tok[1])] = dict(self.seen[q])
        self.clock += 1
        self.age[(tok[0].name, tok[1])] = self.clock
        for b in r:
            b.r.append(tok)
        for b in w:
            b.w = tok
            b.r = []
        return tok

    def barrier(self):
        toks = [(self.sem[k], self.cnt[k]) for k in self.E if self.cnt[k] > 0]
        for e in self.E:
            for t in toks:
                if t[0] is not self.sem[e]:
                    self._wait(e, t)


def host_consts():
    cf = np.zeros((128, NCF), np.float32)
    cf[:, 0:128] = np.eye(128, dtype=np.float32)
    r = np.arange(128)
    cf[:, 128:256] = (r[None, :] // 16 >= r[:, None] // 16).astype(np.float32)
    cf[:, 256:512] = np.arange(256, dtype=np.float32)[None, :]
    cf[:, 512:529] = np.array(KV, np.float32)[None, :]
    cf[:, 529:546] = (np.array(KV, np.float64) / TWO_PI).astype(np.float32)[None, :]
    inv_freq = 1.0 / (10000.0 ** (np.arange(0, 64, 2, dtype=np.float64) / 64.0))
    cf[:, 546] = (inv_freq[r % 32] / TWO_PI).astype(np.float32)
    cf[:, 547] = np.where(r < 64, -1.0, 1.0)
    cf[:, 548] = math.pi / 2
    cf[:, 549] = np.where(r < 64, 1.0, -1.0)
    cf[:, 550] = C2PI
    cf[:, 551] = np.where(r < 64, 1.0, -1.0) * C2PI
    cf[:, 552] = np.where((r % 64) < 32, -1.0, 1.0) * C2PI
    cf[:, 553] = LN_EPS
    cf[:, 554] = RMS_EPS
    cf[:, 555] = 0.0
    cf[:, 556] = 1.0
    cb = np.zeros((128, 512), np.float32)
    cb[:, 0:128] = np.eye(128)
    cb[:, 128:256] = (r[:, None] == (r[None, :] + 64) % 128)
    cb[:, 256:384] = np.where(r[None, :] >= r[:, None], 0.0, -10000.0)
    cb[:, 384:512] = 1.0
    return cf, cb.astype(ml_dtypes.bfloat16)


def build(stage=99, debug=False):
    nc = bass.Bass("TRN2", target_bir_lowering=False)

    def din(name, shape, dt=F32):
        return nc.dram_tensor(name, list(shape), dt, kind="ExternalInput").ap()

    x_d = din("x", [L, D])
    p_d = din("p", [L, 256])
    pos_d = din("pos", [1, L], I32)
    cf_d = din("cf", [128, NCF])
    cb_d = din("cb", [128, 512], BF16)
    g1_d = din("ln_emb_g", [1, D]); b1_d = din("ln_emb_b", [1, D])
    g2_d = din("ln_g", [1, D]); b2_d = din("ln_b", [1, D])
    win_d = din("w_in", [D, 1984])
    are_d = din("a_re", [32, 64]); aim_d = din("a_im", [32, 64]); ldt_d = din("log_dt", [1, 32])
    bre_d = din("b_re", [32, 64, 16]); bim_d = din("b_im", [32, 64, 16])
    cre_d = din("c_re", [512, 64]); cim_d = din("c_im", [512, 64])
    dsk_d = din("d_skip", [32, 16])
    wglu_d = din("w_glu", [512, 512]); bglu_d = din("b_glu", [512, 1])
    qg_d = din("q_norm_g", [256, 1]); wuq_d = din("w_uq", [256, 768])
    kvg_d = din("kv_norm_g", [128, 1]); wukv_d = din("w_ukv", [128, 1024])
    wout_d = din("w_out", [D, D]); wpg_d = din("w_pg", [D, D]); wpp_d = din("w_pp", [256, D])
    out_d = nc.dram_tensor("out", [L, D], F32, kind="ExternalOutput").ap()
    dbg_d = None
    if debug:
        dbg_d = nc.dram_tensor("dbg", [128, 8 * 2048], F32, kind="ExternalOutput").ap()

    es = contextlib.ExitStack()
    with es:
        S = Sched(nc, es)
        ARENA = 211968
        arena = es.enter_context(nc.sbuf_tensor("arena", [128, ARENA // 2], BF16))
        psum = [es.enter_context(nc.psum_tensor("ps%d" % i, [128, 512], F32)) for i in range(8)]
        PB = [Buf("psb%d" % i) for i in range(8)]

        def view(off, shape, dt):
            n = 1
            for s in shape:
                n *= s
            esz = 2 if dt == BF16 else 4
            assert off % 4 == 0 and off + n * esz <= ARENA, (off, shape)
            a = arena[:, off // 2: off // 2 + n * esz // 2]
            if dt != BF16:
                a = a.bitcast(dt)
            if len(shape) == 2:
                a = a.rearrange("p (a b) -> p a b", a=shape[0])
            elif len(shape) == 3:
                a = a.rearrange("p (a b c) -> p a b c", a=shape[0], b=shape[1])
            return a

        cf = view(0, [NCF], F32)
        cb = view(2304, [512], BF16)
        mixT = view(3328, [8, L], BF16)
        tmp = [view(36096 + 2048 * i, [512], F32) for i in range(4)]
        TB = [Buf("tmp%d" % i) for i in range(4)]
        PT = [view(44288 + 1024 * i, [512], BF16) for i in range(3)]
        PTB = [Buf("pt%d" % i) for i in range(3)]
        small = view(47360, [960], F32)
        ident_f = cf[:, 0:128]
        ident_b = cb[:, 0:128]
        pswap_b = cb[:, 128:256]
        trimask = cb[:, 256:384]
        ones_b = cb[:, 384:512]
        ZO = 51200
        xsT = view(ZO, [4, L], BF16)
        krT = view(ZO + 16384, [L], BF16)
        cqT = view(ZO + 20480, [2, L], BF16)
        ckvT = view(ZO + 28672, [L], BF16)
        COSr = view(ZO + 32768, [L], F32)
        SINr = view(ZO + 40960, [L], F32)
        WO = 100352

        B_const = Buf("const")
        B_mix = [[Buf("mix%d_%d" % (k, c)) for c in range(4)] for k in range(8)]
        B_xs = [Buf("xs%d" % q) for q in range(4)]
        B_cq = [Buf("cq%d" % c) for c in range(4)]
        B_ckv = [Buf("ckv%d" % c) for c in range(4)]
        B_kr = [Buf("kr%d" % c) for c in range(4)]
        B_rope = Buf("rope")
        B_small = Buf("small")

        S.dma("sp", cf, cf_d, w=[B_const])
        S.dma("sp", cb, cb_d, w=[B_const])

        def act(out, in_, func, r, w, scale=1.0, bias=None, eng="act"):
            if bias is None:
                return S.op("act", lambda e: e.activation(out=out, in_=in_, func=func, scale=scale), r, w)
            return S.op("act", lambda e: e.activation(out=out, in_=in_, func=func, scale=scale, bias=bias), r, w)

        def tt(eng, out, in0, in1, op, r, w):
            return S.op(eng, lambda e: e.tensor_tensor(out=out, in0=in0, in1=in1, op=op), r, w)

        def ts(eng, out, in0, s1, s2, op0, op1, r, w):
            if s2 is None:
                return S.op(eng, lambda e: e.tensor_scalar(out=out, in0=in0, scalar1=s1, scalar2=None, op0=op0), r, w)
            return S.op(eng, lambda e: e.tensor_scalar(out=out, in0=in0, scalar1=s1, scalar2=s2, op0=op0, op1=op1), r, w)

        def cp(eng, out, in_, r, w):
            if eng == "act":
                return S.op("act", lambda e: e.copy(out=out, in_=in_), r, w)
            return S.op(eng, lambda e: e.tensor_copy(out=out, in_=in_), r, w)

        def mm(out, lhsT, rhs, start, stop, r, w, inc=None):
            if inc is None:
                inc = stop
            return S.op("pe", lambda e: e.matmul(out, lhsT=lhsT, rhs=rhs, start=start, stop=stop), r, w, inc=inc)

        def tr(out, in_, r, w, inc=True):
            return S.op("pe", lambda e: e.transpose(out=out, in_=in_, identity=ident_f), r, w, inc=inc)

        def sincos(eng, y, k_i32, fr, afr, sin_out, cos_out, sin_scale, r, w_tmp, w_out):
            cp(eng, k_i32, y, r + w_tmp, w_tmp)
            tt(eng, fr, y, k_i32, ALU.subtract, w_tmp, w_tmp)
            act(afr, fr, AF.Abs, w_tmp, w_tmp)
            act(sin_out, fr, AF.Sin, w_tmp, w_out, scale=sin_scale)
            act(cos_out, afr, AF.Sin, w_tmp, w_out, scale=-C2PI, bias=cf[:, 548:549])

        def ln_stats(xin, stat, r, rb, nmr=False):
            st = stat[:, 0:12].rearrange("p (a b) -> p a b", a=2)
            S.op("dve", lambda e: e.bn_stats(out=st[:, 0, :], in_=xin[:, 0:512]), r, [rb])
            S.op("dve", lambda e: e.bn_stats(out=st[:, 1, :], in_=xin[:, 512:1024]), r, [rb])
            S.op("dve", lambda e: e.bn_aggr(out=stat[:, 12:14], in_=stat[:, 0:12]), [rb], [rb])
            act(stat[:, 14:15], stat[:, 13:14], AF.Ln, [rb, B_const], [rb], bias=cf[:, 553:554])
            act(stat[:, 15:16], stat[:, 14:15], AF.Exp, [rb], [rb], scale=-0.5)
            if nmr:
                ts("dve", stat[:, 16:17], stat[:, 12:13], stat[:, 15:16], -1.0, ALU.mult, ALU.mult, [rb], [rb])

        def ln_apply(xin, xout, gt, bt, stat, r, w, rb, beng="pool", norm="dve"):
            if norm == "act":
                act(xout, xin, AF.Identity, r + [rb], w, scale=stat[:, 15:16], bias=stat[:, 16:17])
            else:
                ts("dve", xout, xin, stat[:, 12:13], stat[:, 15:16], ALU.subtract, ALU.mult, r + [rb], w)
            tt("dve", xout, xout, gt, ALU.mult, w, w)
            tt(beng, xout, xout, bt, ALU.add, w, w)

        xt = [view(WO + 4096 * i, [D], F32) for i in range(2)]
        xn = [view(WO + 8192 + 4096 * i, [D], F32) for i in range(2)]
        g1 = view(WO + 16384, [D], F32)
        b1 = view(WO + 20480, [D], F32)
        xnT = view(WO + 24576, [8, L], BF16)
        wall = [view(WO + 57344 + 2048 * i, [8, 128], BF16) for i in range(16)]
        wkr = view(WO + 57344 + 2048 * 16, [8, 256], BF16)
        rt0 = view(ZO, [L], F32)
        rt1 = view(ZO + 8192, [L], F32)
        rt2 = view(ZO + 20480, [L], F32)
        B_xt = [Buf("xt0"), Buf("xt1")]
        B_xn = [Buf("xn0"), Buf("xn1")]
        B_g1 = Buf("g1b1")
        B_xnT = [Buf("xnT%d" % c) for c in range(4)]
        B_wall = [Buf("wall%d" % i) for i in range(16)]
        B_wkr = Buf("wkr")
        B_rt = Buf("rt")
        stat = [small[:, 0:20], small[:, 20:40]]
        B_stat = [Buf("stat0"), Buf("stat1")]

        S.dma("sp", g1, g1_d.partition_broadcast(128)[:, 0, :], w=[B_g1])
        S.dma("sp", b1, b1_d.partition_broadcast(128)[:, 0, :], w=[B_g1])

        mtiles = [("xs", q, 128 * q) for q in range(4)] + [("gs", q, 512 + 128 * q) for q in range(4)] + \
                 [("cq", j, 1024 + 128 * j) for j in range(2)] + [("ckv", 0, 1280)] + \
                 [("gm", h, 1472 + 128 * h) for h in range(4)] + [("kr", 0, 1408)]
        win_v = win_d.rearrange("(k p) n -> p k n", p=128)
        def wload(mi):
            kind, idx, col = mtiles[mi]
            if kind != "kr":
                S.dma("pool", wall[mi], win_v[:, :, col:col + 128], w=[B_wall[mi]])
            else:
                S.dma("pool", wall[mi][:, :, 0:64], win_v[:, :, col:col + 64], w=[B_wall[mi]])

        def wkr_build():
            krsl = wall[15][:, :, 0:64]
            for (o, s0, wd) in [(0, 0, 64), (64, 0, 64), (128, 32, 32), (160, 0, 32), (192, 32, 32), (224, 0, 32)]:
                cp("pool", wkr[:, :, o:o + wd], krsl[:, :, s0:s0 + wd], [B_wall[15]], [B_wkr])

        for mi in (15, 0, 1, 2, 3):
            wload(mi)

        def p1_a1(i):
            s_ = i % 2
            S.dma("sp", xt[s_], x_d[i * 128:(i + 1) * 128, :], w=[B_xt[s_]])
            ln_stats(xt[s_], stat[s_], [B_xt[s_]], B_stat[s_])

        def p1_a2(i):
            s_ = i % 2
            ln_apply(xt[s_], xn[s_], g1, b1, stat[s_], [B_xt[s_], B_g1], [B_xn[s_]], B_stat[s_], beng="dve")

        def p1_b(i):
            s_ = i % 2
            for hb in range(2):
                pb = hb
                for j in range(4):
                    k = hb * 4 + j
                    tr(psum[pb][:, j * 128:(j + 1) * 128], xn[s_][:, k * 128:(k + 1) * 128],
                       [B_xn[s_], B_const], [PB[pb]], inc=(j == 3))
                dst = xnT[:, hb * 4:hb * 4 + 4, i * 128:(i + 1) * 128]
                src = psum[pb].rearrange("p (a b) -> p a b", a=4)
                cp("act", dst, src, [PB[pb]], [B_xnT[i // 4]])

        xs_v = xsT.rearrange("p q (t c) -> p q t c", t=8)
        rot2 = [0]

        def p2_unit(mi, cc):
            kind, idx, col = mtiles[mi]
            cs = slice(cc * 512, (cc + 1) * 512)
            if kind != "kr":
                lhs_list = [(wall[mi], B_wall[mi], None)]
            else:
                lhs_list = [(wkr[:, :, 0:128], B_wkr, 6), (wkr[:, :, 128:256], B_wkr, 7)]
            pbs = []
            for (lw, lb, fixed) in lhs_list:
                if fixed is None:
                    pb = 2 + rot2[0] % 4
                    rot2[0] += 1
                else:
                    pb = fixed
                pbs.append(pb)
                for k in range(8):
                    mm(psum[pb][:, :], lw[:, k, :], xnT[:, k, cs], k == 0, k == 7, [lb, B_xnT[cc]], [PB[pb]])
            pb = pbs[0]
            if kind == "xs":
                src = psum[pb].rearrange("p (c t) -> p t c", t=8)
                cp("dve", xs_v[:, idx, :, cc * 64:(cc + 1) * 64], src, [PB[pb]], [B_xs[idx], B_rt])
            elif kind == "gs":
                act(mixT[:, idx, cs], psum[pb][:, :], AF.Silu, [PB[pb]], [B_mix[idx][cc]])
            elif kind == "gm":
                act(mixT[:, 4 + idx, cs], psum[pb][:, :], AF.Silu, [PB[pb]], [B_mix[4 + idx][cc]])
            elif kind == "cq":
                cp("dve", cqT[:, idx, cs], psum[pb][:, :], [PB[pb]], [B_cq[cc], B_rt])
            elif kind == "ckv":
                cp("dve", ckvT[:, cs], psum[pb][:, :], [PB[pb]], [B_ckv[cc]])
            elif kind == "kr":
                tt("dve", tmp[0], psum[pbs[0]][:, :], COSr[:, cs], ALU.mult, [PB[pbs[0]], B_rope], [TB[0]])
                tt("dve", tmp[1], psum[pbs[1]][:, :], SINr[:, cs], ALU.mult, [PB[pbs[1]], B_rope], [TB[1]])
                tt("dve", krT[:, cs], tmp[0], tmp[1], ALU.add, [TB[0], TB[1]], [B_kr[cc]])

        def rope_gen():
            S.dma("sp", rt0.bitcast(I32), pos_d.partition_broadcast(128)[:, 0, :], w=[B_rt])
            cp("dve", rt1, rt0.bitcast(I32), [B_rt], [B_rt])
            ts("dve", rt1, rt1, cf[:, 546:547], None, ALU.mult, None, [B_rt, B_const], [B_rt])
            sincos("dve", rt1, rt0.bitcast(I32), rt2, rt0, SINr, COSr, cf[:, 552:553], [B_const], [B_rt], [B_rope])

        units = {cc: [(mi, cc) for mi in range(16)] for cc in range(4)}
        split = [4, 4, 4, 4]
        rope_gen()
        for step in range(16 + 2):
            if 0 <= step - 2 < 16:
                p1_b(step - 2)
            if 0 <= step - 1 < 16:
                p1_a2(step - 1)
            if step < 16:
                p1_a1(step)
            if step == 1:
                for mi in (4, 5, 6, 7):
                    wload(mi)
            if step == 2:
                for mi in (8, 9, 10, 11):
                    wload(mi)
            if step == 3:
                for mi in (12, 13, 14):
                    wload(mi)
                wkr_build()
            t_done = step - 2
            cprev = ((t_done + 1) // 4) - 1 if t_done >= 0 else -1
            if t_done >= 0 and 0 <= cprev < 3:
                part = (t_done + 1) % 4
                a0 = sum(split[:part])
                for (mi, cc) in units[cprev][a0:a0 + split[part]]:
                    p2_unit(mi, cc)
        xs_scr = nc.dram_tensor("xs_scr", [512, L], BF16, kind="ExternalOutput").ap()
        y_scr = nc.dram_tensor("y_scr", [512, L], BF16, kind="ExternalOutput").ap()
        B_xscr = [Buf("xscr%d" % q) for q in range(4)]
        B_yscr = [Buf("yscr%d" % q) for q in range(4)]
        xs_scr_v = xs_scr.rearrange("(g c) (t n) -> t c g n", c=16, t=8)
        y_scr_v = y_scr.rearrange("(g c) (t n) -> t c g n", c=16, t=8)
        U3 = []
        S.rec = U3
        for (mi, cc) in units[3]:
            p2_unit(mi, cc)
            if mtiles[mi][0] == "xs":
                q = mtiles[mi][1]
                S.dma("sp", xs_scr[q * 128:(q + 1) * 128, :], xsT[:, q, :], r=[B_xs[q]], w=[B_xscr[q]])
        S.rec = None
        if debug and stage == 2:
            for f_ in U3:
                f_()

        if debug and stage == 2:
            S.barrier()
            dv = dbg_d.rearrange("p (a b) -> p a b", a=8)
            B_dbg = Buf("dbg")
            srcs = [xsT[:, 0, :], xsT[:, 3, :], cqT[:, 0, :], cqT[:, 1, :], ckvT, krT, mixT[:, 1, :], mixT[:, 6, :]]
            for n_, sv in enumerate(srcs):
                for cc in range(4):
                    cs = slice(cc * 512, (cc + 1) * 512)
                    cp("dve", tmp[cc % 4], sv[:, cs], [], [TB[cc % 4]])
                    S.dma("sp", dv[:, n_, cs], tmp[cc % 4], r=[TB[cc % 4]], w=[B_dbg])
            S._wait("sp", B_dbg.w)
            return nc


        S.barrier()
        cur = [WO]

        def alloc(shape, dt):
            n = 1
            for v_ in shape:
                n *= v_
            esz = 2 if dt == BF16 else 4
            off = cur[0]
            cur[0] += (n * esz + 3) // 4 * 4
            return view(off, shape, dt)

        B_s5 = Buf("s5par")
        areT = alloc([32], F32); aimT = alloc([32], F32); ldt = alloc([32], F32); dtt = alloc([32], F32)
        lam = alloc([32], F32); th = alloc([32], F32)
        Bst = alloc([32, 16], F32); Bsw = alloc([32, 16], F32); Cst = alloc([32, 16], F32); Csw = alloc([32, 16], F32)
        s_a = [alloc([32], F32) for _ in range(8)]
        FXre = alloc([8, 32], F32); FXim = alloc([8, 32], F32); FZre = alloc([8, 32], F32); FZim = alloc([8, 32], F32)
        FXims = alloc([8, 32], F32); FZims = alloc([8, 32], F32); FZimB = alloc([8, 32], F32)
        FY9imn = alloc([9, 32], F32)
        f_t = [small[:, 448:704].rearrange("p (a b) -> p a b", a=8), small[:, 704:960].rearrange("p (a b) -> p a b", a=8)]
        Dst = alloc([32], F32)
        Pre = alloc([17, 32], F32); Pim = alloc([17, 32], F32)
        mag8 = alloc([32], F32); fr16 = alloc([32], F32)
        assert cur[0] <= WO + 24576, cur[0] - WO
        T0 = WO + 94208
        cur[0] = T0
        EL = alloc([17, 32], F32); Ee = alloc([17, 32], F32); Y1 = alloc([17, 32], F32); K1 = alloc([17, 32], I32)
        FR = alloc([17, 32], F32); AFR = alloc([17, 32], F32); SN = alloc([17, 32], F32); CS = alloc([17, 32], F32)
        assert cur[0] <= ARENA
        cur[0] = WO + 24576
        Xg = alloc([8, 8, 16], F32); Yg = alloc([8, 9, 16], F32); Zg = alloc([8, 8, 16], F32); Zs = alloc([8, 8, 16], F32)
        gt1 = alloc([8, 8, 16], F32); gt2 = alloc([8, 9, 16], F32); gt3 = Zs
        Wtoep = [alloc([8, 128], BF16) for _ in range(2)]; Wend = [alloc([8, 128], BF16) for _ in range(2)]
        Wendsw = [alloc([8, 128], BF16) for _ in range(2)]; Wc = [alloc([8, 8, 16], BF16) for _ in range(2)]
        Uu = [alloc([8, 256], BF16) for _ in range(2)]; Hprev = alloc([8, 256], BF16)
        st4 = [alloc([256], F32) for _ in range(2)]
        Gb = [alloc([256], BF16) for _ in range(2)]
        assert cur[0] <= WO + 90112
        wglu = view(WO + 90112, [4, 512], BF16)
        bglu = small[:, 404:408]
        st1 = [tmp[0][:, 0:256], tmp[0][:, 256:512]]; st2 = [tmp[1][:, 0:256], tmp[1][:, 256:512]]
        sxr = [tmp[2][:, 0:256], tmp[2][:, 256:512]]; st3 = [tmp[3][:, 0:256], tmp[3][:, 256:512]]
        B_wglu = Buf("wglu"); B_bglu = Buf("bglu")

        SU = []
        S.rec = SU
        B_sta = [Buf("sta0"), Buf("sta1")]
        sta = [view(T0, [128], F32), view(T0 + 512, [128], F32)]
        ccs = [[view(T0 + 1024 + 1024 * q, [128], F32), view(T0 + 1536 + 1024 * q, [128], F32)] for q in range(4)]
        B_ccs = [Buf("ccs%d" % q) for q in range(4)]
        B_dst = Buf("dst"); B_bld = Buf("bld")
        for i_, src in enumerate([are_d, aim_d]):
            S.dma("sp", sta[i_][0:32, 0:64], src, w=[B_sta[i_]])
            S.dma("sp", sta[i_][0:32, 64:128], src, w=[B_sta[i_]])
        S.dma("sp", ldt, ldt_d.partition_broadcast(128)[:, 0, :], w=[B_bld])
        for q in range(4):
            S.dma("sp", ccs[q][0][:, 0:64], cre_d[q * 128:(q + 1) * 128, :], w=[B_ccs[q]])
            S.dma("sp", ccs[q][0][:, 64:128], cim_d[q * 128:(q + 1) * 128, :], w=[B_ccs[q]])
            S.dma("sp", ccs[q][1][:, 0:64], cim_d[q * 128:(q + 1) * 128, :], w=[B_ccs[q]])
            S.dma("sp", ccs[q][1][:, 64:128], cre_d[q * 128:(q + 1) * 128, :], w=[B_ccs[q]])
        S.dma("sp", Bst[0:64], bre_d.rearrange("g p c -> p g c"), w=[B_bld])
        S.dma("sp", Bst[64:128], bim_d.rearrange("g p c -> p g c"), w=[B_bld])
        S.dma("sp", Bsw[0:64], bim_d.rearrange("g p c -> p g c"), w=[B_bld])
        S.dma("sp", Bsw[64:128], bre_d.rearrange("g p c -> p g c"), w=[B_bld])
        for t_ in range(8):
            S.dma("sp", Dst[t_ * 16:(t_ + 1) * 16, :], dsk_d.rearrange("g c -> c g"), w=[B_dst],
                  allow_slow_non_contiguous=True)
        for i_, dstT in enumerate([areT, aimT]):
            S.op("pe", lambda e, i_=i_: e.transpose(out=psum[i_][:, 0:32], in_=sta[i_][0:32, :], identity=ident_f[0:32, 0:32]),
                 [B_sta[i_], B_const], [PB[i_]])
            cp("dve", dstT, psum[i_][:, 0:32], [PB[i_]], [B_s5])
        Cst_f = Cst.rearrange("p g c -> p (g c)")
        Csw_f = Csw.rearrange("p g c -> p (g c)")
        for q in range(4):
            pa, pb_ = 0, 1
            tr(psum[pa][:, 0:128], ccs[q][0], [B_ccs[q], B_const], [PB[pa]])
            tr(psum[pb_][:, 0:128], ccs[q][1], [B_ccs[q]], [PB[pb_]])
            cp("dve", Cst_f[:, q * 128:(q + 1) * 128], psum[pa][:, 0:128], [PB[pa]], [B_s5])
            cp("act", Csw_f[:, q * 128:(q + 1) * 128], psum[pb_][:, 0:128], [PB[pb_]], [B_s5])
        ts("dve", Cst[64:128], Cst[64:128], -1.0, None, ALU.mult, None, [B_s5], [B_s5])
        for bb in B_sta + B_ccs:
            B_s5.r.extend(bb.r)

        P5 = [B_s5, B_const, B_bld]

        def v(out, a, b, op, eng="dve"):
            tt(eng, out, a, b, op, P5, [B_s5])

        act(dtt, ldt, AF.Exp, P5, [B_s5])
        v(lam, dtt, areT, ALU.mult)
        v(th, dtt, aimT, ALU.mult)
        bc17 = lambda a: a.unsqueeze(1).to_broadcast([128, 17, 32])
        kvb = cf[:, 512:529].unsqueeze(2).to_broadcast([128, 17, 32])
        kv2b = cf[:, 529:546].unsqueeze(2).to_broadcast([128, 17, 32])
        v(EL, bc17(lam), kvb, ALU.mult)
        act(Ee, EL, AF.Exp, P5, [B_s5])
        v(Y1, bc17(th), kv2b, ALU.mult)
        cp("dve", K1, Y1, P5, [B_s5])
        v(FR, Y1, K1, ALU.subtract)
        act(AFR, FR, AF.Abs, P5, [B_s5])
        act(SN, FR, AF.Sin, P5, [B_s5], scale=C2PI)
        act(CS, AFR, AF.Sin, P5, [B_s5], scale=-C2PI, bias=cf[:, 548:549])
        v(Pre, Ee, CS, ALU.mult)
        v(Pim, Ee, SN, ALU.mult)
        cp("dve", mag8, Ee[:, 16, :], P5, [B_s5])
        cp("dve", fr16, FR[:, 16, :], P5, [B_s5])
        abre = Pre[:, 9, :]; abim = Pim[:, 9, :]
        nr, den, rden, t_a, t_b, kre, kim, t_c = s_a
        ts("dve", nr, abre, -1.0, None, ALU.add, None, P5, [B_s5])
        v(den, areT, areT, ALU.mult)
        v(t_a, aimT, aimT, ALU.mult)
        v(den, den, t_a, ALU.add)
        S.op("dve", lambda e: e.reciprocal(out=rden, in_=den), P5, [B_s5])
        v(t_a, nr, areT, ALU.mult); v(t_b, abim, aimT, ALU.mult); v(t_a, t_a, t_b, ALU.add); v(kre, t_a, rden, ALU.mult)
        v(t_a, abim, areT, ALU.mult); v(t_b, nr, aimT, ALU.mult); v(t_a, t_a, t_b, ALU.subtract); v(kim, t_a, rden, ALU.mult)
        bc8 = lambda a: a.unsqueeze(1).to_broadcast([128, 8, 32])
        v(FXre, Pre[:, 0:8, :], bc8(kre), ALU.mult); v(f_t[0], Pim[:, 0:8, :], bc8(kim), ALU.mult)
        v(FXre, FXre, f_t[0], ALU.subtract)
        v(FXim, Pre[:, 0:8, :], bc8(kim), ALU.mult); v(f_t[0], Pim[:, 0:8, :], bc8(kre), ALU.mult)
        v(FXim, FXim, f_t[0], ALU.add)
        a7re = bc8(Pre[:, 15, :]); a7im = bc8(Pim[:, 15, :])
        v(FZre, FXre, a7re, ALU.mult); v(f_t[0], FXim, a7im, ALU.mult); v(FZre, FZre, f_t[0], ALU.subtract)
        v(FZim, FXre, a7im, ALU.mult); v(f_t[0], FXim, a7re, ALU.mult); v(FZim, FZim, f_t[0], ALU.add)
        ts("dve", FXims, FXim, cf[:, 547:548], None, ALU.mult, None, P5, [B_s5])
        ts("dve", FZims, FZim, cf[:, 547:548], None, ALU.mult, None, P5, [B_s5])
        ts("dve", FZimB, FZim, cf[:, 549:550], None, ALU.mult, None, P5, [B_s5])
        ts("dve", FY9imn, Pim[:, 8:17, :], -1.0, None, ALU.mult, None, P5, [B_s5])
        S.rec = None
        nu_, ns_ = len(U3), len(SU)
        iu = is_ = 0
        while iu < nu_ or is_ < ns_:
            if is_ >= ns_ or (iu < nu_ and iu * ns_ <= is_ * nu_):
                U3[iu](); iu += 1
            else:
                SU[is_](); is_ += 1
        S.barrier()
        S.dma("pool", wglu, wglu_d.rearrange("(k p) n -> p k n", p=128), w=[B_wglu])
        S.dma("sp", bglu, bglu_d.rearrange("(k p) o -> p (k o)", p=128), w=[B_bglu], allow_slow_non_contiguous=True)
        tabC = view(T0, [8, 256], F32)
        tabS = view(T0 + 8192, [8, 256], F32)

        B_X = Buf("genX"); B_Y = Buf("genY"); B_Z = Buf("genZ"); B_Zs = Buf("genZs"); B_gt = Buf("gt"); B_gt1 = Buf("gt1")
        B_W = [Buf("s5w0"), Buf("s5w1")]
        B_Wc = [Buf("s5wc0"), Buf("s5wc1")]
        B_U = [[Buf("U%d_%d" % (w_, g)) for g in range(8)] for w_ in range(2)]
        B_H = [Buf("H%d" % g) for g in range(8)]
        B_tab = [Buf("tab%d" % i) for i in range(8)]
        B_tt = Buf("tabtmp")
        B_st1 = [Buf("st1_0"), Buf("st1_1")]; B_st2 = [Buf("st2_0"), Buf("st2_1")]; B_sxr = [Buf("sxr0"), Buf("sxr1")]
        B_st3 = [Buf("st3_0"), Buf("st3_1")]; B_st4 = [Buf("st4_0"), Buf("st4_1")]; B_yp = [Buf("yp0"), Buf("yp1")]
        B_G = [Buf("G0"), Buf("G1")]
        PBG = [PB[2], PB[3]]; PBY = [PB[6], PB[7]]
        psG = [psum[2][:, 0:256], psum[3][:, 0:256]]
        psY = [psum[6][:, 0:256], psum[7][:, 0:256]]

        def gen(eng, out, Mst, Msw, Fre, Fim, gs, wbuf, to_tmp=False, nj=8):
            in0 = Mst[:, gs, :].unsqueeze(2).to_broadcast([128, 8, nj, 16])
            in0s = Msw[:, gs, :].unsqueeze(2).to_broadcast([128, 8, nj, 16])
            f1 = Fre[:, :, gs].rearrange("p j g -> p g j").unsqueeze(3).to_broadcast([128, 8, nj, 16])
            f2 = Fim[:, :, gs].rearrange("p j g -> p g j").unsqueeze(3).to_broadcast([128, 8, nj, 16])
            if to_tmp:
                tt(eng, gt1, in0, f1, ALU.mult, P5, [B_gt1])
                tt(eng, gt3, in0s, f2, ALU.mult, P5, [B_gt1])
                tt(eng, out, gt1, gt3, ALU.add, [B_gt1], [wbuf])
            else:
                tt(eng, out, in0, f1, ALU.mult, P5, [wbuf])
                g2v = gt2[:, :, 0:nj, :]
                tt(eng, g2v, in0s, f2, ALU.mult, P5, [B_gt])
                tt(eng, out, out, g2v, ALU.add, [B_gt], [wbuf])

        tmask4 = cf[:, 128:256].unsqueeze(1).to_broadcast([128, 4, 128])

        def uload(qb):
            ws = qb % 2
            for t_ in range(8):
                S.dma("sp", Uu[ws][t_ * 16:(t_ + 1) * 16, :, :], xs_scr_v[t_][:, 8 * qb:8 * qb + 8, :],
                      r=[B_xscr[qb]], w=B_U[ws])

        def gen_pool_xyz(qb):
            gs = slice(qb * 8, (qb + 1) * 8)
            gen("dve", Xg, Bst, Bsw, FXre, FXims, gs, B_X)
            gen("dve", Yg, Cst, Csw, Pre[:, 8:17, :], FY9imn, gs, B_Y, nj=9)
            gen("dve", Zg, Bst, Bsw, FZre, FZims, gs, B_Z)

        def gen_pool_wc(qb):
            gs = slice(qb * 8, (qb + 1) * 8)
            cp("act", Wc[qb % 2], Yg[:, :, 1:9, :], [B_Y], [B_Wc[qb % 2]])

        def gen_pe(qb):
            ws = qb % 2
            for hb in range(2):
                pb = 4 + hb
                for j in range(4):
                    g = hb * 4 + j
                    S.op("pe", lambda e: e.matmul(psum[pb][:, j * 128:(j + 1) * 128],
                                                  lhsT=Xg[:, g].rearrange("p a b -> p (a b)"),
                                                  rhs=Yg[:, g, 0:8, :].rearrange("p a b -> p (a b)"), start=True, stop=True),
                         [B_X, B_Y], [PB[pb]], inc=(j == 3))
                tt("dve", Wtoep[ws][:, hb * 4:hb * 4 + 4, :], psum[pb].rearrange("p (a b) -> p a b", a=4), tmask4,
                   ALU.mult, [PB[pb], B_const], [B_W[ws]])
            for hb in range(2):
                pb = 4 + hb
                for j in range(4):
                    g = hb * 4 + j
                    tr(psum[pb][:, j * 128:(j + 1) * 128], Zg[:, g].rearrange("p a b -> p (a b)"),
                       [B_Z, B_const], [PB[pb]], inc=(j == 3))
                pv = psum[pb].rearrange("p (a b) -> p a b", a=4)
                cp("act", Wend[ws][:, hb * 4:hb * 4 + 4, :], pv, [PB[pb]], [B_W[ws]])
                cp("act", Wendsw[ws][:, hb * 4:hb * 4 + 4, 0:64], pv[:, :, 64:128], [PB[pb]], [B_W[ws]])
                cp("act", Wendsw[ws][:, hb * 4:hb * 4 + 4, 64:128], pv[:, :, 0:64], [PB[pb]], [B_W[ws]])

        def tables(g):
            sl8 = g % 8
            tS = tabS[:, sl8, :]; tC = tabC[:, sl8, :]
            bt_ = [B_tab[sl8]]
            act(tS, cf[:, 256:512], AF.Identity, P5, bt_, scale=fr16[:, g:g + 1])
            act(tC.bitcast(I32), cf[:, 256:512], AF.Identity, P5, bt_, scale=fr16[:, g:g + 1])
            tt("dve", tS, tS, tC.bitcast(I32), ALU.subtract, bt_, bt_)
            act(tC, tS, AF.Abs, bt_, bt_)
            act(tC, tC, AF.Sin, bt_ + [B_const], bt_, scale=-C2PI, bias=cf[:, 548:549])
            act(tS, tS, AF.Sin, bt_ + [B_const], bt_, scale=cf[:, 551:552])

        def s0(g):
            qb, gl = divmod(g, 8)
            ws = qb % 2; sl = g % 2
            mm(psum[sl][:, 0:256], Wend[ws][:, gl, :], Uu[ws][:, gl, :], True, True, [B_W[ws], B_U[ws][gl]], [PB[sl]], inc=False)
            mm(psum[sl][:, 256:512], Wendsw[ws][:, gl, :], Uu[ws][:, gl, :], True, True, [B_W[ws], B_U[ws][gl]], [PB[sl]])

        def s1(g):
            sl = g % 2; sl8 = g % 8
            tt("dve", st1[sl], psum[sl][:, 0:256], tabC[:, sl8, :], ALU.mult, [PB[sl], B_tab[sl8]], [B_st1[sl]])
            tt("dve", st2[sl], psum[sl][:, 256:512], tabS[:, sl8, :], ALU.mult, [PB[sl], B_tab[sl8]], [B_st2[sl]])
            tt("dve", sxr[sl], st1[sl], st2[sl], ALU.add, [B_st1[sl], B_st2[sl]], [B_sxr[sl]])
            S.op("dve", lambda e: e.tensor_tensor_scan(out=Gb[sl], data0=mag8[:, g:g + 1].to_broadcast([128, 256]),
                                                       data1=sxr[sl], initial=0.0, op0=ALU.mult, op1=ALU.add),
                 [B_sxr[sl], B_s5], [B_G[sl]])

        def s2(g):
            qb, gl = divmod(g, 8)
            sl = g % 2; sl8 = g % 8
            mm(psG[sl], pswap_b, Gb[sl], True, True, [B_G[sl], B_const], [PBG[sl]])
            tt("dve", st3[sl], Gb[sl], tabC[:, sl8, :], ALU.mult, [B_G[sl], B_tab[sl8]], [B_st3[sl]])
            tt("dve", st4[sl], psG[sl], tabS[:, sl8, :], ALU.mult, [PBG[sl], B_tab[sl8]], [B_st4[sl]])
            tt("pool", Hprev[:, gl, 1:256], st3[sl][:, 0:255], st4[sl][:, 0:255], ALU.subtract,
               [B_st3[sl], B_st4[sl]], [B_H[gl]])

        def s3(g):
            qb, gl = divmod(g, 8)
            ws = qb % 2; sl = g % 2
            mm(psY[sl], Wtoep[ws][:, gl, :], Uu[ws][:, gl, :], True, False, [B_W[ws], B_U[ws][gl]], [PBY[sl]], inc=False)
            mm(psY[sl], Wc[ws][:, gl].rearrange("p a b -> p (a b)"), Hprev[:, gl, :], False, True,
               [B_Wc[ws], B_H[gl]], [PBY[sl]])
            S.op("dve", lambda e: e.scalar_tensor_tensor(out=Uu[ws][:, gl, :], in0=Uu[ws][:, gl, :], scalar=Dst[:, g:g + 1],
                                                          in1=psY[sl], op0=ALU.mult, op1=ALU.add),
                 [PBY[sl], B_dst], [B_U[ws][gl]])

        def writeback(qb):
            ws = qb % 2
            for t_ in range(8):
                S.dma("sp", y_scr_v[t_][:, 8 * qb:8 * qb + 8, :], Uu[ws][t_ * 16:(t_ + 1) * 16, :, :],
                      r=B_U[ws], w=[B_yscr[qb]])
            S.dma("sp", xsT[:, qb, :], y_scr[qb * 128:(qb + 1) * 128, :], r=[B_yscr[qb]], w=[B_xs[qb]])

        for gl_ in range(8):
            S.op("pool", lambda e, gl_=gl_: e.memset(Hprev[:, gl_, 0:1], 0.0), [], [B_H[gl_]])
        NG = 32
        uload(0); gen_pool_xyz(0); gen_pool_wc(0)
        for g in range(4):
            tables(g)
        gen_pe(0)
        for step in range(NG + 4):
            if 0 <= step - 3 < NG:
                s3(step - 3)
                if (step - 3) % 8 == 7:
                    writeback((step - 3) // 8)
            if 0 <= step - 2 < NG:
                s2(step - 2)
            if 4 <= step + 3 < NG:
                tables(step + 3)
            if 0 <= step - 1 < NG:
                s1(step - 1)
            nb = (step + 7) // 8
            if step + 7 == 8 * nb and nb < 4:
                gen_pool_xyz(nb)
            if step + 5 == 8 * nb and nb < 4 and nb >= 1:
                uload(nb)
                gen_pool_wc(nb)
            if step + 2 == 8 * ((step + 2) // 8) and 1 <= (step + 2) // 8 < 4:
                gen_pe((step + 2) // 8)
            if step < NG:
                s0(step)


        hbglu = small[:, 400:404]
        B_hb = Buf("hbglu")

        def glu_pre():
            for q in range(4):
                for cc in range(4):
                    cs_ = slice(cc * 512, (cc + 1) * 512)
                    act(xsT[:, q, cs_], xsT[:, q, cs_], AF.Gelu, [], [B_xs[q]])
            ts("dve", hbglu, bglu, 0.5, None, ALU.mult, None, [B_bglu], [B_hb])

        def glu_units():
            for m in range(4):
                for cc in range(4):
                    c0 = cc * 64
                    pb = 7
                    ncs = slice(cc * 512, (cc + 1) * 512)
                    yv = lambda k_: xsT[:, k_, :].rearrange("p (t c) -> p t c", t=8)[:, :, c0:c0 + 64]
                    for k in range(4):
                        mm(psum[pb][:, :], wglu[:, k, m * 128:(m + 1) * 128], yv(k), k == 0, k == 3,
                           [B_wglu, B_xs[k]], [PB[pb]])
                    act(tmp[0], psum[pb][:, :], AF.Tanh, [PB[pb], B_hb], [TB[0]], scale=0.5, bias=hbglu[:, m:m + 1])
                    t0v = tmp[0].rearrange("p (t c) -> p t c", t=8)
                    t1v = tmp[1].rearrange("p (t c) -> p t c", t=8)
                    ym_ = yv(m)
                    S.op("dve", lambda e, t0v=t0v, t1v=t1v, ym_=ym_: e.scalar_tensor_tensor(
                        out=t1v, in0=t0v, scalar=1.0, in1=ym_, op0=ALU.add, op1=ALU.mult), [B_xs[m], TB[0]], [TB[1]])
                    mv2 = mixT[:, m, ncs].rearrange("p (c t) -> p c t", t=8)
                    t1c = tmp[1].rearrange("p (t c) -> p c t", t=8)
                    S.op("dve", lambda e, mv2=mv2, t1c=t1c: e.scalar_tensor_tensor(
                        out=mv2, in0=t1c, scalar=0.5, in1=mv2, op0=ALU.mult, op1=ALU.mult), [TB[1]], [B_mix[m][cc]])

        def glu_all():
            glu_pre()
            glu_units()

        if debug and stage == 4:
            glu_all()
            S.barrier()
            dv = dbg_d.rearrange("p (a b) -> p a b", a=8)
            B_dbg = Buf("dbg")
            srcs = [mixT[:, 0, :], mixT[:, 1, :], mixT[:, 2, :], mixT[:, 3, :], xsT[:, 0, :], xsT[:, 1, :], xsT[:, 2, :], xsT[:, 3, :]]
            for n_, sv in enumerate(srcs):
                for cc in range(4):
                    cs = slice(cc * 512, (cc + 1) * 512)
                    cp("dve", tmp[cc % 4], sv[:, cs], [], [TB[cc % 4]])
                    S.dma("sp", dv[:, n_, cs], tmp[cc % 4], r=[TB[cc % 4]], w=[B_dbg])
            S._wait("sp", B_dbg.w)
            return nc


        S.barrier()
        if debug and stage == 40:
            glu_all()
            return nc
        cur[0] = WO
        qnT = alloc([4, L], BF16); knT = alloc([4, L], BF16); Vv = alloc([16, 512], BF16); qrT = alloc([2, L], BF16)
        wuq_f = alloc([2, 768], F32); wuq = alloc([2, 768], BF16); wqr = alloc([2, 4, 128], BF16); qg = alloc([2], F32)
        wukv_f = alloc([1024], F32); wukv = alloc([1024], BF16); kvg = alloc([1], F32)
        sq = [alloc([3, 512], BF16) for _ in range(2)]
        rstdq = [alloc([512], F32) for _ in range(2)]
        rstdk = [alloc([512], F32) for _ in range(2)]
        rkt = [alloc([4], F32) for _ in range(2)]
        B_wq = Buf("wuq"); B_wkv = Buf("wukv")
        B_sq = [Buf("sq0"), Buf("sq1")]
        B_rq = [Buf("rq0"), Buf("rq1")]
        B_rk = [Buf("rk0"), Buf("rk1")]
        B_rkt = [Buf("rkt0"), Buf("rkt1")]
        B_qn = [[Buf("qn%d_%d" % (h, c)) for c in range(4)] for h in range(4)]
        B_kn = [[Buf("kn%d_%d" % (h, c)) for c in range(4)] for h in range(4)]
        B_qr = [[Buf("qr%d_%d" % (h, c)) for c in range(4)] for h in range(2)]
        B_V = [Buf("V%d" % i) for i in range(16)]

        S.dma("sp", wuq_f, wuq_d.rearrange("(k p) n -> p k n", p=128), w=[B_wq])
        S.dma("sp", qg, qg_d.rearrange("(k p) o -> p (k o)", p=128), w=[B_wq], allow_slow_non_contiguous=True)
        S.dma("sp", wukv_f, wukv_d, w=[B_wkv])
        S.dma("sp", kvg, kvg_d, w=[B_wkv])
        for k in range(2):
            ts("dve", wuq[:, k, :], wuq_f[:, k, :], qg[:, k:k + 1], None, ALU.mult, None, [B_wq], [B_wq])
        for pair in range(2):
            for hh in range(2):
                base = (2 * pair + hh) * 192 + 128
                cp("pool", wqr[:, :, 2 * pair, hh * 64:hh * 64 + 64], wuq[:, :, base:base + 64], [B_wq], [B_wq])
                cp("pool", wqr[:, :, 2 * pair + 1, hh * 64:hh * 64 + 32], wuq[:, :, base + 32:base + 64], [B_wq], [B_wq])
                cp("pool", wqr[:, :, 2 * pair + 1, hh * 64 + 32:hh * 64 + 64], wuq[:, :, base:base + 32], [B_wq], [B_wq])
        ts("dve", wukv, wukv_f, kvg[:, 0:1], None, ALU.mult, None, [B_wkv], [B_wkv])
        wukv_v = wukv.rearrange("p (h x) -> p h x", h=4)[:, :, 128:256]

        def rsqrt_from(out, src_ps, scale_, r, w):
            act(out, src_ps, AF.Ln, r + [B_const], w, scale=scale_, bias=cf[:, 554:555])
            act(out, out, AF.Exp, w, w, scale=-0.5)

        rotc = [0]

        def nb_():
            pb = rotc[0] % 8
            rotc[0] += 1
            return pb

        def prep0(cc):
            cs = slice(cc * 512, (cc + 1) * 512)
            sl = cc % 2
            for j in range(2):
                act(sq[sl][:, j, :], cqT[:, j, cs], AF.Square, [B_cq[cc]], [B_sq[sl]])
            act(sq[sl][:, 2, :], ckvT[:, cs], AF.Square, [B_ckv[cc]], [B_sq[sl]])
            pbq = nb_()
            mm(psum[pbq][:, :], ones_b, sq[sl][:, 0, :], True, False, [B_const, B_sq[sl]], [PB[pbq]])
            mm(psum[pbq][:, :], ones_b, sq[sl][:, 1, :], False, True, [B_const, B_sq[sl]], [PB[pbq]])
            pbk = nb_()
            mm(psum[pbk][:, :], ones_b, sq[sl][:, 2, :], True, True, [B_const, B_sq[sl]], [PB[pbk]])
            pbt = nb_()
            for ii in range(4):
                mm(psum[pbt][:, ii:ii + 1], sq[sl][:, 2, ii * 128:(ii + 1) * 128], ones_b[:, 0:1], True, True,
                   [B_const, B_sq[sl]], [PB[pbt]], inc=(ii == 3))
            act(rstdq[sl], psum[pbq][:, :], AF.Ln, [PB[pbq], B_const], [B_rq[sl]], scale=1.0 / 256.0, bias=cf[:, 554:555])
            act(rstdk[sl], psum[pbk][:, :], AF.Ln, [PB[pbk], B_const], [B_rk[sl]], scale=1.0 / 128.0, bias=cf[:, 554:555])
            act(rkt[sl], psum[pbt][:, 0:4], AF.Ln, [PB[pbt], B_const], [B_rkt[sl]], scale=1.0 / 128.0, bias=cf[:, 554:555])
            act(rstdq[sl], rstdq[sl], AF.Exp, [], [B_rq[sl]], scale=-0.5)
            act(rstdk[sl], rstdk[sl], AF.Exp, [], [B_rk[sl]], scale=-0.5)
            act(rkt[sl], rkt[sl], AF.Exp, [], [B_rkt[sl]], scale=-0.5)

        def prep1(cc):
            cs = slice(cc * 512, (cc + 1) * 512)
            sl = cc % 2
            for h in range(4):
                pb = nb_()
                for k in range(2):
                    mm(psum[pb][:, :], wuq[:, k, h * 192:h * 192 + 128], cqT[:, k, cs], k == 0, k == 1,
                       [B_wq, B_cq[cc]], [PB[pb]])
                tt("dve", qnT[:, h, cs], psum[pb][:, :], rstdq[sl], ALU.mult, [PB[pb], B_rq[sl]], [B_qn[h][cc]])
            for pair in range(2):
                pb1 = nb_()
                pb2 = nb_()
                for k in range(2):
                    mm(psum[pb1][:, :], wqr[:, k, 2 * pair, :], cqT[:, k, cs], k == 0, k == 1, [B_wq, B_cq[cc]], [PB[pb1]])
                for k in range(2):
                    mm(psum[pb2][:, :], wqr[:, k, 2 * pair + 1, :], cqT[:, k, cs], k == 0, k == 1, [B_wq, B_cq[cc]], [PB[pb2]])
                ta = tmp[2 * pair]; tb = tmp[2 * pair + 1]
                tt("dve", ta, psum[pb1][:, :], COSr[:, cs], ALU.mult, [PB[pb1], B_rope], [TB[2 * pair]])
                tt("dve", tb, psum[pb2][:, :], SINr[:, cs], ALU.mult, [PB[pb2], B_rope], [TB[2 * pair + 1]])
                tt("dve", ta, ta, tb, ALU.add, [TB[2 * pair + 1]], [TB[2 * pair]])
                tt("dve", qrT[:, pair, cs], ta, rstdq[sl], ALU.mult, [TB[2 * pair], B_rq[sl]], [B_qr[pair][cc]])
            for h in range(4):
                pb = nb_()
                mm(psum[pb][:, :], wukv[:, h * 256:h * 256 + 128], ckvT[:, cs], True, True, [B_wkv, B_ckv[cc]], [PB[pb]])
                tt("dve", knT[:, h, cs], psum[pb][:, :], rstdk[sl], ALU.mult, [PB[pb], B_rk[sl]], [B_kn[h][cc]])
            for ii in range(4):
                i = cc * 4 + ii
                pb = nb_()
                mm(psum[pb][:, :], ckvT[:, i * 128:(i + 1) * 128], wukv_v, True, True, [B_wkv, B_ckv[cc]], [PB[pb]])
                S.op("act", lambda e: e.activation(out=Vv[:, i, :], in_=psum[pb][:, :], func=AF.Copy,
                                                   scale=rkt[sl][:, ii:ii + 1]),
                     [PB[pb], B_rkt[sl]], [B_V[i]])

        prep0(0); prep0(1); prep1(0); prep0(2); prep1(1); prep0(3); prep1(2)
        glu_pre()
        prep1(3)

        S.barrier()
        if debug and stage == 45:
            return nc
        wout = view(WO + 57344, [8, D], BF16)
        pTall = view(WO + 73728, [2, L], BF16)
        pstage = [small[:, 448:704], small[:, 704:960]]
        B_pst = [Buf("pst0"), Buf("pst1")]; B_pTall = Buf("pTall")
        wpg = view(ZO + 20480, [8, D], BF16)
        g2 = view(ZO + 36864, [D], F32); b2 = view(ZO + 40960, [D], F32)
        g1 = view(ARENA - 8192, [D], F32); b1 = view(ARENA - 4096, [D], F32)
        B_wout = Buf("wout"); B_wpg = Buf("wpg"); B_g2 = Buf("g2b2"); B_g1x = Buf("g1b1x")
        S.dma("pool", wout, wout_d.rearrange("(k p) n -> p k n", p=128), w=[B_wout])
        S.dma("pool", wpg, wpg_d.rearrange("(k p) n -> p k n", p=128), w=[B_wpg])
        S.dma("sp", g2, g2_d.partition_broadcast(128)[:, 0, :], w=[B_g2])
        S.dma("sp", b2, b2_d.partition_broadcast(128)[:, 0, :], w=[B_g2])

        krB = view(WO + 81920, [L], BF16)
        B_krB = Buf("krB")
        cp("pool", krB[64:128, :], krT[64:128, :], B_kr, [B_krB])
        S.op("pool", lambda e: e.memset(krB[0:64, :], 0.0), [], [B_krB])
        S.op("pool", lambda e: e.memset(krT[64:128, :], 0.0), [B_krB], B_kr)
        attops = []
        S.rec = attops
        items = []
        for h in range(4):
            for Q in range(4):
                for j in range(4 * Q + 4):
                    items.append((h, Q, j))

        def emit_S(idx):
            h, Q, j = items[idx]
            a = j - 4 * Q
            c0 = 128 * a if a > 0 else 0
            pb = idx % 3
            ptb = idx % 3
            hb = 64 * (h % 2)
            pair = h // 2
            qs = slice(Q * 512 + c0, (Q + 1) * 512)
            ks = slice(j * 128, (j + 1) * 128)
            mm(psum[pb][:, c0:512], knT[:, h, ks], qnT[:, h, qs], True, False, [B_kn[h][j // 4], B_qn[h][Q]], [PB[pb]], inc=False)
            mm(psum[pb][:, c0:512], (krT if h % 2 == 0 else krB)[:, ks], qrT[:, pair, qs], False, a < 0,
               [B_kr[j // 4], B_krB, B_qr[pair][Q]], [PB[pb]], inc=(a < 0))
            if a >= 0:
                mm(psum[pb][:, c0:c0 + 128], ident_b, trimask, False, True, [B_const], [PB[pb]])
            act(PT[ptb][:, c0:512], psum[pb][:, c0:512], AF.Exp, [PB[pb]], [PTB[ptb]], scale=SCALE)

        def emit_PV(idx):
            h, Q, j = items[idx]
            a = j - 4 * Q
            c0 = 128 * a if a > 0 else 0
            ptb = idx % 3
            hq = h * 4 + Q
            po = 3 + hq % 2
            pl = 5 + hq % 2
            last = (j == 4 * Q + 3)
            mm(psum[po][:, c0:512], Vv[:, j, h * 128:(h + 1) * 128], PT[ptb][:, c0:512], j == 0, last,
               [B_V[j], PTB[ptb]], [PB[po]], inc=last)
            mm(psum[pl][:, c0:512], ones_b, PT[ptb][:, c0:512], j == 0, last, [B_const, PTB[ptb]], [PB[pl]], inc=True)
            if last:
                Qs = slice(Q * 512, (Q + 1) * 512)
                S.op("dve", lambda e: e.reciprocal(out=tmp[2], in_=psum[pl][:, :]), [PB[pl]], [TB[2]])
                tt("dve", tmp[3], psum[po][:, :], tmp[2], ALU.mult, [PB[po], TB[2]], [TB[3]])
                tt("pool", mixT[:, 4 + h, Qs], tmp[3], mixT[:, 4 + h, Qs], ALU.mult, [TB[3]], [B_mix[4 + h][Q]])

        for idx in range(len(items)):
            emit_S(idx)
            if idx > 1:
                emit_PV(idx - 2)
        emit_PV(len(items) - 2)
        emit_PV(len(items) - 1)
        gluops = []
        S.rec = gluops
        for i in range(16):
            s_ = i % 2
            S.dma("sp", pstage[s_], p_d[i * 128:(i + 1) * 128, :], w=[B_pst[s_]])
            for k in range(2):
                tr(psum[7][:, k * 128:(k + 1) * 128], pstage[s_][:, k * 128:(k + 1) * 128], [B_pst[s_], B_const], [PB[7]], inc=(k == 1))
            cp("act", pTall[:, :, i * 128:(i + 1) * 128], psum[7][:, 0:256].rearrange("p (a b) -> p a b", a=2),
               [PB[7]], [B_pTall])
        glu_units()
        S.rec = None
        na_, ng_ = len(attops), len(gluops)
        ia = ig = 0
        while ia < na_ or ig < ng_:
            if ig >= ng_ or (ia < na_ and ia * ng_ <= ig * na_):
                attops[ia](); ia += 1
            else:
                gluops[ig](); ig += 1
        S.dma("sp", g1, g1_d.partition_broadcast(128)[:, 0, :], w=[B_g1x, B_wglu, B_bglu])
        S.dma("sp", b1, b1_d.partition_broadcast(128)[:, 0, :], w=[B_g1x, B_wglu, B_bglu])

        if debug and stage == 5:
            S.barrier()
            dv = dbg_d.rearrange("p (a b) -> p a b", a=8)
            B_dbg = Buf("dbg")
            srcs = [mixT[:, 4, :], mixT[:, 5, :], mixT[:, 6, :], mixT[:, 7, :], qnT[:, 0, :], knT[:, 0, :], qrT[:, 0, :], Vv.rearrange("p a b -> p (a b)")[:, 0:2048]]
            for n_, sv in enumerate(srcs):
                for cc in range(4):
                    cs = slice(cc * 512, (cc + 1) * 512)
                    cp("dve", tmp[cc % 4], sv[:, cs], [], [TB[cc % 4]])
                    S.dma("sp", dv[:, n_, cs], tmp[cc % 4], r=[TB[cc % 4]], w=[B_dbg])
            S._wait("sp", B_dbg.w)
            return nc

        S.barrier()
        xt = [view(ZO + 4096 * i, [D], F32) for i in range(4)]
        xn = [view(WO + 4096 * i, [D], F32) for i in range(5)]
        uu = [view(WO + 20480 + 4096 * i, [D], F32) for i in range(6)]
        uTb = [view(WO + 45056 + 2048 * i, [8, 128], BF16) for i in range(2)]
        gate = [view(WO + 81920 + 4096 * i, [D], F32) for i in range(5)] + [view(WO + 49152 + 4096 * i, [D], F32) for i in range(2)]
        assert WO + 81920 + 5 * 4096 <= ARENA - 8192
        wpp = view(36096, [2, D], BF16)
        B_xt = [Buf("fxt%d" % i) for i in range(4)]; B_xn = [Buf("fxn%d" % i) for i in range(5)]
        B_u = [Buf("u%d" % i) for i in range(6)]; B_gate = [Buf("gate%d" % i) for i in range(7)]
        B_uT = [Buf("uT0"), Buf("uT1")]; B_wpp = Buf("wpp")
        B_out = [Buf("out%d" % i) for i in range(7)]
        st1_ = [small[:, 20 * i:20 * i + 20] for i in range(4)]
        st2_ = [small[:, 80 + 20 * i:100 + 20 * i] for i in range(4)]
        B_s1 = [Buf("fs1_%d" % i) for i in range(4)]; B_s2 = [Buf("fs2_%d" % i) for i in range(4)]
        S.dma("pool", wpp, wpp_d.rearrange("(k p) n -> p k n", p=128), w=[B_wpp])

        def tsl(i):
            return slice(i * 128, (i + 1) * 128)

        def bn(xin, stat, r, rb):
            st = stat[:, 0:12].rearrange("p (a b) -> p a b", a=2)
            S.op("dve", lambda e: e.bn_stats(out=st[:, 0, :], in_=xin[:, 0:512]), r, [rb])
            S.op("dve", lambda e: e.bn_stats(out=st[:, 1, :], in_=xin[:, 512:1024]), r, [rb])
            S.op("dve", lambda e: e.bn_aggr(out=stat[:, 12:14], in_=stat[:, 0:12]), [rb], [rb])

        def lnexp(stat, rb):
            act(stat[:, 14:15], stat[:, 13:14], AF.Ln, [rb, B_const], [rb], bias=cf[:, 553:554])
            act(stat[:, 15:16], stat[:, 14:15], AF.Exp, [rb], [rb], scale=-0.5)

        def nmr(stat, rb):
            ts("dve", stat[:, 16:17], stat[:, 12:13], stat[:, 15:16], -1.0, ALU.mult, ALU.mult, [rb], [rb])

        def m0(i):
            S.dma("sp", xt[i % 4], x_d[tsl(i), :], w=[B_xt[i % 4]])
            bn(xt[i % 4], st1_[i % 4], [B_xt[i % 4]], B_s1[i % 4])

        def m1(i):
            lnexp(st1_[i % 4], B_s1[i % 4])

        def m2(i):
            nmr(st1_[i % 4], B_s1[i % 4])

        def m3(i):
            st = st1_[i % 4]
            act(xn[i % 5], xt[i % 4], AF.Identity, [B_xt[i % 4], B_s1[i % 4]], [B_xn[i % 5]], scale=st[:, 15:16], bias=st[:, 16:17])

        def m4(i):
            tt("dve", xn[i % 5], xn[i % 5], g1, ALU.mult, [B_g1x], [B_xn[i % 5]])

        def m5(i):
            tt("pool", xn[i % 5], xn[i % 5], b1, ALU.add, [B_g1x], [B_xn[i % 5]])

        def m6(i):
            for hf in range(2):
                hs = slice(hf * 512, (hf + 1) * 512)
                for k in range(8):
                    mm(psum[hf][:, :], mixT[:, k, tsl(i)], wout[:, k, hs], k == 0, k == 7,
                       [B_mix[k][i // 4], B_wout], [PB[hf]])

        def m7(i):
            for hf in range(2):
                hs = slice(hf * 512, (hf + 1) * 512)
                S.op("dve", lambda e: e.scalar_tensor_tensor(out=uu[i % 6][:, hs], in0=xn[i % 5][:, hs], scalar=ALPHA,
                                                              in1=psum[hf][:, :], op0=ALU.mult, op1=ALU.add),
                     [B_xn[i % 5], PB[hf]], [B_u[i % 6]])

        def m8(i):
            for hb in range(2):
                pb = 2 + hb
                for j in range(4):
                    k = hb * 4 + j
                    tr(psum[pb][:, j * 128:(j + 1) * 128], uu[i % 6][:, k * 128:(k + 1) * 128], [B_u[i % 6], B_const], [PB[pb]], inc=(j == 3))

        def m9(i):
            for hb in range(2):
                pb = 2 + hb
                cp("act", uTb[i % 2][:, hb * 4:hb * 4 + 4, :], psum[pb].rearrange("p (a b) -> p a b", a=4),
                   [PB[pb]], [B_uT[i % 2]])

        def m10(i):
            for hf in range(2):
                hs = slice(hf * 512, (hf + 1) * 512)
                pb = 4 + hf
                for k in range(8):
                    mm(psum[pb][:, :], uTb[i % 2][:, k, :], wpg[:, k, hs], k == 0, k == 7, [B_uT[i % 2], B_wpg], [PB[pb]])

        def m11(i):
            for hf in range(2):
                hs = slice(hf * 512, (hf + 1) * 512)
                act(gate[i % 7][:, hs], psum[4 + hf][:, :], AF.Sigmoid, [PB[4 + hf]], [B_gate[i % 7]])
            for hf in range(2):
                hs = slice(hf * 512, (hf + 1) * 512)
                pb = 6 + hf
                for k in range(2):
                    mm(psum[pb][:, :], pTall[:, k, tsl(i)], wpp[:, k, hs], k == 0, k == 1, [B_pTall, B_wpp], [PB[pb]])

        def m12(i):
            gt_ = gate[i % 7]
            for hf in range(2):
                hs = slice(hf * 512, (hf + 1) * 512)
                tt("dve", gt_[:, hs], gt_[:, hs], psum[6 + hf][:, :], ALU.mult, [PB[6 + hf]], [B_gate[i % 7]])
            tt("dve", gt_, gt_, uu[i % 6], ALU.add, [B_u[i % 6]], [B_gate[i % 7]])
            bn(gt_, st2_[i % 4], [B_gate[i % 7]], B_s2[i % 4])

        def m13(i):
            lnexp(st2_[i % 4], B_s2[i % 4])

        def m14(i):
            nmr(st2_[i % 4], B_s2[i % 4])

        def m15(i):
            st = st2_[i % 4]
            act(gate[i % 7], gate[i % 7], AF.Identity, [B_s2[i % 4]], [B_gate[i % 7]], scale=st[:, 15:16], bias=st[:, 16:17])

        def m16(i):
            tt("dve", gate[i % 7], gate[i % 7], g2, ALU.mult, [B_g2], [B_gate[i % 7]])

        def m17(i):
            tt("pool", gate[i % 7], gate[i % 7], b2, ALU.add, [B_g2], [B_gate[i % 7]])
            S.dma("sp", out_d[tsl(i), :], gate[i % 7], r=[B_gate[i % 7]], w=[B_out[i % 7]])

        stages = [m0, m1, m2, m3, m4, m5, m6, m7, m8, m9, m10, m11, m12, m13, m14, m15, m16, m17]
        for step in range(16 + len(stages) - 1):
            for si in range(len(stages) - 1, -1, -1):
                i = step - si
                if 0 <= i < 16:
                    stages[si](i)
        for bo in B_out:
            S._wait("sp", bo.w)
    return nc


def make_in_maps(inputs):
    cf, cb = host_consts()
    maps = []
    f = lambda a: np.ascontiguousarray(np.asarray(a, dtype=np.float32))
    for b in range(NCORES):
        m = {
            "x": f(inputs["x"][b]), "p": f(inputs["p"][0, b]),
            "pos": np.ascontiguousarray(np.asarray(inputs["positions"][b], dtype=np.int32).reshape(1, L)),
            "cf": cf, "cb": cb,
            "ln_emb_g": f(inputs["ln_emb_g"]).reshape(1, D), "ln_emb_b": f(inputs["ln_emb_b"]).reshape(1, D),
            "ln_g": f(inputs["ln_g"][0]).reshape(1, D), "ln_b": f(inputs["ln_b"][0]).reshape(1, D),
            "w_in": f(inputs["w_in"][0]),
            "a_re": f(inputs["a_re"][0]), "a_im": f(inputs["a_im"][0]), "log_dt": f(inputs["log_dt"][0]).reshape(1, 32),
            "b_re": f(inputs["b_re"][0]), "b_im": f(inputs["b_im"][0]),
            "c_re": f(inputs["c_re"][0]).reshape(512, 64), "c_im": f(inputs["c_im"][0]).reshape(512, 64),
            "d_skip": f(inputs["d_skip"][0]).reshape(32, 16),
            "w_glu": f(inputs["w_glu"][0]), "b_glu": f(inputs["b_glu"][0]).reshape(512, 1),
            "q_norm_g": f(inputs["q_norm_g"][0]).reshape(256, 1), "w_uq": f(inputs["w_uq"][0]),
            "kv_norm_g": f(inputs["kv_norm_g"][0]).reshape(128, 1), "w_ukv": f(inputs["w_ukv"][0]),
            "w_out": f(inputs["w_out"][0]), "w_pg": f(inputs["w_pg"][0]), "w_pp": f(inputs["w_pp"][0]),
        }
        maps.append(m)
    return maps


def kernel(**inputs):
    nc = build()
    in_maps = make_in_maps(inputs)
    res = run_bass_kernel_spmd(nc, in_maps, core_ids=list(range(NCORES)))
    out = np.stack([np.asarray(r["out"], dtype=np.float32) for r in res.results], axis=0)
    return out
```

```python
import math
import contextlib
import numpy as np
import ml_dtypes
import concourse.bass as bass
import concourse.mybir as mybir
from concourse.bass_utils import run_bass_kernel_spmd

F32 = mybir.dt.float32
I32 = mybir.dt.int32
BF16 = mybir.dt.bfloat16
ALU = mybir.AluOpType
AF = mybir.ActivationFunctionType

L = 2048
D = 1024
NCORES = 8
TWO_PI = 2.0 * math.pi
C2PI = TWO_PI * (1.0 - 1e-6)
LN_EPS = 1e-5
RMS_EPS = 1e-6
ALPHA = 2.0 ** 0.25
SCALE = 192.0 ** -0.5
KV = [0.0] + [-float(s) for s in range(1, 8)] + [float(t) for t in range(0, 8)] + [8.0]
NCF = 576


class Buf:
    __slots__ = ("name", "w", "r", "dsem", "dcnt")

    def __init__(self, name):
        self.name = name
        self.w = None
        self.r = []
        self.dsem = None
        self.dcnt = 0


class Sched:
    def __init__(self, nc, es):
        self.nc = nc
        self.es = es
        self.E = {"pe": nc.tensor, "act": nc.scalar, "dve": nc.vector, "pool": nc.gpsimd, "sp": nc.sync}
        self.sem = {k: es.enter_context(nc.semaphore("prog_" + k)) for k in self.E}
        self.cnt = {k: 0 for k in self.E}
        self.seen = {k: {} for k in self.E}
        self.pe_pending = []
        self.nsem = 0
        self.rec = None
        self.snap = {}
        self.age = {}
        self.clock = 0

    def _wait(self, eng, tok):
        if tok is None:
            return
        sem, val = tok
        if eng == "pe" and sem is self.sem["pe"]:
            return
        key = sem.name
        if self.seen[eng].get(key, 0) >= val:
            return
        self.seen[eng][key] = val
        self.E[eng].wait_ge(sem, val)

    def _deps(self, eng, r, w):
        for b in r:
            self._wait(eng, b.w)
        for b in w:
            self._wait(eng, b.w)
            for t in b.r:
                self._wait(eng, t)

    def op(self, eng, fn, r=(), w=(), inc=True):
        if self.rec is not None:
            r = list(r); w = list(w)
            self.rec.append(lambda: self._op(eng, fn, r, w, inc))
            return None
        return self._op(eng, fn, r, w, inc)

    def _collect(self, eng, r, w):
        need = {}

        def add(tok):
            if tok is None:
                return
            sem, val = tok
            if eng == "pe" and sem is self.sem["pe"]:
                return
            if self.seen[eng].get(sem.name, 0) >= val:
                return
            if sem.name not in need or need[sem.name][1] < val:
                need[sem.name] = (sem, val)

        cand = [b.w for b in r]
        for b in w:
            cand.append(b.w)
            cand.extend(b.r)
        cand = [t for t in cand if t is not None]
        cand.sort(key=lambda t: -t[1])
        for tok in cand:
            before = len(need)
            had = need.get(tok[0].name)
            add(tok)
            if need.get(tok[0].name) is not had or len(need) != before:
                self.seen[eng][tok[0].name] = max(self.seen[eng].get(tok[0].name, 0), tok[1])
                snap = self.snap.get((tok[0].name, tok[1]))
                if snap:
                    se = self.seen[eng]
                    for k_, v_ in snap.items():
                        if se.get(k_, 0) < v_:
                            se[k_] = v_
        for name, (sem, val) in need.items():
            self.seen[eng][name] = max(self.seen[eng].get(name, 0), val)
        return sorted(need.values(), key=lambda t: self.age.get((t[0].name, t[1]), 0))

    def _op(self, eng, fn, r=(), w=(), inc=True):
        toks = self._collect(eng, r, w)
        for (sem, val) in toks[:-1]:
            self.E[eng].wait_ge(sem, val)
        ins = fn(self.E[eng])
        if toks:
            ins._wait_ge(toks[-1][0], toks[-1][1])
        if inc:
            self.cnt[eng] += 1
            ins.then_inc(self.sem[eng], 1)
            tok = (self.sem[eng], self.cnt[eng])
            self.snap[(tok[0].name, tok[1])] = dict(self.seen[eng])
            self.clock += 1
            self.age[(tok[0].name, tok[1])] = self.clock
            for b in r:
                b.r.append(tok)
            for b in w:
                b.w = tok
                b.r = []
            if eng == "pe":
                for b in self.pe_pending:
                    b.r.append(tok)
                self.pe_pending = []
        else:
            assert eng == "pe"
            self.pe_pending.extend(r)
            for b in w:
                b.r = []
        return ins

    def dma(self, q, out, in_, r=(), w=(), **kw):
        if self.rec is not None:
            r = list(r); w = list(w)
            self.rec.append(lambda: self._dma(q, out, in_, r, w, **kw))
            return None
        return self._dma(q, out, in_, r, w, **kw)

    def _dma(self, q, out, in_, r=(), w=(), **kw):
        dst = w[0]
        for b in r:
            self._wait(q, b.w)
        for b in w:
            if not (b.w is not None and b.dsem is not None and b.w[0] is b.dsem):
                self._wait(q, b.w)
            for t in b.r:
                self._wait(q, t)
        if dst.dsem is None:
            dst.dsem = self.es.enter_context(self.nc.semaphore("d%d_%s" % (self.nsem, dst.name)))
            dst.dcnt = [0]
            self.nsem += 1
        for b in w:
            if b.dsem is None:
                b.dsem = dst.dsem
                b.dcnt = dst.dcnt
        ins = self.E[q].dma_start(out=out, in_=in_, **kw)
        ins.then_inc(dst.dsem, 16)
        dst.dcnt[0] += 16
        tok = (dst.dsem, dst.dcnt[0])
        self.snap[(tok[0].name, tok[1])] = dict(self.seen[q])
        self.clock += 1
        self.age[(tok[0].name, tok[1])] = self.clock
        for b in r:
            b.r.append(tok)
        for b in w:
            b.w = tok
            b.r = []
        return tok

    def barrier(self):
        toks = [(self.sem[k], self.cnt[k]) for k in self.E if self.cnt[k] > 0]
        for e in self.E:
            for t in toks:
                if t[0] is not self.sem[e]:
                    self._wait(e, t)


def host_consts():
    cf = np.zeros((128, NCF), np.float32)
    cf[:, 0:128] = np.eye(128, dtype=np.float32)
    r = np.arange(128)
    cf[:, 128:256] = (r[None, :] // 16 >= r[:, None] // 16).astype(np.float32)
    cf[:, 256:512] = np.arange(256, dtype=np.float32)[None, :]
    cf[:, 512:529] = np.array(KV, np.float32)[None, :]
    cf[:, 529:546] = (np.array(KV, np.float64) / TWO_PI).astype(np.float32)[None, :]
    inv_freq = 1.0 / (10000.0 ** (np.arange(0, 64, 2, dtype=np.float64) / 64.0))
    cf[:, 546] = (inv_freq[r % 32] / TWO_PI).astype(np.float32)
    cf[:, 547] = np.where(r < 64, -1.0, 1.0)
    cf[:, 548] = math.pi / 2
    cf[:, 549] = np.where(r < 64, 1.0, -1.0)
    cf[:, 550] = C2PI
    cf[:, 551] = np.where(r < 64, 1.0, -1.0) * C2PI
    cf[:, 552] = np.where((r % 64) < 32, -1.0, 1.0) * C2PI
    cf[:, 553] = LN_EPS
    cf[:, 554] = RMS_EPS
    cf[:, 555] = 0.0
    cf[:, 556] = 1.0
    cb = np.zeros((128, 512), np.float32)
    cb[:, 0:128] = np.eye(128)
    cb[:, 128:256] = (r[:, None] == (r[None, :] + 64) % 128)
    cb[:, 256:384] = np.where(r[None, :] >= r[:, None], 0.0, -10000.0)
    cb[:, 384:512] = 1.0
    return cf, cb.astype(ml_dtypes.bfloat16)


def build(stage=99, debug=False):
    nc = bass.Bass("TRN2", target_bir_lowering=False)

    def din(name, shape, dt=F32):
        return nc.dram_tensor(name, list(shape), dt, kind="ExternalInput").ap()

    x_d = din("x", [L, D])
    p_d = din("p", [L, 256])
    pos_d = din("pos", [1, L], I32)
    cf_d = din("cf", [128, NCF])
    cb_d = din("cb", [128, 512], BF16)
    g1_d = din("ln_emb_g", [1, D]); b1_d = din("ln_emb_b", [1, D])
    g2_d = din("ln_g", [1, D]); b2_d = din("ln_b", [1, D])
    win_d = din("w_in", [D, 1984])
    are_d = din("a_re", [32, 64]); aim_d = din("a_im", [32, 64]); ldt_d = din("log_dt", [1, 32])
    bre_d = din("b_re", [32, 64, 16]); bim_d = din("b_im", [32, 64, 16])
    cre_d = din("c_re", [512, 64]); cim_d = din("c_im", [512, 64])
    dsk_d = din("d_skip", [32, 16])
    wglu_d = din("w_glu", [512, 512]); bglu_d = din("b_glu", [512, 1])
    qg_d = din("q_norm_g", [256, 1]); wuq_d = din("w_uq", [256, 768])
    kvg_d = din("kv_norm_g", [128, 1]); wukv_d = din("w_ukv", [128, 1024])
    wout_d = din("w_out", [D, D]); wpg_d = din("w_pg", [D, D]); wpp_d = din("w_pp", [256, D])
    out_d = nc.dram_tensor("out", [L, D], F32, kind="ExternalOutput").ap()
    dbg_d = None
    if debug:
        dbg_d = nc.dram_tensor("dbg", [128, 8 * 2048], F32, kind="ExternalOutput").ap()

    es = contextlib.ExitStack()
    with es:
        S = Sched(nc, es)
        ARENA = 211968
        arena = es.enter_context(nc.sbuf_tensor("arena", [128, ARENA // 2], BF16))
        psum = [es.enter_context(nc.psum_tensor("ps%d" % i, [128, 512], F32)) for i in range(8)]
        PB = [Buf("psb%d" % i) for i in range(8)]

        def view(off, shape, dt):
            n = 1
            for s in shape:
                n *= s
            esz = 2 if dt == BF16 else 4
            assert off % 4 == 0 and off + n * esz <= ARENA, (off, shape)
            a = arena[:, off // 2: off // 2 + n * esz // 2]
            if dt != BF16:
                a = a.bitcast(dt)
            if len(shape) == 2:
                a = a.rearrange("p (a b) -> p a b", a=shape[0])
            elif len(shape) == 3:
                a = a.rearrange("p (a b c) -> p a b c", a=shape[0], b=shape[1])
            return a

        cf = view(0, [NCF], F32)
        cb = view(2304, [512], BF16)
        mixT = view(3328, [8, L], BF16)
        tmp = [view(36096 + 2048 * i, [512], F32) for i in range(4)]
        TB = [Buf("tmp%d" % i) for i in range(4)]
        PT = [view(44288 + 1024 * i, [512], BF16) for i in range(3)]
        PTB = [Buf("pt%d" % i) for i in range(3)]
        small = view(47360, [960], F32)
        ident_f = cf[:, 0:128]
        ident_b = cb[:, 0:128]
        pswap_b = cb[:, 128:256]
        trimask = cb[:, 256:384]
        ones_b = cb[:, 384:512]
        ZO = 51200
        xsT = view(ZO, [4, L], BF16)
        krT = view(ZO + 16384, [L], BF16)
        cqT = view(ZO + 20480, [2, L], BF16)
        ckvT = view(ZO + 28672, [L], BF16)
        COSr = view(ZO + 32768, [L], F32)
        SINr = view(ZO + 40960, [L], F32)
        WO = 100352

        B_const = Buf("const")
        B_mix = [[Buf("mix%d_%d" % (k, c)) for c in range(4)] for k in range(8)]
        B_xs = [Buf("xs%d" % q) for q in range(4)]
        B_cq = [Buf("cq%d" % c) for c in range(4)]
        B_ckv = [Buf("ckv%d" % c) for c in range(4)]
        B_kr = [Buf("kr%d" % c) for c in range(4)]
        B_rope = Buf("rope")
        B_small = Buf("small")

        S.dma("sp", cf, cf_d, w=[B_const])
        S.dma("sp", cb, cb_d, w=[B_const])

        def act(out, in_, func, r, w, scale=1.0, bias=None, eng="act"):
            if bias is None:
                return S.op("act", lambda e: e.activation(out=out, in_=in_, func=func, scale=scale), r, w)
            return S.op("act", lambda e: e.activation(out=out, in_=in_, func=func, scale=scale, bias=bias), r, w)

        def tt(eng, out, in0, in1, op, r, w):
            return S.op(eng, lambda e: e.tensor_tensor(out=out, in0=in0, in1=in1, op=op), r, w)

        def ts(eng, out, in0, s1, s2, op0, op1, r, w):
            if s2 is None:
                return S.op(eng, lambda e: e.tensor_scalar(out=out, in0=in0, scalar1=s1, scalar2=None, op0=op0), r, w)
            return S.op(eng, lambda e: e.tensor_scalar(out=out, in0=in0, scalar1=s1, scalar2=s2, op0=op0, op1=op1), r, w)

        def cp(eng, out, in_, r, w):
            if eng == "act":
                return S.op("act", lambda e: e.copy(out=out, in_=in_), r, w)
            return S.op(eng, lambda e: e.tensor_copy(out=out, in_=in_), r, w)

        def mm(out, lhsT, rhs, start, stop, r, w, inc=None):
            if inc is None:
                inc = stop
            return S.op("pe", lambda e: e.matmul(out, lhsT=lhsT, rhs=rhs, start=start, stop=stop), r, w, inc=inc)

        def tr(out, in_, r, w, inc=True):
            return S.op("pe", lambda e: e.transpose(out=out, in_=in_, identity=ident_f), r, w, inc=inc)

        def sincos(eng, y, k_i32, fr, afr, sin_out, cos_out, sin_scale, r, w_tmp, w_out):
            cp(eng, k_i32, y, r + w_tmp, w_tmp)
            tt(eng, fr, y, k_i32, ALU.subtract, w_tmp, w_tmp)
            act(afr, fr, AF.Abs, w_tmp, w_tmp)
            act(sin_out, fr, AF.Sin, w_tmp, w_out, scale=sin_scale)
            act(cos_out, afr, AF.Sin, w_tmp, w_out, scale=-C2PI, bias=cf[:, 548:549])

        def ln_stats(xin, stat, r, rb, nmr=False):
            st = stat[:, 0:12].rearrange("p (a b) -> p a b", a=2)
            S.op("dve", lambda e: e.bn_stats(out=st[:, 0, :], in_=xin[:, 0:512]), r, [rb])
            S.op("dve", lambda e: e.bn_stats(out=st[:, 1, :], in_=xin[:, 512:1024]), r, [rb])
            S.op("dve", lambda e: e.bn_aggr(out=stat[:, 12:14], in_=stat[:, 0:12]), [rb], [rb])
            act(stat[:, 14:15], stat[:, 13:14], AF.Ln, [rb, B_const], [rb], bias=cf[:, 553:554])
            act(stat[:, 15:16], stat[:, 14:15], AF.Exp, [rb], [rb], scale=-0.5)
            if nmr:
                ts("dve", stat[:, 16:17], stat[:, 12:13], stat[:, 15:16], -1.0, ALU.mult, ALU.mult, [rb], [rb])

        def ln_apply(xin, xout, gt, bt, stat, r, w, rb, beng="pool", norm="dve"):
            if norm == "act":
                act(xout, xin, AF.Identity, r + [rb], w, scale=stat[:, 15:16], bias=stat[:, 16:17])
            else:
                ts("dve", xout, xin, stat[:, 12:13], stat[:, 15:16], ALU.subtract, ALU.mult, r + [rb], w)
            tt("dve", xout, xout, gt, ALU.mult, w, w)
            tt(beng, xout, xout, bt, ALU.add, w, w)

        xt = [view(WO + 4096 * i, [D], F32) for i in range(2)]
        xn = [view(WO + 8192 + 4096 * i, [D], F32) for i in range(2)]
        g1 = view(WO + 16384, [D], F32)
        b1 = view(WO + 20480, [D], F32)
        xnT = view(WO + 24576, [8, L], BF16)
        wall = [view(WO + 57344 + 2048 * i, [8, 128], BF16) for i in range(16)]
        wkr = view(WO + 57344 + 2048 * 16, [8, 256], BF16)
        rt0 = view(ZO, [L], F32)
        rt1 = view(ZO + 8192, [L], F32)
        rt2 = view(ZO + 20480, [L], F32)
        B_xt = [Buf("xt0"), Buf("xt1")]
        B_xn = [Buf("xn0"), Buf("xn1")]
        B_g1 = Buf("g1b1")
        B_xnT = [Buf("xnT%d" % c) for c in range(4)]
        B_wall = [Buf("wall%d" % i) for i in range(16)]
        B_wkr = Buf("wkr")
        B_rt = Buf("rt")
        stat = [small[:, 0:20], small[:, 20:40]]
        B_stat = [Buf("stat0"), Buf("stat1")]

        S.dma("sp", g1, g1_d.partition_broadcast(128)[:, 0, :], w=[B_g1])
        S.dma("sp", b1, b1_d.partition_broadcast(128)[:, 0, :], w=[B_g1])

        mtiles = [("xs", q, 128 * q) for q in range(4)] + [("gs", q, 512 + 128 * q) for q in range(4)] + \
                 [("cq", j, 1024 + 128 * j) for j in range(2)] + [("ckv", 0, 1280)] + \
                 [("gm", h, 1472 + 128 * h) for h in range(4)] + [("kr", 0, 1408)]
        win_v = win_d.rearrange("(k p) n -> p k n", p=128)
        def wload(mi):
            kind, idx, col = mtiles[mi]
            if kind != "kr":
                S.dma("pool", wall[mi], win_v[:, :, col:col + 128], w=[B_wall[mi]])
            else:
                S.dma("pool", wall[mi][:, :, 0:64], win_v[:, :, col:col + 64], w=[B_wall[mi]])

        def wkr_build():
            krsl = wall[15][:, :, 0:64]
            for (o, s0, wd) in [(0, 0, 64), (64, 0, 64), (128, 32, 32), (160, 0, 32), (192, 32, 32), (224, 0, 32)]:
                cp("pool", wkr[:, :, o:o + wd], krsl[:, :, s0:s0 + wd], [B_wall[15]], [B_wkr])

        for mi in (15, 0, 1, 2, 3):
            wload(mi)

        def p1_a1(i):
            s_ = i % 2
            S.dma("sp", xt[s_], x_d[i * 128:(i + 1) * 128, :], w=[B_xt[s_]])
            ln_stats(xt[s_], stat[s_], [B_xt[s_]], B_stat[s_])

        def p1_a2(i):
            s_ = i % 2
            ln_apply(xt[s_], xn[s_], g1, b1, stat[s_], [B_xt[s_], B_g1], [B_xn[s_]], B_stat[s_], beng="dve")

        def p1_b(i):
            s_ = i % 2
            for hb in range(2):
                pb = hb
                for j in range(4):
                    k = hb * 4 + j
                    tr(psum[pb][:, j * 128:(j + 1) * 128], xn[s_][:, k * 128:(k + 1) * 128],
                       [B_xn[s_], B_const], [PB[pb]], inc=(j == 3))
                dst = xnT[:, hb * 4:hb * 4 + 4, i * 128:(i + 1) * 128]
                src = psum[pb].rearrange("p (a b) -> p a b", a=4)
                cp("act", dst, src, [PB[pb]], [B_xnT[i // 4]])

        xs_v = xsT.rearrange("p q (t c) -> p q t c", t=8)
        rot2 = [0]

        def p2_unit(mi, cc):
            kind, idx, col = mtiles[mi]
            cs = slice(cc * 512, (cc + 1) * 512)
            if kind != "kr":
                lhs_list = [(wall[mi], B_wall[mi], None)]
            else:
                lhs_list = [(wkr[:, :, 0:128], B_wkr, 6), (wkr[:, :, 128:256], B_wkr, 7)]
            pbs = []
            for (lw, lb, fixed) in lhs_list:
                if fixed is None:
                    pb = 2 + rot2[0] % 4
                    rot2[0] += 1
                else:
                    pb = fixed
                pbs.append(pb)
                for k in range(8):
                    mm(psum[pb][:, :], lw[:, k, :], xnT[:, k, cs], k == 0, k == 7, [lb, B_xnT[cc]], [PB[pb]])
            pb = pbs[0]
            if kind == "xs":
                src = psum[pb].rearrange("p (c t) -> p t c", t=8)
                cp("dve", xs_v[:, idx, :, cc * 64:(cc + 1) * 64], src, [PB[pb]], [B_xs[idx], B_rt])
            elif kind == "gs":
                act(mixT[:, idx, cs], psum[pb][:, :], AF.Silu, [PB[pb]], [B_mix[idx][cc]])
            elif kind == "gm":
                act(mixT[:, 4 + idx, cs], psum[pb][:, :], AF.Silu, [PB[pb]], [B_mix[4 + idx][cc]])
            elif kind == "cq":
                cp("dve", cqT[:, idx, cs], psum[pb][:, :], [PB[pb]], [B_cq[cc], B_rt])
            elif kind == "ckv":
                cp("dve", ckvT[:, cs], psum[pb][:, :], [PB[pb]], [B_ckv[cc]])
            elif kind == "kr":
                tt("dve", tmp[0], psum[pbs[0]][:, :], COSr[:, cs], ALU.mult, [PB[pbs[0]], B_rope], [TB[0]])
                tt("dve", tmp[1], psum[pbs[1]][:, :], SINr[:, cs], ALU.mult, [PB[pbs[1]], B_rope], [TB[1]])
                tt("dve", krT[:, cs], tmp[0], tmp[1], ALU.add, [TB[0], TB[1]], [B_kr[cc]])

        def rope_gen():
            S.dma("sp", rt0.bitcast(I32), pos_d.partition_broadcast(128)[:, 0, :], w=[B_rt])
            cp("dve", rt1, rt0.bitcast(I32), [B_rt], [B_rt])
            ts("dve", rt1, rt1, cf[:, 546:547], None, ALU.mult, None, [B_rt, B_const], [B_rt])
            sincos("dve", rt1, rt0.bitcast(I32), rt2, rt0, SINr, COSr, cf[:, 552:553], [B_const], [B_rt], [B_rope])

        units = {cc: [(mi, cc) for mi in range(16)] for cc in range(4)}
        split = [4, 4, 4, 4]
        rope_gen()
        for step in range(16 + 2):
            if 0 <= step - 2 < 16:
                p1_b(step - 2)
            if 0 <= step - 1 < 16:
                p1_a2(step - 1)
            if step < 16:
                p1_a1(step)
            if step == 1:
                for mi in (4, 5, 6, 7):
                    wload(mi)
            if step == 2:
                for mi in (8, 9, 10, 11):
                    wload(mi)
            if step == 3:
                for mi in (12, 13, 14):
                    wload(mi)
                wkr_build()
            t_done = step - 2
            cprev = ((t_done + 1) // 4) - 1 if t_done >= 0 else -1
            if t_done >= 0 and 0 <= cprev < 3:
                part = (t_done + 1) % 4
                a0 = sum(split[:part])
                for (mi, cc) in units[cprev][a0:a0 + split[part]]:
                    p2_unit(mi, cc)
        xs_scr = nc.dram_tensor("xs_scr", [512, L], BF16, kind="ExternalOutput").ap()
        y_scr = nc.dram_tensor("y_scr", [512, L], BF16, kind="ExternalOutput").ap()
        B_xscr = [Buf("xscr%d" % q) for q in range(4)]
        B_yscr = [Buf("yscr%d" % q) for q in range(4)]
        xs_scr_v = xs_scr.rearrange("(g c) (t n) -> t c g n", c=16, t=8)
        y_scr_v = y_scr.rearrange("(g c) (t n) -> t c g n", c=16, t=8)
        U3 = []
        S.rec = U3
        for (mi, cc) in units[3]:
            p2_unit(mi, cc)
            if mtiles[mi][0] == "xs":
                q = mtiles[mi][1]
                S.dma("sp", xs_scr[q * 128:(q + 1) * 128, :], xsT[:, q, :], r=[B_xs[q]], w=[B_xscr[q]])
        S.rec = None
        if debug and stage == 2:
            for f_ in U3:
                f_()

        if debug and stage == 2:
            S.barrier()
            dv = dbg_d.rearrange("p (a b) -> p a b", a=8)
            B_dbg = Buf("dbg")
            srcs = [xsT[:, 0, :], xsT[:, 3, :], cqT[:, 0, :], cqT[:, 1, :], ckvT, krT, mixT[:, 1, :], mixT[:, 6, :]]
            for n_, sv in enumerate(srcs):
                for cc in range(4):
                    cs = slice(cc * 512, (cc + 1) * 512)
                    cp("dve", tmp[cc % 4], sv[:, cs], [], [TB[cc % 4]])
                    S.dma("sp", dv[:, n_, cs], tmp[cc % 4], r=[TB[cc % 4]], w=[B_dbg])
            S._wait("sp", B_dbg.w)
            return nc


        S.barrier()
        cur = [WO]

        def alloc(shape, dt):
            n = 1
            for v_ in shape:
                n *= v_
            esz = 2 if dt == BF16 else 4
            off = cur[0]
            cur[0] += (n * esz + 3) // 4 * 4
            return view(off, shape, dt)

        B_s5 = Buf("s5par")
        areT = alloc([32], F32); aimT = alloc([32], F32); ldt = alloc([32], F32); dtt = alloc([32], F32)
        lam = alloc([32], F32); th = alloc([32], F32)
        Bst = alloc([32, 16], F32); Bsw = alloc([32, 16], F32); Cst = alloc([32, 16], F32); Csw = alloc([32, 16], F32)
        s_a = [alloc([32], F32) for _ in range(8)]
        FXre = alloc([8, 32], F32); FXim = alloc([8, 32], F32); FZre = alloc([8, 32], F32); FZim = alloc([8, 32], F32)
        FXims = alloc([8, 32], F32); FZims = alloc([8, 32], F32); FZimB = alloc([8, 32], F32)
        FY9imn = alloc([9, 32], F32)
        f_t = [small[:, 448:704].rearrange("p (a b) -> p a b", a=8), small[:, 704:960].rearrange("p (a b) -> p a b", a=8)]
        Dst = alloc([32], F32)
        Pre = alloc([17, 32], F32); Pim = alloc([17, 32], F32)
        mag8 = alloc([32], F32); fr16 = alloc([32], F32)
        assert cur[0] <= WO + 24576, cur[0] - WO
        T0 = WO + 94208
        cur[0] = T0
        EL = alloc([17, 32], F32); Ee = alloc([17, 32], F32); Y1 = alloc([17, 32], F32); K1 = alloc([17, 32], I32)
        FR = alloc([17, 32], F32); AFR = alloc([17, 32], F32); SN = alloc([17, 32], F32); CS = alloc([17, 32], F32)
        assert cur[0] <= ARENA
        cur[0] = WO + 24576
        Xg = alloc([8, 8, 16], F32); Yg = alloc([8, 9, 16], F32); Zg = alloc([8, 8, 16], F32); Zs = alloc([8, 8, 16], F32)
        gt1 = alloc([8, 8, 16], F32); gt2 = alloc([8, 9, 16], F32); gt3 = Zs
        Wtoep = [alloc([8, 128], BF16) for _ in range(2)]; Wend = [alloc([8, 128], BF16) for _ in range(2)]
        Wendsw = [alloc([8, 128], BF16) for _ in range(2)]; Wc = [alloc([8, 8, 16], BF16) for _ in range(2)]
        Uu = [alloc([8, 256], BF16) for _ in range(2)]; Hprev = alloc([8, 256], BF16)
        st4 = [alloc([256], F32) for _ in range(2)]
        Gb = [alloc([256], BF16) for _ in range(2)]
        assert cur[0] <= WO + 90112
        wglu = view(WO + 90112, [4, 512], BF16)
        bglu = small[:, 404:408]
        st1 = [tmp[0][:, 0:256], tmp[0][:, 256:512]]; st2 = [tmp[1][:, 0:256], tmp[1][:, 256:512]]
        sxr = [tmp[2][:, 0:256], tmp[2][:, 256:512]]; st3 = [tmp[3][:, 0:256], tmp[3][:, 256:512]]
        B_wglu = Buf("wglu"); B_bglu = Buf("bglu")

        SU = []
        S.rec = SU
        B_sta = [Buf("sta0"), Buf("sta1")]
        sta = [view(T0, [128], F32), view(T0 + 512, [128], F32)]
        ccs = [[view(T0 + 1024 + 1024 * q, [128], F32), view(T0 + 1536 + 1024 * q, [128], F32)] for q in range(4)]
        B_ccs = [Buf("ccs%d" % q) for q in range(4)]
        B_dst = Buf("dst"); B_bld = Buf("bld")
        for i_, src in enumerate([are_d, aim_d]):
            S.dma("sp", sta[i_][0:32, 0:64], src, w=[B_sta[i_]])
            S.dma("sp", sta[i_][0:32, 64:128], src, w=[B_sta[i_]])
        S.dma("sp", ldt, ldt_d.partition_broadcast(128)[:, 0, :], w=[B_bld])
        for q in range(4):
            S.dma("sp", ccs[q][0][:, 0:64], cre_d[q * 128:(q + 1) * 128, :], w=[B_ccs[q]])
            S.dma("sp", ccs[q][0][:, 64:128], cim_d[q * 128:(q + 1) * 128, :], w=[B_ccs[q]])
            S.dma("sp", ccs[q][1][:, 0:64], cim_d[q * 128:(q + 1) * 128, :], w=[B_ccs[q]])
            S.dma("sp", ccs[q][1][:, 64:128], cre_d[q * 128:(q + 1) * 128, :], w=[B_ccs[q]])
        S.dma("sp", Bst[0:64], bre_d.rearrange("g p c -> p g c"), w=[B_bld])
        S.dma("sp", Bst[64:128], bim_d.rearrange("g p c -> p g c"), w=[B_bld])
        S.dma("sp", Bsw[0:64], bim_d.rearrange("g p c -> p g c"), w=[B_bld])
        S.dma("sp", Bsw[64:128], bre_d.rearrange("g p c -> p g c"), w=[B_bld])
        for t_ in range(8):
            S.dma("sp", Dst[t_ * 16:(t_ + 1) * 16, :], dsk_d.rearrange("g c -> c g"), w=[B_dst],
                  allow_slow_non_contiguous=True)
        for i_, dstT in enumerate([areT, aimT]):
            S.op("pe", lambda e, i_=i_: e.transpose(out=psum[i_][:, 0:32], in_=sta[i_][0:32, :], identity=ident_f[0:32, 0:32]),
                 [B_sta[i_], B_const], [PB[i_]])
            cp("dve", dstT, psum[i_][:, 0:32], [PB[i_]], [B_s5])
        Cst_f = Cst.rearrange("p g c -> p (g c)")
        Csw_f = Csw.rearrange("p g c -> p (g c)")
        for q in range(4):
            pa, pb_ = 0, 1
            tr(psum[pa][:, 0:128], ccs[q][0], [B_ccs[q], B_const], [PB[pa]])
            tr(psum[pb_][:, 0:128], ccs[q][1], [B_ccs[q]], [PB[pb_]])
            cp("dve", Cst_f[:, q * 128:(q + 1) * 128], psum[pa][:, 0:128], [PB[pa]], [B_s5])
            cp("act", Csw_f[:, q * 128:(q + 1) * 128], psum[pb_][:, 0:128], [PB[pb_]], [B_s5])
        ts("dve", Cst[64:128], Cst[64:128], -1.0, None, ALU.mult, None, [B_s5], [B_s5])
        for bb in B_sta + B_ccs:
            B_s5.r.extend(bb.r)

        P5 = [B_s5, B_const, B_bld]

        def v(out, a, b, op, eng="dve"):
            tt(eng, out, a, b, op, P5, [B_s5])

        act(dtt, ldt, AF.Exp, P5, [B_s5])
        v(lam, dtt, areT, ALU.mult)
        v(th, dtt, aimT, ALU.mult)
        bc17 = lambda a: a.unsqueeze(1).to_broadcast([128, 17, 32])
        kvb = cf[:, 512:529].unsqueeze(2).to_broadcast([128, 17, 32])
        kv2b = cf[:, 529:546].unsqueeze(2).to_broadcast([128, 17, 32])
        v(EL, bc17(lam), kvb, ALU.mult)
        act(Ee, EL, AF.Exp, P5, [B_s5])
        v(Y1, bc17(th), kv2b, ALU.mult)
        cp("dve", K1, Y1, P5, [B_s5])
        v(FR, Y1, K1, ALU.subtract)
        act(AFR, FR, AF.Abs, P5, [B_s5])
        act(SN, FR, AF.Sin, P5, [B_s5], scale=C2PI)
        act(CS, AFR, AF.Sin, P5, [B_s5], scale=-C2PI, bias=cf[:, 548:549])
        v(Pre, Ee, CS, ALU.mult)
        v(Pim, Ee, SN, ALU.mult)
        cp("dve", mag8, Ee[:, 16, :], P5, [B_s5])
        cp("dve", fr16, FR[:, 16, :], P5, [B_s5])
        abre = Pre[:, 9, :]; abim = Pim[:, 9, :]
        nr, den, rden, t_a, t_b, kre, kim, t_c = s_a
        ts("dve", nr, abre, -1.0, None, ALU.add, None, P5, [B_s5])
        v(den, areT, areT, ALU.mult)
        v(t_a, aimT, aimT, ALU.mult)
        v(den, den, t_a, ALU.add)
        S.op("dve", lambda e: e.reciprocal(out=rden, in_=den), P5, [B_s5])
        v(t_a, nr, areT, ALU.mult); v(t_b, abim, aimT, ALU.mult); v(t_a, t_a, t_b, ALU.add); v(kre, t_a, rden, ALU.mult)
        v(t_a, abim, areT, ALU.mult); v(t_b, nr, aimT, ALU.mult); v(t_a, t_a, t_b, ALU.subtract); v(kim, t_a, rden, ALU.mult)
        bc8 = lambda a: a.unsqueeze(1).to_broadcast([128, 8, 32])
        v(FXre, Pre[:, 0:8, :], bc8(kre), ALU.mult); v(f_t[0], Pim[:, 0:8, :], bc8(kim), ALU.mult)
        v(FXre, FXre, f_t[0], ALU.subtract)
        v(FXim, Pre[:, 0:8, :], bc8(kim), ALU.mult); v(f_t[0], Pim[:, 0:8, :], bc8(kre), ALU.mult)
        v(FXim, FXim, f_t[0], ALU.add)
        a7re = bc8(Pre[:, 15, :]); a7im = bc8(Pim[:, 15, :])
        v(FZre, FXre, a7re, ALU.mult); v(f_t[0], FXim, a7im, ALU.mult); v(FZre, FZre, f_t[0], ALU.subtract)
        v(FZim, FXre, a7im, ALU.mult); v(f_t[0], FXim, a7re, ALU.mult); v(FZim, FZim, f_t[0], ALU.add)
        ts("dve", FXims, FXim, cf[:, 547:548], None, ALU.mult, None, P5, [B_s5])
        ts("dve", FZims, FZim, cf[:, 547:548], None, ALU.mult, None, P5, [B_s5])
        ts("dve", FZimB, FZim, cf[:, 549:550], None, ALU.mult, None, P5, [B_s5])
        ts("dve", FY9imn, Pim[:, 8:17, :], -1.0, None, ALU.mult, None, P5, [B_s5])
        S.rec = None
        nu_, ns_ = len(U3), len(SU)
        iu = is_ = 0
        while iu < nu_ or is_ < ns_:
            if is_ >= ns_ or (iu < nu_ and iu * ns_ <= is_ * nu_):
                U3[iu](); iu += 1
            else:
                SU[is_](); is_ += 1
        S.barrier()
        S.dma("pool", wglu, wglu_d.rearrange("(k p) n -> p k n", p=128), w=[B_wglu])
        S.dma("sp", bglu, bglu_d.rearrange("(k p) o -> p (k o)", p=128), w=[B_bglu], allow_slow_non_contiguous=True)
        tabC = view(T0, [8, 256], F32)
        tabS = view(T0 + 8192, [8, 256], F32)

        B_X = Buf("genX"); B_Y = Buf("genY"); B_Z = Buf("genZ"); B_Zs = Buf("genZs"); B_gt = Buf("gt"); B_gt1 = Buf("gt1")
        B_W = [Buf("s5w0"), Buf("s5w1")]
        B_Wc = [Buf("s5wc0"), Buf("s5wc1")]
        B_U = [[Buf("U%d_%d" % (w_, g)) for g in range(8)] for w_ in range(2)]
        B_H = [Buf("H%d" % g) for g in range(8)]
        B_tab = [Buf("tab%d" % i) for i in range(8)]
        B_tt = Buf("tabtmp")
        B_st1 = [Buf("st1_0"), Buf("st1_1")]; B_st2 = [Buf("st2_0"), Buf("st2_1")]; B_sxr = [Buf("sxr0"), Buf("sxr1")]
        B_st3 = [Buf("st3_0"), Buf("st3_1")]; B_st4 = [Buf("st4_0"), Buf("st4_1")]; B_yp = [Buf("yp0"), Buf("yp1")]
        B_G = [Buf("G0"), Buf("G1")]
        PBG = [PB[2], PB[3]]; PBY = [PB[6], PB[7]]
        psG = [psum[2][:, 0:256], psum[3][:, 0:256]]
        psY = [psum[6][:, 0:256], psum[7][:, 0:256]]

        def gen(eng, out, Mst, Msw, Fre, Fim, gs, wbuf, to_tmp=False, nj=8):
            in0 = Mst[:, gs, :].unsqueeze(2).to_broadcast([128, 8, nj, 16])
            in0s = Msw[:, gs, :].unsqueeze(2).to_broadcast([128, 8, nj, 16])
            f1 = Fre[:, :, gs].rearrange("p j g -> p g j").unsqueeze(3).to_broadcast([128, 8, nj, 16])
            f2 = Fim[:, :, gs].rearrange("p j g -> p g j").unsqueeze(3).to_broadcast([128, 8, nj, 16])
            if to_tmp:
                tt(eng, gt1, in0, f1, ALU.mult, P5, [B_gt1])
                tt(eng, gt3, in0s, f2, ALU.mult, P5, [B_gt1])
                tt(eng, out, gt1, gt3, ALU.add, [B_gt1], [wbuf])
            else:
                tt(eng, out, in0, f1, ALU.mult, P5, [wbuf])
                g2v = gt2[:, :, 0:nj, :]
                tt(eng, g2v, in0s, f2, ALU.mult, P5, [B_gt])
                tt(eng, out, out, g2v, ALU.add, [B_gt], [wbuf])

        tmask4 = cf[:, 128:256].unsqueeze(1).to_broadcast([128, 4, 128])

        def uload(qb):
            ws = qb % 2
            for t_ in range(8):
                S.dma("sp", Uu[ws][t_ * 16:(t_ + 1) * 16, :, :], xs_scr_v[t_][:, 8 * qb:8 * qb + 8, :],
                      r=[B_xscr[qb]], w=B_U[ws])

        def gen_pool_xyz(qb):
            gs = slice(qb * 8, (qb + 1) * 8)
            gen("dve", Xg, Bst, Bsw, FXre, FXims, gs, B_X)
            gen("dve", Yg, Cst, Csw, Pre[:, 8:17, :], FY9imn, gs, B_Y, nj=9)
            gen("dve", Zg, Bst, Bsw, FZre, FZims, gs, B_Z)

        def gen_pool_wc(qb):
            gs = slice(qb * 8, (qb + 1) * 8)
            cp("act", Wc[qb % 2], Yg[:, :, 1:9, :], [B_Y], [B_Wc[qb % 2]])

        def gen_pe(qb):
            ws = qb % 2
            for hb in range(2):
                pb = 4 + hb
                for j in range(4):
                    g = hb * 4 + j
                    S.op("pe", lambda e: e.matmul(psum[pb][:, j * 128:(j + 1) * 128],
                                                  lhsT=Xg[:, g].rearrange("p a b -> p (a b)"),
                                                  rhs=Yg[:, g, 0:8, :].rearrange("p a b -> p (a b)"), start=True, stop=True),
                         [B_X, B_Y], [PB[pb]], inc=(j == 3))
                tt("dve", Wtoep[ws][:, hb * 4:hb * 4 + 4, :], psum[pb].rearrange("p (a b) -> p a b", a=4), tmask4,
                   ALU.mult, [PB[pb], B_const], [B_W[ws]])
            for hb in range(2):
                pb = 4 + hb
                for j in range(4):
                    g = hb * 4 + j
                    tr(psum[pb][:, j * 128:(j + 1) * 128], Zg[:, g].rearrange("p a b -> p (a b)"),
                       [B_Z, B_const], [PB[pb]], inc=(j == 3))
                pv = psum[pb].rearrange("p (a b) -> p a b", a=4)
                cp("act", Wend[ws][:, hb * 4:hb * 4 + 4, :], pv, [PB[pb]], [B_W[ws]])
                cp("act", Wendsw[ws][:, hb * 4:hb * 4 + 4, 0:64], pv[:, :, 64:128], [PB[pb]], [B_W[ws]])
                cp("act", Wendsw[ws][:, hb * 4:hb * 4 + 4, 64:128], pv[:, :, 0:64], [PB[pb]], [B_W[ws]])

        def tables(g):
            sl8 = g % 8
            tS = tabS[:, sl8, :]; tC = tabC[:, sl8, :]
            bt_ = [B_tab[sl8]]
            act(tS, cf[:, 256:512], AF.Identity, P5, bt_, scale=fr16[:, g:g + 1])
            act(tC.bitcast(I32), cf[:, 256:512], AF.Identity, P5, bt_, scale=fr16[:, g:g + 1])
            tt("dve", tS, tS, tC.bitcast(I32), ALU.subtract, bt_, bt_)
            act(tC, tS, AF.Abs, bt_, bt_)
            act(tC, tC, AF.Sin, bt_ + [B_const], bt_, scale=-C2PI, bias=cf[:, 548:549])
            act(tS, tS, AF.Sin, bt_ + [B_const], bt_, scale=cf[:, 551:552])

        def s0(g):
            qb, gl = divmod(g, 8)
            ws = qb % 2; sl = g % 2
            mm(psum[sl][:, 0:256], Wend[ws][:, gl, :], Uu[ws][:, gl, :], True, True, [B_W[ws], B_U[ws][gl]], [PB[sl]], inc=False)
            mm(psum[sl][:, 256:512], Wendsw[ws][:, gl, :], Uu[ws][:, gl, :], True, True, [B_W[ws], B_U[ws][gl]], [PB[sl]])

        def s1(g):
            sl = g % 2; sl8 = g % 8
            tt("dve", st1[sl], psum[sl][:, 0:256], tabC[:, sl8, :], ALU.mult, [PB[sl], B_tab[sl8]], [B_st1[sl]])
            tt("dve", st2[sl], psum[sl][:, 256:512], tabS[:, sl8, :], ALU.mult, [PB[sl], B_tab[sl8]], [B_st2[sl]])
            tt("dve", sxr[sl], st1[sl], st2[sl], ALU.add, [B_st1[sl], B_st2[sl]], [B_sxr[sl]])
            S.op("dve", lambda e: e.tensor_tensor_scan(out=Gb[sl], data0=mag8[:, g:g + 1].to_broadcast([128, 256]),
                                                       data1=sxr[sl], initial=0.0, op0=ALU.mult, op1=ALU.add),
                 [B_sxr[sl], B_s5], [B_G[sl]])

        def s2(g):
            qb, gl = divmod(g, 8)
            sl = g % 2; sl8 = g % 8
            mm(psG[sl], pswap_b, Gb[sl], True, True, [B_G[sl], B_const], [PBG[sl]])
            tt("pool", st3[sl], Gb[sl], tabC[:, sl8, :], ALU.mult, [B_G[sl], B_tab[sl8]], [B_st3[sl]])
            tt("dve", st4[sl], psG[sl], tabS[:, sl8, :], ALU.mult, [PBG[sl], B_tab[sl8]], [B_st4[sl]])
            tt("pool", Hprev[:, gl, 1:256], st3[sl][:, 0:255], st4[sl][:, 0:255], ALU.subtract,
               [B_st3[sl], B_st4[sl]], [B_H[gl]])

        def s3(g):
            qb, gl = divmod(g, 8)
            ws = qb % 2; sl = g % 2
            mm(psY[sl], Wtoep[ws][:, gl, :], Uu[ws][:, gl, :], True, False, [B_W[ws], B_U[ws][gl]], [PBY[sl]], inc=False)
            mm(psY[sl], Wc[ws][:, gl].rearrange("p a b -> p (a b)"), Hprev[:, gl, :], False, True,
               [B_Wc[ws], B_H[gl]], [PBY[sl]])
            S.op("dve", lambda e: e.scalar_tensor_tensor(out=Uu[ws][:, gl, :], in0=Uu[ws][:, gl, :], scalar=Dst[:, g:g + 1],
                                                          in1=psY[sl], op0=ALU.mult, op1=ALU.add),
                 [PBY[sl], B_dst], [B_U[ws][gl]])

        def writeback(qb):
            ws = qb % 2
            for t_ in range(8):
                S.dma("sp", y_scr_v[t_][:, 8 * qb:8 * qb + 8, :], Uu[ws][t_ * 16:(t_ + 1) * 16, :, :],
                      r=B_U[ws], w=[B_yscr[qb]])
            S.dma("sp", xsT[:, qb, :], y_scr[qb * 128:(qb + 1) * 128, :], r=[B_yscr[qb]], w=[B_xs[qb]])

        for gl_ in range(8):
            S.op("pool", lambda e, gl_=gl_: e.memset(Hprev[:, gl_, 0:1], 0.0), [], [B_H[gl_]])
        NG = 32
        uload(0); gen_pool_xyz(0); gen_pool_wc(0)
        for g in range(4):
            tables(g)
        gen_pe(0)
        for step in range(NG + 4):
            if 0 <= step - 3 < NG:
                s3(step - 3)
                if (step - 3) % 8 == 7:
                    writeback((step - 3) // 8)
            if 0 <= step - 2 < NG:
                s2(step - 2)
            if 4 <= step + 3 < NG:
                tables(step + 3)
            if 0 <= step - 1 < NG:
                s1(step - 1)
            nb = (step + 7) // 8
            if step + 7 == 8 * nb and nb < 4:
                gen_pool_xyz(nb)
            if step + 5 == 8 * nb and nb < 4 and nb >= 1:
                uload(nb)
                gen_pool_wc(nb)
            if step + 2 == 8 * ((step + 2) // 8) and 1 <= (step + 2) // 8 < 4:
                gen_pe((step + 2) // 8)
            if step < NG:
                s0(step)


        hbglu = small[:, 400:404]
        B_hb = Buf("hbglu")

        def glu_pre():
            for q in range(4):
                for cc in range(4):
                    cs_ = slice(cc * 512, (cc + 1) * 512)
                    act(xsT[:, q, cs_], xsT[:, q, cs_], AF.Gelu, [], [B_xs[q]])
            ts("dve", hbglu, bglu, 0.5, None, ALU.mult, None, [B_bglu], [B_hb])

        def glu_units():
            for m in range(4):
                for cc in range(4):
                    c0 = cc * 64
                    pb = 7
                    ncs = slice(cc * 512, (cc + 1) * 512)
                    yv = lambda k_: xsT[:, k_, :].rearrange("p (t c) -> p t c", t=8)[:, :, c0:c0 + 64]
                    for k in range(4):
                        mm(psum[pb][:, :], wglu[:, k, m * 128:(m + 1) * 128], yv(k), k == 0, k == 3,
                           [B_wglu, B_xs[k]], [PB[pb]])
                    act(tmp[0], psum[pb][:, :], AF.Tanh, [PB[pb], B_hb], [TB[0]], scale=0.5, bias=hbglu[:, m:m + 1])
                    t0v = tmp[0].rearrange("p (t c) -> p t c", t=8)
                    t1v = tmp[1].rearrange("p (t c) -> p t c", t=8)
                    ym_ = yv(m)
                    S.op("dve", lambda e, t0v=t0v, t1v=t1v, ym_=ym_: e.scalar_tensor_tensor(
                        out=t1v, in0=t0v, scalar=1.0, in1=ym_, op0=ALU.add, op1=ALU.mult), [B_xs[m], TB[0]], [TB[1]])
                    mv2 = mixT[:, m, ncs].rearrange("p (c t) -> p c t", t=8)
                    t1c = tmp[1].rearrange("p (t c) -> p c t", t=8)
                    S.op("dve", lambda e, mv2=mv2, t1c=t1c: e.scalar_tensor_tensor(
                        out=mv2, in0=t1c, scalar=0.5, in1=mv2, op0=ALU.mult, op1=ALU.mult), [TB[1]], [B_mix[m][cc]])

        def glu_all():
            glu_pre()
            glu_units()

        if debug and stage == 4:
            glu_all()
            S.barrier()
            dv = dbg_d.rearrange("p (a b) -> p a b", a=8)
            B_dbg = Buf("dbg")
            srcs = [mixT[:, 0, :], mixT[:, 1, :], mixT[:, 2, :], mixT[:, 3, :], xsT[:, 0, :], xsT[:, 1, :], xsT[:, 2, :], xsT[:, 3, :]]
            for n_, sv in enumerate(srcs):
                for cc in range(4):
                    cs = slice(cc * 512, (cc + 1) * 512)
                    cp("dve", tmp[cc % 4], sv[:, cs], [], [TB[cc % 4]])
                    S.dma("sp", dv[:, n_, cs], tmp[cc % 4], r=[TB[cc % 4]], w=[B_dbg])
            S._wait("sp", B_dbg.w)
            return nc


        S.barrier()
        if debug and stage == 40:
            glu_all()
            return nc
        cur[0] = WO
        qnT = alloc([4, L], BF16); knT = alloc([4, L], BF16); Vv = alloc([16, 512], BF16); qrT = alloc([2, L], BF16)
        wuq_f = alloc([2, 768], F32); wuq = alloc([2, 768], BF16); wqr = alloc([2, 4, 128], BF16); qg = alloc([2], F32)
        wukv_f = alloc([1024], F32); wukv = alloc([1024], BF16); kvg = alloc([1], F32)
        sq = [alloc([3, 512], BF16) for _ in range(2)]
        rstdq = [alloc([512], F32) for _ in range(2)]
        rstdk = [alloc([512], F32) for _ in range(2)]
        rkt = [alloc([4], F32) for _ in range(2)]
        B_wq = Buf("wuq"); B_wkv = Buf("wukv")
        B_sq = [Buf("sq0"), Buf("sq1")]
        B_rq = [Buf("rq0"), Buf("rq1")]
        B_rk = [Buf("rk0"), Buf("rk1")]
        B_rkt = [Buf("rkt0"), Buf("rkt1")]
        B_qn = [[Buf("qn%d_%d" % (h, c)) for c in range(4)] for h in range(4)]
        B_kn = [[Buf("kn%d_%d" % (h, c)) for c in range(4)] for h in range(4)]
        B_qr = [[Buf("qr%d_%d" % (h, c)) for c in range(4)] for h in range(2)]
        B_V = [Buf("V%d" % i) for i in range(16)]

        S.dma("sp", wuq_f, wuq_d.rearrange("(k p) n -> p k n", p=128), w=[B_wq])
        S.dma("sp", qg, qg_d.rearrange("(k p) o -> p (k o)", p=128), w=[B_wq], allow_slow_non_contiguous=True)
        S.dma("sp", wukv_f, wukv_d, w=[B_wkv])
        S.dma("sp", kvg, kvg_d, w=[B_wkv])
        for k in range(2):
            ts("dve", wuq[:, k, :], wuq_f[:, k, :], qg[:, k:k + 1], None, ALU.mult, None, [B_wq], [B_wq])
        for pair in range(2):
            for hh in range(2):
                base = (2 * pair + hh) * 192 + 128
                cp("pool", wqr[:, :, 2 * pair, hh * 64:hh * 64 + 64], wuq[:, :, base:base + 64], [B_wq], [B_wq])
                cp("pool", wqr[:, :, 2 * pair + 1, hh * 64:hh * 64 + 32], wuq[:, :, base + 32:base + 64], [B_wq], [B_wq])
                cp("pool", wqr[:, :, 2 * pair + 1, hh * 64 + 32:hh * 64 + 64], wuq[:, :, base:base + 32], [B_wq], [B_wq])
        ts("dve", wukv, wukv_f, kvg[:, 0:1], None, ALU.mult, None, [B_wkv], [B_wkv])
        wukv_v = wukv.rearrange("p (h x) -> p h x", h=4)[:, :, 128:256]

        def rsqrt_from(out, src_ps, scale_, r, w):
            act(out, src_ps, AF.Ln, r + [B_const], w, scale=scale_, bias=cf[:, 554:555])
            act(out, out, AF.Exp, w, w, scale=-0.5)

        rotc = [0]

        def nb_():
            pb = rotc[0] % 8
            rotc[0] += 1
            return pb

        def prep0(cc):
            cs = slice(cc * 512, (cc + 1) * 512)
            sl = cc % 2
            for j in range(2):
                act(sq[sl][:, j, :], cqT[:, j, cs], AF.Square, [B_cq[cc]], [B_sq[sl]])
            act(sq[sl][:, 2, :], ckvT[:, cs], AF.Square, [B_ckv[cc]], [B_sq[sl]])
            pbq = nb_()
            mm(psum[pbq][:, :], ones_b, sq[sl][:, 0, :], True, False, [B_const, B_sq[sl]], [PB[pbq]])
            mm(psum[pbq][:, :], ones_b, sq[sl][:, 1, :], False, True, [B_const, B_sq[sl]], [PB[pbq]])
            pbk = nb_()
            mm(psum[pbk][:, :], ones_b, sq[sl][:, 2, :], True, True, [B_const, B_sq[sl]], [PB[pbk]])
            pbt = nb_()
            for ii in range(4):
                mm(psum[pbt][:, ii:ii + 1], sq[sl][:, 2, ii * 128:(ii + 1) * 128], ones_b[:, 0:1], True, True,
                   [B_const, B_sq[sl]], [PB[pbt]], inc=(ii == 3))
            act(rstdq[sl], psum[pbq][:, :], AF.Ln, [PB[pbq], B_const], [B_rq[sl]], scale=1.0 / 256.0, bias=cf[:, 554:555])
            act(rstdk[sl], psum[pbk][:, :], AF.Ln, [PB[pbk], B_const], [B_rk[sl]], scale=1.0 / 128.0, bias=cf[:, 554:555])
            act(rkt[sl], psum[pbt][:, 0:4], AF.Ln, [PB[pbt], B_const], [B_rkt[sl]], scale=1.0 / 128.0, bias=cf[:, 554:555])
            act(rstdq[sl], rstdq[sl], AF.Exp, [], [B_rq[sl]], scale=-0.5)
            act(rstdk[sl], rstdk[sl], AF.Exp, [], [B_rk[sl]], scale=-0.5)
            act(rkt[sl], rkt[sl], AF.Exp, [], [B_rkt[sl]], scale=-0.5)

        def prep1(cc):
            cs = slice(cc * 512, (cc + 1) * 512)
            sl = cc % 2
            for h in range(4):
                pb = nb_()
                for k in range(2):
                    mm(psum[pb][:, :], wuq[:, k, h * 192:h * 192 + 128], cqT[:, k, cs], k == 0, k == 1,
                       [B_wq, B_cq[cc]], [PB[pb]])
                tt("dve", qnT[:, h, cs], psum[pb][:, :], rstdq[sl], ALU.mult, [PB[pb], B_rq[sl]], [B_qn[h][cc]])
            for pair in range(2):
                pb1 = nb_()
                pb2 = nb_()
                for k in range(2):
                    mm(psum[pb1][:, :], wqr[:, k, 2 * pair, :], cqT[:, k, cs], k == 0, k == 1, [B_wq, B_cq[cc]], [PB[pb1]])
                for k in range(2):
                    mm(psum[pb2][:, :], wqr[:, k, 2 * pair + 1, :], cqT[:, k, cs], k == 0, k == 1, [B_wq, B_cq[cc]], [PB[pb2]])
                ta = tmp[2 * pair]; tb = tmp[2 * pair + 1]
                tt("dve", ta, psum[pb1][:, :], COSr[:, cs], ALU.mult, [PB[pb1], B_rope], [TB[2 * pair]])
                tt("dve", tb, psum[pb2][:, :], SINr[:, cs], ALU.mult, [PB[pb2], B_rope], [TB[2 * pair + 1]])
                tt("dve", ta, ta, tb, ALU.add, [TB[2 * pair + 1]], [TB[2 * pair]])
                tt("dve", qrT[:, pair, cs], ta, rstdq[sl], ALU.mult, [TB[2 * pair], B_rq[sl]], [B_qr[pair][cc]])
            for h in range(4):
                pb = nb_()
                mm(psum[pb][:, :], wukv[:, h * 256:h * 256 + 128], ckvT[:, cs], True, True, [B_wkv, B_ckv[cc]], [PB[pb]])
                tt("dve", knT[:, h, cs], psum[pb][:, :], rstdk[sl], ALU.mult, [PB[pb], B_rk[sl]], [B_kn[h][cc]])
            for ii in range(4):
                i = cc * 4 + ii
                pb = nb_()
                mm(psum[pb][:, :], ckvT[:, i * 128:(i + 1) * 128], wukv_v, True, True, [B_wkv, B_ckv[cc]], [PB[pb]])
                S.op("act", lambda e: e.activation(out=Vv[:, i, :], in_=psum[pb][:, :], func=AF.Copy,
                                                   scale=rkt[sl][:, ii:ii + 1]),
                     [PB[pb], B_rkt[sl]], [B_V[i]])

        prep0(0); prep0(1); prep1(0); prep0(2); prep1(1); prep0(3); prep1(2)
        glu_pre()
        prep1(3)

        S.barrier()
        if debug and stage == 45:
            return nc
        wout = view(WO + 57344, [8, D], BF16)
        pTall = view(WO + 73728, [2, L], BF16)
        pstage = [small[:, 448:704], small[:, 704:960]]
        B_pst = [Buf("pst0"), Buf("pst1")]; B_pTall = Buf("pTall")
        wpg = view(ZO + 20480, [8, D], BF16)
        g2 = view(ZO + 36864, [D], F32); b2 = view(ZO + 40960, [D], F32)
        g1 = view(ARENA - 8192, [D], F32); b1 = view(ARENA - 4096, [D], F32)
        B_wout = Buf("wout"); B_wpg = Buf("wpg"); B_g2 = Buf("g2b2"); B_g1x = Buf("g1b1x")
        S.dma("pool", wout, wout_d.rearrange("(k p) n -> p k n", p=128), w=[B_wout])
        S.dma("pool", wpg, wpg_d.rearrange("(k p) n -> p k n", p=128), w=[B_wpg])
        S.dma("sp", g2, g2_d.partition_broadcast(128)[:, 0, :], w=[B_g2])
        S.dma("sp", b2, b2_d.partition_broadcast(128)[:, 0, :], w=[B_g2])

        krB = view(WO + 81920, [L], BF16)
        B_krB = Buf("krB")
        cp("pool", krB[64:128, :], krT[64:128, :], B_kr, [B_krB])
        S.op("pool", lambda e: e.memset(krB[0:64, :], 0.0), [], [B_krB])
        S.op("pool", lambda e: e.memset(krT[64:128, :], 0.0), [B_krB], B_kr)
        attops = []
        S.rec = attops
        items = []
        for h in range(4):
            for Q in range(4):
                for j in range(4 * Q + 4):
                    items.append((h, Q, j))

        def emit_S(idx):
            h, Q, j = items[idx]
            a = j - 4 * Q
            c0 = 128 * a if a > 0 else 0
            pb = idx % 3
            ptb = idx % 3
            hb = 64 * (h % 2)
            pair = h // 2
            qs = slice(Q * 512 + c0, (Q + 1) * 512)
            ks = slice(j * 128, (j + 1) * 128)
            mm(psum[pb][:, c0:512], knT[:, h, ks], qnT[:, h, qs], True, False, [B_kn[h][j // 4], B_qn[h][Q]], [PB[pb]], inc=False)
            mm(psum[pb][:, c0:512], (krT if h % 2 == 0 else krB)[:, ks], qrT[:, pair, qs], False, a < 0,
               [B_kr[j // 4], B_krB, B_qr[pair][Q]], [PB[pb]], inc=(a < 0))
            if a >= 0:
                mm(psum[pb][:, c0:c0 + 128], ident_b, trimask, False, True, [B_const], [PB[pb]])
            act(PT[ptb][:, c0:512], psum[pb][:, c0:512], AF.Exp, [PB[pb]], [PTB[ptb]], scale=SCALE)

        def emit_PV(idx):
            h, Q, j = items[idx]
            a = j - 4 * Q
            c0 = 128 * a if a > 0 else 0
            ptb = idx % 3
            hq = h * 4 + Q
            po = 3 + hq % 2
            pl = 5 + hq % 2
            last = (j == 4 * Q + 3)
            mm(psum[po][:, c0:512], Vv[:, j, h * 128:(h + 1) * 128], PT[ptb][:, c0:512], j == 0, last,
               [B_V[j], PTB[ptb]], [PB[po]], inc=last)
            mm(psum[pl][:, c0:512], ones_b, PT[ptb][:, c0:512], j == 0, last, [B_const, PTB[ptb]], [PB[pl]], inc=True)
            if last:
                Qs = slice(Q * 512, (Q + 1) * 512)
                S.op("dve", lambda e: e.reciprocal(out=tmp[2], in_=psum[pl][:, :]), [PB[pl]], [TB[2]])
                tt("dve", tmp[3], psum[po][:, :], tmp[2], ALU.mult, [PB[po], TB[2]], [TB[3]])
                tt("dve", mixT[:, 4 + h, Qs], tmp[3], mixT[:, 4 + h, Qs], ALU.mult, [TB[3]], [B_mix[4 + h][Q]])

        for idx in range(len(items)):
            emit_S(idx)
            if idx > 1:
                emit_PV(idx - 2)
        emit_PV(len(items) - 2)
        emit_PV(len(items) - 1)
        gluops = []
        S.rec = gluops
        for i in range(16):
            s_ = i % 2
            S.dma("sp", pstage[s_], p_d[i * 128:(i + 1) * 128, :], w=[B_pst[s_]])
            for k in range(2):
                tr(psum[7][:, k * 128:(k + 1) * 128], pstage[s_][:, k * 128:(k + 1) * 128], [B_pst[s_], B_const], [PB[7]], inc=(k == 1))
            cp("act", pTall[:, :, i * 128:(i + 1) * 128], psum[7][:, 0:256].rearrange("p (a b) -> p a b", a=2),
               [PB[7]], [B_pTall])
        glu_units()
        S.rec = None
        na_, ng_ = len(attops), len(gluops)
        ia = ig = 0
        while ia < na_ or ig < ng_:
            if ig >= ng_ or (ia < na_ and ia * ng_ <= ig * na_):
                attops[ia](); ia += 1
            else:
                gluops[ig](); ig += 1
        S.dma("sp", g1, g1_d.partition_broadcast(128)[:, 0, :], w=[B_g1x, B_wglu, B_bglu])
        S.dma("sp", b1, b1_d.partition_broadcast(128)[:, 0, :], w=[B_g1x, B_wglu, B_bglu])

        if debug and stage == 5:
            S.barrier()
            dv = dbg_d.rearrange("p (a b) -> p a b", a=8)
            B_dbg = Buf("dbg")
            srcs = [mixT[:, 4, :], mixT[:, 5, :], mixT[:, 6, :], mixT[:, 7, :], qnT[:, 0, :], knT[:, 0, :], qrT[:, 0, :], Vv.rearrange("p a b -> p (a b)")[:, 0:2048]]
            for n_, sv in enumerate(srcs):
                for cc in range(4):
                    cs = slice(cc * 512, (cc + 1) * 512)
                    cp("dve", tmp[cc % 4], sv[:, cs], [], [TB[cc % 4]])
                    S.dma("sp", dv[:, n_, cs], tmp[cc % 4], r=[TB[cc % 4]], w=[B_dbg])
            S._wait("sp", B_dbg.w)
            return nc

        S.barrier()
        xt = [view(ZO + 4096 * i, [D], F32) for i in range(4)]
        xn = [view(WO + 4096 * i, [D], F32) for i in range(5)]
        uu = [view(WO + 20480 + 4096 * i, [D], F32) for i in range(6)]
        uTb = [view(WO + 45056 + 2048 * i, [8, 128], BF16) for i in range(2)]
        gate = [view(WO + 81920 + 4096 * i, [D], F32) for i in range(5)] + [view(WO + 49152 + 4096 * i, [D], F32) for i in range(2)]
        assert WO + 81920 + 5 * 4096 <= ARENA - 8192
        wpp = view(36096, [2, D], BF16)
        B_xt = [Buf("fxt%d" % i) for i in range(4)]; B_xn = [Buf("fxn%d" % i) for i in range(5)]
        B_u = [Buf("u%d" % i) for i in range(6)]; B_gate = [Buf("gate%d" % i) for i in range(7)]
        B_uT = [Buf("uT0"), Buf("uT1")]; B_wpp = Buf("wpp")
        B_out = [Buf("out%d" % i) for i in range(7)]
        st1_ = [small[:, 20 * i:20 * i + 20] for i in range(4)]
        st2_ = [small[:, 80 + 20 * i:100 + 20 * i] for i in range(4)]
        B_s1 = [Buf("fs1_%d" % i) for i in range(4)]; B_s2 = [Buf("fs2_%d" % i) for i in range(4)]
        S.dma("pool", wpp, wpp_d.rearrange("(k p) n -> p k n", p=128), w=[B_wpp])

        def tsl(i):
            return slice(i * 128, (i + 1) * 128)

        def bn(xin, stat, r, rb):
            st = stat[:, 0:12].rearrange("p (a b) -> p a b", a=2)
            S.op("dve", lambda e: e.bn_stats(out=st[:, 0, :], in_=xin[:, 0:512]), r, [rb])
            S.op("dve", lambda e: e.bn_stats(out=st[:, 1, :], in_=xin[:, 512:1024]), r, [rb])
            S.op("dve", lambda e: e.bn_aggr(out=stat[:, 12:14], in_=stat[:, 0:12]), [rb], [rb])

        def lnexp(stat, rb):
            act(stat[:, 14:15], stat[:, 13:14], AF.Ln, [rb, B_const], [rb], bias=cf[:, 553:554])
            act(stat[:, 15:16], stat[:, 14:15], AF.Exp, [rb], [rb], scale=-0.5)

        def nmr(stat, rb):
            ts("dve", stat[:, 16:17], stat[:, 12:13], stat[:, 15:16], -1.0, ALU.mult, ALU.mult, [rb], [rb])

        def m0(i):
            S.dma("sp", xt[i % 4], x_d[tsl(i), :], w=[B_xt[i % 4]])
            bn(xt[i % 4], st1_[i % 4], [B_xt[i % 4]], B_s1[i % 4])

        def m1(i):
            lnexp(st1_[i % 4], B_s1[i % 4])

        def m2(i):
            nmr(st1_[i % 4], B_s1[i % 4])

        def m3(i):
            st = st1_[i % 4]
            act(xn[i % 5], xt[i % 4], AF.Identity, [B_xt[i % 4], B_s1[i % 4]], [B_xn[i % 5]], scale=st[:, 15:16], bias=st[:, 16:17])

        def m4(i):
            tt("dve", xn[i % 5], xn[i % 5], g1, ALU.mult, [B_g1x], [B_xn[i % 5]])

        def m5(i):
            tt("pool", xn[i % 5], xn[i % 5], b1, ALU.add, [B_g1x], [B_xn[i % 5]])

        def m6(i):
            for hf in range(2):
                hs = slice(hf * 512, (hf + 1) * 512)
                for k in range(8):
                    mm(psum[hf][:, :], mixT[:, k, tsl(i)], wout[:, k, hs], k == 0, k == 7,
                       [B_mix[k][i // 4], B_wout], [PB[hf]])

        def m7(i):
            for hf in range(2):
                hs = slice(hf * 512, (hf + 1) * 512)
                S.op("dve", lambda e: e.scalar_tensor_tensor(out=uu[i % 6][:, hs], in0=xn[i % 5][:, hs], scalar=ALPHA,
                                                              in1=psum[hf][:, :], op0=ALU.mult, op1=ALU.add),
                     [B_xn[i % 5], PB[hf]], [B_u[i % 6]])

        def m8(i):
            for hb in range(2):
                pb = 2 + hb
                for j in range(4):
                    k = hb * 4 + j
                    tr(psum[pb][:, j * 128:(j + 1) * 128], uu[i % 6][:, k * 128:(k + 1) * 128], [B_u[i % 6], B_const], [PB[pb]], inc=(j == 3))

        def m9(i):
            for hb in range(2):
                pb = 2 + hb
                cp("act", uTb[i % 2][:, hb * 4:hb * 4 + 4, :], psum[pb].rearrange("p (a b) -> p a b", a=4),
                   [PB[pb]], [B_uT[i % 2]])

        def m10(i):
            for hf in range(2):
                hs = slice(hf * 512, (hf + 1) * 512)
                pb = 4 + hf
                for k in range(8):
                    mm(psum[pb][:, :], uTb[i % 2][:, k, :], wpg[:, k, hs], k == 0, k == 7, [B_uT[i % 2], B_wpg], [PB[pb]])

        def m11(i):
            for hf in range(2):
                hs = slice(hf * 512, (hf + 1) * 512)
                act(gate[i % 7][:, hs], psum[4 + hf][:, :], AF.Sigmoid, [PB[4 + hf]], [B_gate[i % 7]])
            for hf in range(2):
                hs = slice(hf * 512, (hf + 1) * 512)
                pb = 6 + hf
                for k in range(2):
                    mm(psum[pb][:, :], pTall[:, k, tsl(i)], wpp[:, k, hs], k == 0, k == 1, [B_pTall, B_wpp], [PB[pb]])

        def m12(i):
            gt_ = gate[i % 7]
            for hf in range(2):
                hs = slice(hf * 512, (hf + 1) * 512)
                tt("dve", gt_[:, hs], gt_[:, hs], psum[6 + hf][:, :], ALU.mult, [PB[6 + hf]], [B_gate[i % 7]])
            tt("dve", gt_, gt_, uu[i % 6], ALU.add, [B_u[i % 6]], [B_gate[i % 7]])
            bn(gt_, st2_[i % 4], [B_gate[i % 7]], B_s2[i % 4])

        def m13(i):
            lnexp(st2_[i % 4], B_s2[i % 4])

        def m14(i):
            nmr(st2_[i % 4], B_s2[i % 4])

        def m15(i):
            st = st2_[i % 4]
            act(gate[i % 7], gate[i % 7], AF.Identity, [B_s2[i % 4]], [B_gate[i % 7]], scale=st[:, 15:16], bias=st[:, 16:17])

        def m16(i):
            tt("dve", gate[i % 7], gate[i % 7], g2, ALU.mult, [B_g2], [B_gate[i % 7]])

        def m17(i):
            tt("pool", gate[i % 7], gate[i % 7], b2, ALU.add, [B_g2], [B_gate[i % 7]])
            S.dma("sp", out_d[tsl(i), :], gate[i % 7], r=[B_gate[i % 7]], w=[B_out[i % 7]])

        stages = [m0, m1, m2, m3, m4, m5, m6, m7, m8, m9, m10, m11, m12, m13, m14, m15, m16, m17]
        for step in range(16 + len(stages) - 1):
            for si in range(len(stages) - 1, -1, -1):
                i = step - si
                if 0 <= i < 16:
                    stages[si](i)
        for bo in B_out:
            S._wait("sp", bo.w)
    return nc


def make_in_maps(inputs):
    cf, cb = host_consts()
    maps = []
    f = lambda a: np.ascontiguousarray(np.asarray(a, dtype=np.float32))
    for b in range(NCORES):
        m = {
            "x": f(inputs["x"][b]), "p": f(inputs["p"][0, b]),
            "pos": np.ascontiguousarray(np.asarray(inputs["positions"][b], dtype=np.int32).reshape(1, L)),
            "cf": cf, "cb": cb,
            "ln_emb_g": f(inputs["ln_emb_g"]).reshape(1, D), "ln_emb_b": f(inputs["ln_emb_b"]).reshape(1, D),
            "ln_g": f(inputs["ln_g"][0]).reshape(1, D), "ln_b": f(inputs["ln_b"][0]).reshape(1, D),
            "w_in": f(inputs["w_in"][0]),
            "a_re": f(inputs["a_re"][0]), "a_im": f(inputs["a_im"][0]), "log_dt": f(inputs["log_dt"][0]).reshape(1, 32),
            "b_re": f(inputs["b_re"][0]), "b_im": f(inputs["b_im"][0]),
            "c_re": f(inputs["c_re"][0]).reshape(512, 64), "c_im": f(inputs["c_im"][0]).reshape(512, 64),
            "d_skip": f(inputs["d_skip"][0]).reshape(32, 16),
            "w_glu": f(inputs["w_glu"][0]), "b_glu": f(inputs["b_glu"][0]).reshape(512, 1),
            "q_norm_g": f(inputs["q_norm_g"][0]).reshape(256, 1), "w_uq": f(inputs["w_uq"][0]),
            "kv_norm_g": f(inputs["kv_norm_g"][0]).reshape(128, 1), "w_ukv": f(inputs["w_ukv"][0]),
            "w_out": f(inputs["w_out"][0]), "w_pg": f(inputs["w_pg"][0]), "w_pp": f(inputs["w_pp"][0]),
        }
        maps.append(m)
    return maps


def kernel(**inputs):
    nc = build()
    in_maps = make_in_maps(inputs)
    res = run_bass_kernel_spmd(nc, in_maps, core_ids=list(range(NCORES)))
    out = np.stack([np.asarray(r["out"], dtype=np.float32) for r in res.results], axis=0)
    return out
```

```python
import math
import contextlib
import numpy as np
import ml_dtypes
import concourse.bass as bass
import concourse.mybir as mybir
from concourse.bass_utils import run_bass_kernel_spmd

F32 = mybir.dt.float32
I32 = mybir.dt.int32
BF16 = mybir.dt.bfloat16
ALU = mybir.AluOpType
AF = mybir.ActivationFunctionType

L = 2048
D = 1024
NCORES = 8
TWO_PI = 2.0 * math.pi
C2PI = TWO_PI * (1.0 - 1e-6)
LN_EPS = 1e-5
RMS_EPS = 1e-6
ALPHA = 2.0 ** 0.25
SCALE = 192.0 ** -0.5
KV = [0.0] + [-float(s) for s in range(1, 8)] + [float(t) for t in range(0, 8)] + [8.0]
NCF = 576


class Buf:
    __slots__ = ("name", "w", "r", "dsem", "dcnt")

    def __init__(self, name):
        self.name = name
        self.w = None
        self.r = []
        self.dsem = None
        self.dcnt = 0


class Sched:
    def __init__(self, nc, es):
        self.nc = nc
        self.es = es
        self.E = {"pe": nc.tensor, "act": nc.scalar, "dve": nc.vector, "pool": nc.gpsimd, "sp": nc.sync}
        self.sem = {k: es.enter_context(nc.semaphore("prog_" + k)) for k in self.E}
        self.cnt = {k: 0 for k in self.E}
        self.seen = {k: {} for k in self.E}
        self.pe_pending = []
        self.nsem = 0
        self.rec = None
        self.snap = {}
        self.age = {}
        self.clock = 0

    def _wait(self, eng, tok):
        if tok is None:
            return
        sem, val = tok
        if eng == "pe" and sem is self.sem["pe"]:
            return
        key = sem.name
        if self.seen[eng].get(key, 0) >= val:
            return
        self.seen[eng][key] = val
        self.E[eng].wait_ge(sem, val)

    def _deps(self, eng, r, w):
        for b in r:
            self._wait(eng, b.w)
        for b in w:
            self._wait(eng, b.w)
            for t in b.r:
                self._wait(eng, t)

    def op(self, eng, fn, r=(), w=(), inc=True):
        if self.rec is not None:
            r = list(r); w = list(w)
            self.rec.append(lambda: self._op(eng, fn, r, w, inc))
            return None
        return self._op(eng, fn, r, w, inc)

    def _collect(self, eng, r, w):
        need = {}

        def add(tok):
            if tok is None:
                return
            sem, val = tok
            if eng == "pe" and sem is self.sem["pe"]:
                return
            if self.seen[eng].get(sem.name, 0) >= val:
                return
            if sem.name not in need or need[sem.name][1] < val:
                need[sem.name] = (sem, val)

        cand = [b.w for b in r]
        for b in w:
            cand.append(b.w)
            cand.extend(b.r)
        cand = [t for t in cand if t is not None]
        cand.sort(key=lambda t: -t[1])
        for tok in cand:
            before = len(need)
            had = need.get(tok[0].name)
            add(tok)
            if need.get(tok[0].name) is not had or len(need) != before:
                self.seen[eng][tok[0].name] = max(self.seen[eng].get(tok[0].name, 0), tok[1])
                snap = self.snap.get((tok[0].name, tok[1]))
                if snap:
                    se = self.seen[eng]
                    for k_, v_ in snap.items():
                        if se.get(k_, 0) < v_:
                            se[k_] = v_
        for name, (sem, val) in need.items():
            self.seen[eng][name] = max(self.seen[eng].get(name, 0), val)
        return sorted(need.values(), key=lambda t: self.age.get((t[0].name, t[1]), 0))

    def _op(self, eng, fn, r=(), w=(), inc=True):
        toks = self._collect(eng, r, w)
        for (sem, val) in toks[:-1]:
            self.E[eng].wait_ge(sem, val)
        ins = fn(self.E[eng])
        if toks:
            ins._wait_ge(toks[-1][0], toks[-1][1])
        if inc:
            self.cnt[eng] += 1
            ins.then_inc(self.sem[eng], 1)
            tok = (self.sem[eng], self.cnt[eng])
            self.snap[(tok[0].name, tok[1])] = dict(self.seen[eng])
            self.clock += 1
            self.age[(tok[0].name, tok[1])] = self.clock
            for b in r:
                b.r.append(tok)
            for b in w:
                b.w = tok
                b.r = []
            if eng == "pe":
                for b in self.pe_pending:
                    b.r.append(tok)
                self.pe_pending = []
        else:
            assert eng == "pe"
            self.pe_pending.extend(r)
            for b in w:
                b.r = []
        return ins

    def dma(self, q, out, in_, r=(), w=(), **kw):
        if self.rec is not None:
            r = list(r); w = list(w)
            self.rec.append(lambda: self._dma(q, out, in_, r, w, **kw))
            return None
        return self._dma(q, out, in_, r, w, **kw)

    def _dma(self, q, out, in_, r=(), w=(), **kw):
        dst = w[0]
        for b in r:
            self._wait(q, b.w)
        for b in w:
            if not (b.w is not None and b.dsem is not None and b.w[0] is b.dsem):
                self._wait(q, b.w)
            for t in b.r:
                self._wait(q, t)
        if dst.dsem is None:
            dst.dsem = self.es.enter_context(self.nc.semaphore("d%d_%s" % (self.nsem, dst.name)))
            dst.dcnt = [0]
            self.nsem += 1
        for b in w:
            if b.dsem is None:
                b.dsem = dst.dsem
                b.dcnt = dst.dcnt
        ins = self.E[q].dma_start(out=out, in_=in_, **kw)
        ins.then_inc(dst.dsem, 16)
        dst.dcnt[0] += 16
        tok = (dst.dsem, dst.dcnt[0])
        self.snap[(tok[0].name, tok[1])] = dict(self.seen[q])
        self.clock += 1
        self.age[(tok[0].name, tok[1])] = self.clock
        for b in r:
            b.r.append(tok)
        for b in w:
            b.w = tok
            b.r = []
        return tok

    def barrier(self):
        toks = [(self.sem[k], self.cnt[k]) for k in self.E if self.cnt[k] > 0]
        for e in self.E:
            for t in toks:
                if t[0] is not self.sem[e]:
                    self._wait(e, t)


def host_consts():
    cf = np.zeros((128, NCF), np.float32)
    cf[:, 0:128] = np.eye(128, dtype=np.float32)
    r = np.arange(128)
    cf[:, 128:256] = (r[None, :] // 16 >= r[:, None] // 16).astype(np.float32)
    cf[:, 256:512] = np.arange(256, dtype=np.float32)[None, :]
    cf[:, 512:529] = np.array(KV, np.float32)[None, :]
    cf[:, 529:546] = (np.array(KV, np.float64) / TWO_PI).astype(np.float32)[None, :]
    inv_freq = 1.0 / (10000.0 ** (np.arange(0, 64, 2, dtype=np.float64) / 64.0))
    cf[:, 546] = (inv_freq[r % 32] / TWO_PI).astype(np.float32)
    cf[:, 547] = np.where(r < 64, -1.0, 1.0)
    cf[:, 548] = math.pi / 2
    cf[:, 549] = np.where(r < 64, 1.0, -1.0)
    cf[:, 550] = C2PI
    cf[:, 551] = np.where(r < 64, 1.0, -1.0) * C2PI
    cf[:, 552] = np.where((r % 64) < 32, -1.0, 1.0) * C2PI
    cf[:, 553] = LN_EPS
    cf[:, 554] = RMS_EPS
    cf[:, 555] = 0.0
    cf[:, 556] = 1.0
    cb = np.zeros((128, 512), np.float32)
    cb[:, 0:128] = np.eye(128)
    cb[:, 128:256] = (r[:, None] == (r[None, :] + 64) % 128)
    cb[:, 256:384] = np.where(r[None, :] >= r[:, None], 0.0, -10000.0)
    cb[:, 384:512] = 1.0
    return cf, cb.astype(ml_dtypes.bfloat16)


def build(stage=99, debug=False):
    nc = bass.Bass("TRN2", target_bir_lowering=False)

    def din(name, shape, dt=F32):
        return nc.dram_tensor(name, list(shape), dt, kind="ExternalInput").ap()

    x_d = din("x", [L, D])
    p_d = din("p", [L, 256])
    pos_d = din("pos", [1, L], I32)
    cf_d = din("cf", [128, NCF])
    cb_d = din("cb", [128, 512], BF16)
    g1_d = din("ln_emb_g", [1, D]); b1_d = din("ln_emb_b", [1, D])
    g2_d = din("ln_g", [1, D]); b2_d = din("ln_b", [1, D])
    win_d = din("w_in", [D, 1984])
    are_d = din("a_re", [32, 64]); aim_d = din("a_im", [32, 64]); ldt_d = din("log_dt", [1, 32])
    bre_d = din("b_re", [32, 64, 16]); bim_d = din("b_im", [32, 64, 16])
    cre_d = din("c_re", [512, 64]); cim_d = din("c_im", [512, 64])
    dsk_d = din("d_skip", [32, 16])
    wglu_d = din("w_glu", [512, 512]); bglu_d = din("b_glu", [512, 1])
    qg_d = din("q_norm_g", [256, 1]); wuq_d = din("w_uq", [256, 768])
    kvg_d = din("kv_norm_g", [128, 1]); wukv_d = din("w_ukv", [128, 1024])
    wout_d = din("w_out", [D, D]); wpg_d = din("w_pg", [D, D]); wpp_d = din("w_pp", [256, D])
    out_d = nc.dram_tensor("out", [L, D], F32, kind="ExternalOutput").ap()
    dbg_d = None
    if debug:
        dbg_d = nc.dram_tensor("dbg", [128, 8 * 2048], F32, kind="ExternalOutput").ap()

    es = contextlib.ExitStack()
    with es:
        S = Sched(nc, es)
        ARENA = 211968
        arena = es.enter_context(nc.sbuf_tensor("arena", [128, ARENA // 2], BF16))
        psum = [es.enter_context(nc.psum_tensor("ps%d" % i, [128, 512], F32)) for i in range(8)]
        PB = [Buf("psb%d" % i) for i in range(8)]

        def view(off, shape, dt):
            n = 1
            for s in shape:
                n *= s
            esz = 2 if dt == BF16 else 4
            assert off % 4 == 0 and off + n * esz <= ARENA, (off, shape)
            a = arena[:, off // 2: off // 2 + n * esz // 2]
            if dt != BF16:
                a = a.bitcast(dt)
            if len(shape) == 2:
                a = a.rearrange("p (a b) -> p a b", a=shape[0])
            elif len(shape) == 3:
                a = a.rearrange("p (a b c) -> p a b c", a=shape[0], b=shape[1])
            return a

        cf = view(0, [NCF], F32)
        cb = view(2304, [512], BF16)
        mixT = view(3328, [8, L], BF16)
        tmp = [view(36096 + 2048 * i, [512], F32) for i in range(4)]
        TB = [Buf("tmp%d" % i) for i in range(4)]
        PT = [view(44288 + 1024 * i, [512], BF16) for i in range(3)]
        PTB = [Buf("pt%d" % i) for i in range(3)]
        small = view(47360, [960], F32)
        ident_f = cf[:, 0:128]
        ident_b = cb[:, 0:128]
        pswap_b = cb[:, 128:256]
        trimask = cb[:, 256:384]
        ones_b = cb[:, 384:512]
        ZO = 51200
        xsT = view(ZO, [4, L], BF16)
        krT = view(ZO + 16384, [L], BF16)
        cqT = view(ZO + 20480, [2, L], BF16)
        ckvT = view(ZO + 28672, [L], BF16)
        COSr = view(ZO + 32768, [L], F32)
        SINr = view(ZO + 40960, [L], F32)
        WO = 100352

        B_const = Buf("const")
        B_mix = [[Buf("mix%d_%d" % (k, c)) for c in range(4)] for k in range(8)]
        B_xs = [Buf("xs%d" % q) for q in range(4)]
        B_cq = [Buf("cq%d" % c) for c in range(4)]
        B_ckv = [Buf("ckv%d" % c) for c in range(4)]
        B_kr = [Buf("kr%d" % c) for c in range(4)]
        B_rope = Buf("rope")
        B_small = Buf("small")

        S.dma("sp", cf, cf_d, w=[B_const])
        S.dma("sp", cb, cb_d, w=[B_const])

        def act(out, in_, func, r, w, scale=1.0, bias=None, eng="act"):
            if bias is None:
                return S.op("act", lambda e: e.activation(out=out, in_=in_, func=func, scale=scale), r, w)
            return S.op("act", lambda e: e.activation(out=out, in_=in_, func=func, scale=scale, bias=bias), r, w)

        def tt(eng, out, in0, in1, op, r, w):
            return S.op(eng, lambda e: e.tensor_tensor(out=out, in0=in0, in1=in1, op=op), r, w)

        def ts(eng, out, in0, s1, s2, op0, op1, r, w):
            if s2 is None:
                return S.op(eng, lambda e: e.tensor_scalar(out=out, in0=in0, scalar1=s1, scalar2=None, op0=op0), r, w)
            return S.op(eng, lambda e: e.tensor_scalar(out=out, in0=in0, scalar1=s1, scalar2=s2, op0=op0, op1=op1), r, w)

        def cp(eng, out, in_, r, w):
            if eng == "act":
                return S.op("act", lambda e: e.copy(out=out, in_=in_), r, w)
            return S.op(eng, lambda e: e.tensor_copy(out=out, in_=in_), r, w)

        def mm(out, lhsT, rhs, start, stop, r, w, inc=None):
            if inc is None:
                inc = stop
            return S.op("pe", lambda e: e.matmul(out, lhsT=lhsT, rhs=rhs, start=start, stop=stop), r, w, inc=inc)

        def tr(out, in_, r, w, inc=True):
            return S.op("pe", lambda e: e.transpose(out=out, in_=in_, identity=ident_f), r, w, inc=inc)

        def sincos(eng, y, k_i32, fr, afr, sin_out, cos_out, sin_scale, r, w_tmp, w_out):
            cp(eng, k_i32, y, r + w_tmp, w_tmp)
            tt(eng, fr, y, k_i32, ALU.subtract, w_tmp, w_tmp)
            act(afr, fr, AF.Abs, w_tmp, w_tmp)
            act(sin_out, fr, AF.Sin, w_tmp, w_out, scale=sin_scale)
            act(cos_out, afr, AF.Sin, w_tmp, w_out, scale=-C2PI, bias=cf[:, 548:549])

        def ln_stats(xin, stat, r, rb, nmr=False):
            st = stat[:, 0:12].rearrange("p (a b) -> p a b", a=2)
            S.op("dve", lambda e: e.bn_stats(out=st[:, 0, :], in_=xin[:, 0:512]), r, [rb])
            S.op("dve", lambda e: e.bn_stats(out=st[:, 1, :], in_=xin[:, 512:1024]), r, [rb])
            S.op("dve", lambda e: e.bn_aggr(out=stat[:, 12:14], in_=stat[:, 0:12]), [rb], [rb])
            act(stat[:, 14:15], stat[:, 13:14], AF.Ln, [rb, B_const], [rb], bias=cf[:, 553:554])
            act(stat[:, 15:16], stat[:, 14:15], AF.Exp, [rb], [rb], scale=-0.5)
            if nmr:
                ts("dve", stat[:, 16:17], stat[:, 12:13], stat[:, 15:16], -1.0, ALU.mult, ALU.mult, [rb], [rb])

        def ln_apply(xin, xout, gt, bt, stat, r, w, rb, beng="pool", norm="dve"):
            if norm == "act":
                act(xout, xin, AF.Identity, r + [rb], w, scale=stat[:, 15:16], bias=stat[:, 16:17])
            else:
                ts("dve", xout, xin, stat[:, 12:13], stat[:, 15:16], ALU.subtract, ALU.mult, r + [rb], w)
            tt("dve", xout, xout, gt, ALU.mult, w, w)
            tt(beng, xout, xout, bt, ALU.add, w, w)

        xt = [view(WO + 4096 * i, [D], F32) for i in range(2)]
        xn = [view(WO + 8192 + 4096 * i, [D], F32) for i in range(2)]
        g1 = view(WO + 16384, [D], F32)
        b1 = view(WO + 20480, [D], F32)
        xnT = view(WO + 24576, [8, L], BF16)
        wall = [view(WO + 57344 + 2048 * i, [8, 128], BF16) for i in range(16)]
        wkr = view(WO + 57344 + 2048 * 16, [8, 256], BF16)
        rt0 = view(ZO, [L], F32)
        rt1 = view(ZO + 8192, [L], F32)
        rt2 = view(ZO + 20480, [L], F32)
        B_xt = [Buf("xt0"), Buf("xt1")]
        B_xn = [Buf("xn0"), Buf("xn1")]
        B_g1 = Buf("g1b1")
        B_xnT = [Buf("xnT%d" % c) for c in range(4)]
        B_wall = [Buf("wall%d" % i) for i in range(16)]
        B_wkr = Buf("wkr")
        B_rt = Buf("rt")
        stat = [small[:, 0:20], small[:, 20:40]]
        B_stat = [Buf("stat0"), Buf("stat1")]

        S.dma("sp", g1, g1_d.partition_broadcast(128)[:, 0, :], w=[B_g1])
        S.dma("sp", b1, b1_d.partition_broadcast(128)[:, 0, :], w=[B_g1])

        mtiles = [("xs", q, 128 * q) for q in range(4)] + [("gs", q, 512 + 128 * q) for q in range(4)] + \
                 [("cq", j, 1024 + 128 * j) for j in range(2)] + [("ckv", 0, 1280)] + \
                 [("gm", h, 1472 + 128 * h) for h in range(4)] + [("kr", 0, 1408)]
        win_v = win_d.rearrange("(k p) n -> p k n", p=128)
        def wload(mi):
            kind, idx, col = mtiles[mi]
            if kind != "kr":
                S.dma("pool", wall[mi], win_v[:, :, col:col + 128], w=[B_wall[mi]])
            else:
                S.dma("pool", wall[mi][:, :, 0:64], win_v[:, :, col:col + 64], w=[B_wall[mi]])

        def wkr_build():
            krsl = wall[15][:, :, 0:64]
            for (o, s0, wd) in [(0, 0, 64), (64, 0, 64), (128, 32, 32), (160, 0, 32), (192, 32, 32), (224, 0, 32)]:
                cp("pool", wkr[:, :, o:o + wd], krsl[:, :, s0:s0 + wd], [B_wall[15]], [B_wkr])

        for mi in (15, 0, 1, 2, 3):
            wload(mi)

        def p1_a1(i):
            s_ = i % 2
            S.dma("sp", xt[s_], x_d[i * 128:(i + 1) * 128, :], w=[B_xt[s_]])
            ln_stats(xt[s_], stat[s_], [B_xt[s_]], B_stat[s_])

        def p1_a2(i):
            s_ = i % 2
            ln_apply(xt[s_], xn[s_], g1, b1, stat[s_], [B_xt[s_], B_g1], [B_xn[s_]], B_stat[s_], beng="dve")

        def p1_b(i):
            s_ = i % 2
            for hb in range(2):
                pb = hb
                for j in range(4):
                    k = hb * 4 + j
                    tr(psum[pb][:, j * 128:(j + 1) * 128], xn[s_][:, k * 128:(k + 1) * 128],
                       [B_xn[s_], B_const], [PB[pb]], inc=(j == 3))
                dst = xnT[:, hb * 4:hb * 4 + 4, i * 128:(i + 1) * 128]
                src = psum[pb].rearrange("p (a b) -> p a b", a=4)
                cp("act", dst, src, [PB[pb]], [B_xnT[i // 4]])

        xs_v = xsT.rearrange("p q (t c) -> p q t c", t=8)
        rot2 = [0]

        def p2_unit(mi, cc):
            kind, idx, col = mtiles[mi]
            cs = slice(cc * 512, (cc + 1) * 512)
            if kind != "kr":
                lhs_list = [(wall[mi], B_wall[mi], None)]
            else:
                lhs_list = [(wkr[:, :, 0:128], B_wkr, 6), (wkr[:, :, 128:256], B_wkr, 7)]
            pbs = []
            for (lw, lb, fixed) in lhs_list:
                if fixed is None:
                    pb = 2 + rot2[0] % 4
                    rot2[0] += 1
                else:
                    pb = fixed
                pbs.append(pb)
                for k in range(8):
                    mm(psum[pb][:, :], lw[:, k, :], xnT[:, k, cs], k == 0, k == 7, [lb, B_xnT[cc]], [PB[pb]])
            pb = pbs[0]
            if kind == "xs":
                src = psum[pb].rearrange("p (c t) -> p t c", t=8)
                cp("dve", xs_v[:, idx, :, cc * 64:(cc + 1) * 64], src, [PB[pb]], [B_xs[idx], B_rt])
            elif kind == "gs":
                act(mixT[:, idx, cs], psum[pb][:, :], AF.Silu, [PB[pb]], [B_mix[idx][cc]])
            elif kind == "gm":
                act(mixT[:, 4 + idx, cs], psum[pb][:, :], AF.Silu, [PB[pb]], [B_mix[4 + idx][cc]])
            elif kind == "cq":
                cp("dve", cqT[:, idx, cs], psum[pb][:, :], [PB[pb]], [B_cq[cc], B_rt])
            elif kind == "ckv":
                cp("dve", ckvT[:, cs], psum[pb][:, :], [PB[pb]], [B_ckv[cc]])
            elif kind == "kr":
                tt("dve", tmp[0], psum[pbs[0]][:, :], COSr[:, cs], ALU.mult, [PB[pbs[0]], B_rope], [TB[0]])
                tt("dve", tmp[1], psum[pbs[1]][:, :], SINr[:, cs], ALU.mult, [PB[pbs[1]], B_rope], [TB[1]])
                tt("dve", krT[:, cs], tmp[0], tmp[1], ALU.add, [TB[0], TB[1]], [B_kr[cc]])

        def rope_gen():
            S.dma("sp", rt0.bitcast(I32), pos_d.partition_broadcast(128)[:, 0, :], w=[B_rt])
            cp("dve", rt1, rt0.bitcast(I32), [B_rt], [B_rt])
            ts("dve", rt1, rt1, cf[:, 546:547], None, ALU.mult, None, [B_rt, B_const], [B_rt])
            sincos("dve", rt1, rt0.bitcast(I32), rt2, rt0, SINr, COSr, cf[:, 552:553], [B_const], [B_rt], [B_rope])

        units = {cc: [(mi, cc) for mi in range(16)] for cc in range(4)}
        split = [4, 4, 4, 4]
        rope_gen()
        for step in range(16 + 2):
            if 0 <= step - 2 < 16:
                p1_b(step - 2)
            if 0 <= step - 1 < 16:
                p1_a2(step - 1)
            if step < 16:
                p1_a1(step)
            if step == 1:
                for mi in (4, 5, 6, 7):
                    wload(mi)
            if step == 2:
                for mi in (8, 9, 10, 11):
                    wload(mi)
            if step == 3:
                for mi in (12, 13, 14):
                    wload(mi)
                wkr_build()
            t_done = step - 2
            cprev = ((t_done + 1) // 4) - 1 if t_done >= 0 else -1
            if t_done >= 0 and 0 <= cprev < 3:
                part = (t_done + 1) % 4
                a0 = sum(split[:part])
                for (mi, cc) in units[cprev][a0:a0 + split[part]]:
                    p2_unit(mi, cc)
        xs_scr = nc.dram_tensor("xs_scr", [512, L], BF16, kind="ExternalOutput").ap()
        y_scr = nc.dram_tensor("y_scr", [512, L], BF16, kind="ExternalOutput").ap()
        B_xscr = [Buf("xscr%d" % q) for q in range(4)]
        B_yscr = [Buf("yscr%d" % q) for q in range(4)]
        xs_scr_v = xs_scr.rearrange("(g c) (t n) -> t c g n", c=16, t=8)
        y_scr_v = y_scr.rearrange("(g c) (t n) -> t c g n", c=16, t=8)
        U3 = []
        S.rec = U3
        for (mi, cc) in units[3]:
            p2_unit(mi, cc)
            if mtiles[mi][0] == "xs":
                q = mtiles[mi][1]
                S.dma("sp", xs_scr[q * 128:(q + 1) * 128, :], xsT[:, q, :], r=[B_xs[q]], w=[B_xscr[q]])
        S.rec = None
        if debug and stage == 2:
            for f_ in U3:
                f_()

        if debug and stage == 2:
            S.barrier()
            dv = dbg_d.rearrange("p (a b) -> p a b", a=8)
            B_dbg = Buf("dbg")
            srcs = [xsT[:, 0, :], xsT[:, 3, :], cqT[:, 0, :], cqT[:, 1, :], ckvT, krT, mixT[:, 1, :], mixT[:, 6, :]]
            for n_, sv in enumerate(srcs):
                for cc in range(4):
                    cs = slice(cc * 512, (cc + 1) * 512)
                    cp("dve", tmp[cc % 4], sv[:, cs], [], [TB[cc % 4]])
                    S.dma("sp", dv[:, n_, cs], tmp[cc % 4], r=[TB[cc % 4]], w=[B_dbg])
            S._wait("sp", B_dbg.w)
            return nc


        S.barrier()
        cur = [WO]

        def alloc(shape, dt):
            n = 1
            for v_ in shape:
                n *= v_
            esz = 2 if dt == BF16 else 4
            off = cur[0]
            cur[0] += (n * esz + 3) // 4 * 4
            return view(off, shape, dt)

        B_s5 = Buf("s5par")
        areT = alloc([32], F32); aimT = alloc([32], F32); ldt = alloc([32], F32); dtt = alloc([32], F32)
        lam = alloc([32], F32); th = alloc([32], F32)
        Bst = alloc([32, 16], F32); Bsw = alloc([32, 16], F32); Cst = alloc([32, 16], F32); Csw = alloc([32, 16], F32)
        s_a = [alloc([32], F32) for _ in range(8)]
        FXre = alloc([8, 32], F32); FXim = alloc([8, 32], F32); FZre = alloc([8, 32], F32); FZim = alloc([8, 32], F32)
        FXims = alloc([8, 32], F32); FZims = alloc([8, 32], F32); FZimB = alloc([8, 32], F32)
        FY9imn = alloc([9, 32], F32)
        f_t = [small[:, 448:704].rearrange("p (a b) -> p a b", a=8), small[:, 704:960].rearrange("p (a b) -> p a b", a=8)]
        Dst = alloc([32], F32)
        Pre = alloc([17, 32], F32); Pim = alloc([17, 32], F32)
        mag8 = alloc([32], F32); fr16 = alloc([32], F32)
        assert cur[0] <= WO + 24576, cur[0] - WO
        T0 = WO + 94208
        cur[0] = T0
        EL = alloc([17, 32], F32); Ee = alloc([17, 32], F32); Y1 = alloc([17, 32], F32); K1 = alloc([17, 32], I32)
        FR = alloc([17, 32], F32); AFR = alloc([17, 32], F32); SN = alloc([17, 32], F32); CS = alloc([17, 32], F32)
        assert cur[0] <= ARENA
        cur[0] = WO + 24576
        Xg = alloc([8, 8, 16], F32); Yg = alloc([8, 9, 16], F32); Zg = alloc([8, 8, 16], F32); Zs = alloc([8, 8, 16], F32)
        gt1 = alloc([8, 8, 16], F32); gt2 = alloc([8, 9, 16], F32); gt3 = Zs
        Wtoep = [alloc([8, 128], BF16) for _ in range(2)]; Wend = [alloc([8, 128], BF16) for _ in range(2)]
        Wendsw = [alloc([8, 128], BF16) for _ in range(2)]; Wc = [alloc([8, 8, 16], BF16) for _ in range(2)]
        Uu = [alloc([8, 256], BF16) for _ in range(2)]; Hprev = alloc([8, 256], BF16)
        st4 = [alloc([256], F32) for _ in range(2)]
        Gb = [alloc([256], BF16) for _ in range(2)]
        assert cur[0] <= WO + 90112
        wglu = view(WO + 90112, [4, 512], BF16)
        bglu = small[:, 404:408]
        st1 = [tmp[0][:, 0:256], tmp[0][:, 256:512]]; st2 = [tmp[1][:, 0:256], tmp[1][:, 256:512]]
        sxr = [tmp[2][:, 0:256], tmp[2][:, 256:512]]; st3 = [tmp[3][:, 0:256], tmp[3][:, 256:512]]
        B_wglu = Buf("wglu"); B_bglu = Buf("bglu")

        SU = []
        S.rec = SU
        B_sta = [Buf("sta0"), Buf("sta1")]
        sta = [view(T0, [128], F32), view(T0 + 512, [128], F32)]
        ccs = [[view(T0 + 1024 + 1024 * q, [128], F32), view(T0 + 1536 + 1024 * q, [128], F32)] for q in range(4)]
        B_ccs = [Buf("ccs%d" % q) for q in range(4)]
        B_dst = Buf("dst"); B_bld = Buf("bld")
        for i_, src in enumerate([are_d, aim_d]):
            S.dma("sp", sta[i_][0:32, 0:64], src, w=[B_sta[i_]])
            S.dma("sp", sta[i_][0:32, 64:128], src, w=[B_sta[i_]])
        S.dma("sp", ldt, ldt_d.partition_broadcast(128)[:, 0, :], w=[B_bld])
        for q in range(4):
            S.dma("sp", ccs[q][0][:, 0:64], cre_d[q * 128:(q + 1) * 128, :], w=[B_ccs[q]])
            S.dma("sp", ccs[q][0][:, 64:128], cim_d[q * 128:(q + 1) * 128, :], w=[B_ccs[q]])
            S.dma("sp", ccs[q][1][:, 0:64], cim_d[q * 128:(q + 1) * 128, :], w=[B_ccs[q]])
            S.dma("sp", ccs[q][1][:, 64:128], cre_d[q * 128:(q + 1) * 128, :], w=[B_ccs[q]])
        S.dma("sp", Bst[0:64], bre_d.rearrange("g p c -> p g c"), w=[B_bld])
        S.dma("sp", Bst[64:128], bim_d.rearrange("g p c -> p g c"), w=[B_bld])
        S.dma("sp", Bsw[0:64], bim_d.rearrange("g p c -> p g c"), w=[B_bld])
        S.dma("sp", Bsw[64:128], bre_d.rearrange("g p c -> p g c"), w=[B_bld])
        for t_ in range(8):
            S.dma("sp", Dst[t_ * 16:(t_ + 1) * 16, :], dsk_d.rearrange("g c -> c g"), w=[B_dst],
                  allow_slow_non_contiguous=True)
        for i_, dstT in enumerate([areT, aimT]):
            S.op("pe", lambda e, i_=i_: e.transpose(out=psum[i_][:, 0:32], in_=sta[i_][0:32, :], identity=ident_f[0:32, 0:32]),
                 [B_sta[i_], B_const], [PB[i_]])
            cp("dve", dstT, psum[i_][:, 0:32], [PB[i_]], [B_s5])
        Cst_f = Cst.rearrange("p g c -> p (g c)")
        Csw_f = Csw.rearrange("p g c -> p (g c)")
        for q in range(4):
            pa, pb_ = 0, 1
            tr(psum[pa][:, 0:128], ccs[q][0], [B_ccs[q], B_const], [PB[pa]])
            tr(psum[pb_][:, 0:128], ccs[q][1], [B_ccs[q]], [PB[pb_]])
            cp("dve", Cst_f[:, q * 128:(q + 1) * 128], psum[pa][:, 0:128], [PB[pa]], [B_s5])
            cp("act", Csw_f[:, q * 128:(q + 1) * 128], psum[pb_][:, 0:128], [PB[pb_]], [B_s5])
        ts("dve", Cst[64:128], Cst[64:128], -1.0, None, ALU.mult, None, [B_s5], [B_s5])
        for bb in B_sta + B_ccs:
            B_s5.r.extend(bb.r)

        P5 = [B_s5, B_const, B_bld]

        def v(out, a, b, op, eng="dve"):
            tt(eng, out, a, b, op, P5, [B_s5])

        act(dtt, ldt, AF.Exp, P5, [B_s5])
        v(lam, dtt, areT, ALU.mult)
        v(th, dtt, aimT, ALU.mult)
        bc17 = lambda a: a.unsqueeze(1).to_broadcast([128, 17, 32])
        kvb = cf[:, 512:529].unsqueeze(2).to_broadcast([128, 17, 32])
        kv2b = cf[:, 529:546].unsqueeze(2).to_broadcast([128, 17, 32])
        v(EL, bc17(lam), kvb, ALU.mult)
        act(Ee, EL, AF.Exp, P5, [B_s5])
        v(Y1, bc17(th), kv2b, ALU.mult)
        cp("dve", K1, Y1, P5, [B_s5])
        v(FR, Y1, K1, ALU.subtract)
        act(AFR, FR, AF.Abs, P5, [B_s5])
        act(SN, FR, AF.Sin, P5, [B_s5], scale=C2PI)
        act(CS, AFR, AF.Sin, P5, [B_s5], scale=-C2PI, bias=cf[:, 548:549])
        v(Pre, Ee, CS, ALU.mult)
        v(Pim, Ee, SN, ALU.mult)
        cp("dve", mag8, Ee[:, 16, :], P5, [B_s5])
        cp("dve", fr16, FR[:, 16, :], P5, [B_s5])
        abre = Pre[:, 9, :]; abim = Pim[:, 9, :]
        nr, den, rden, t_a, t_b, kre, kim, t_c = s_a
        ts("dve", nr, abre, -1.0, None, ALU.add, None, P5, [B_s5])
        v(den, areT, areT, ALU.mult)
        v(t_a, aimT, aimT, ALU.mult)
        v(den, den, t_a, ALU.add)
        S.op("dve", lambda e: e.reciprocal(out=rden, in_=den), P5, [B_s5])
        v(t_a, nr, areT, ALU.mult); v(t_b, abim, aimT, ALU.mult); v(t_a, t_a, t_b, ALU.add); v(kre, t_a, rden, ALU.mult)
        v(t_a, abim, areT, ALU.mult); v(t_b, nr, aimT, ALU.mult); v(t_a, t_a, t_b, ALU.subtract); v(kim, t_a, rden, ALU.mult)
        bc8 = lambda a: a.unsqueeze(1).to_broadcast([128, 8, 32])
        v(FXre, Pre[:, 0:8, :], bc8(kre), ALU.mult); v(f_t[0], Pim[:, 0:8, :], bc8(kim), ALU.mult)
        v(FXre, FXre, f_t[0], ALU.subtract)
        v(FXim, Pre[:, 0:8, :], bc8(kim), ALU.mult); v(f_t[0], Pim[:, 0:8, :], bc8(kre), ALU.mult)
        v(FXim, FXim, f_t[0], ALU.add)
        a7re = bc8(Pre[:, 15, :]); a7im = bc8(Pim[:, 15, :])
        v(FZre, FXre, a7re, ALU.mult); v(f_t[0], FXim, a7im, ALU.mult); v(FZre, FZre, f_t[0], ALU.subtract)
        v(FZim, FXre, a7im, ALU.mult); v(f_t[0], FXim, a7re, ALU.mult); v(FZim, FZim, f_t[0], ALU.add)
        ts("dve", FXims, FXim, cf[:, 547:548], None, ALU.mult, None, P5, [B_s5])
        ts("dve", FZims, FZim, cf[:, 547:548], None, ALU.mult, None, P5, [B_s5])
        ts("dve", FZimB, FZim, cf[:, 549:550], None, ALU.mult, None, P5, [B_s5])
        ts("dve", FY9imn, Pim[:, 8:17, :], -1.0, None, ALU.mult, None, P5, [B_s5])
        S.rec = None
        nu_, ns_ = len(U3), len(SU)
        iu = is_ = 0
        while iu < nu_ or is_ < ns_:
            if is_ >= ns_ or (iu < nu_ and iu * ns_ <= is_ * nu_):
                U3[iu](); iu += 1
            else:
                SU[is_](); is_ += 1
        S.barrier()
        S.dma("pool", wglu, wglu_d.rearrange("(k p) n -> p k n", p=128), w=[B_wglu])
        S.dma("sp", bglu, bglu_d.rearrange("(k p) o -> p (k o)", p=128), w=[B_bglu], allow_slow_non_contiguous=True)
        tabC = view(T0, [8, 256], F32)
        tabS = view(T0 + 8192, [8, 256], F32)

        B_X = Buf("genX"); B_Y = Buf("genY"); B_Z = Buf("genZ"); B_Zs = Buf("genZs"); B_gt = Buf("gt"); B_gt1 = Buf("gt1")
        B_W = [Buf("s5w0"), Buf("s5w1")]
        B_Wc = [Buf("s5wc0"), Buf("s5wc1")]
        B_U = [[Buf("U%d_%d" % (w_, g)) for g in range(8)] for w_ in range(2)]
        B_H = [Buf("H%d" % g) for g in range(8)]
        B_tab = [Buf("tab%d" % i) for i in range(8)]
        B_tt = Buf("tabtmp")
        B_st1 = [Buf("st1_0"), Buf("st1_1")]; B_st2 = [Buf("st2_0"), Buf("st2_1")]; B_sxr = [Buf("sxr0"), Buf("sxr1")]
        B_st3 = [Buf("st3_0"), Buf("st3_1")]; B_st4 = [Buf("st4_0"), Buf("st4_1")]; B_yp = [Buf("yp0"), Buf("yp1")]
        B_G = [Buf("G0"), Buf("G1")]
        PBG = [PB[2], PB[3]]; PBY = [PB[6], PB[7]]
        psG = [psum[2][:, 0:256], psum[3][:, 0:256]]
        psY = [psum[6][:, 0:256], psum[7][:, 0:256]]

        def gen(eng, out, Mst, Msw, Fre, Fim, gs, wbuf, to_tmp=False, nj=8):
            in0 = Mst[:, gs, :].unsqueeze(2).to_broadcast([128, 8, nj, 16])
            in0s = Msw[:, gs, :].unsqueeze(2).to_broadcast([128, 8, nj, 16])
            f1 = Fre[:, :, gs].rearrange("p j g -> p g j").unsqueeze(3).to_broadcast([128, 8, nj, 16])
            f2 = Fim[:, :, gs].rearrange("p j g -> p g j").unsqueeze(3).to_broadcast([128, 8, nj, 16])
            if to_tmp:
                tt(eng, gt1, in0, f1, ALU.mult, P5, [B_gt1])
                tt(eng, gt3, in0s, f2, ALU.mult, P5, [B_gt1])
                tt(eng, out, gt1, gt3, ALU.add, [B_gt1], [wbuf])
            else:
                tt(eng, out, in0, f1, ALU.mult, P5, [wbuf])
                g2v = gt2[:, :, 0:nj, :]
                tt(eng, g2v, in0s, f2, ALU.mult, P5, [B_gt])
                tt(eng, out, out, g2v, ALU.add, [B_gt], [wbuf])

        tmask4 = cf[:, 128:256].unsqueeze(1).to_broadcast([128, 4, 128])

        def uload(qb):
            ws = qb % 2
            for t_ in range(8):
                S.dma("sp", Uu[ws][t_ * 16:(t_ + 1) * 16, :, :], xs_scr_v[t_][:, 8 * qb:8 * qb + 8, :],
                      r=[B_xscr[qb]], w=B_U[ws])

        def gen_pool_xyz(qb):
            gs = slice(qb * 8, (qb + 1) * 8)
            gen("dve", Xg, Bst, Bsw, FXre, FXims, gs, B_X)
            gen("dve", Yg, Cst, Csw, Pre[:, 8:17, :], FY9imn, gs, B_Y, nj=9)
            gen("dve", Zg, Bst, Bsw, FZre, FZims, gs, B_Z)

        def gen_pool_wc(qb):
            gs = slice(qb * 8, (qb + 1) * 8)
            cp("act", Wc[qb % 2], Yg[:, :, 1:9, :], [B_Y], [B_Wc[qb % 2]])

        def gen_pe(qb):
            ws = qb % 2
            for hb in range(2):
                pb = 4 + hb
                for j in range(4):
                    g = hb * 4 + j
                    S.op("pe", lambda e: e.matmul(psum[pb][:, j * 128:(j + 1) * 128],
                                                  lhsT=Xg[:, g].rearrange("p a b -> p (a b)"),
                                                  rhs=Yg[:, g, 0:8, :].rearrange("p a b -> p (a b)"), start=True, stop=True),
                         [B_X, B_Y], [PB[pb]], inc=(j == 3))
                tt("dve", Wtoep[ws][:, hb * 4:hb * 4 + 4, :], psum[pb].rearrange("p (a b) -> p a b", a=4), tmask4,
                   ALU.mult, [PB[pb], B_const], [B_W[ws]])
            for hb in range(2):
                pb = 4 + hb
                for j in range(4):
                    g = hb * 4 + j
                    tr(psum[pb][:, j * 128:(j + 1) * 128], Zg[:, g].rearrange("p a b -> p (a b)"),
                       [B_Z, B_const], [PB[pb]], inc=(j == 3))
                pv = psum[pb].rearrange("p (a b) -> p a b", a=4)
                cp("act", Wend[ws][:, hb * 4:hb * 4 + 4, :], pv, [PB[pb]], [B_W[ws]])
                cp("act", Wendsw[ws][:, hb * 4:hb * 4 + 4, 0:64], pv[:, :, 64:128], [PB[pb]], [B_W[ws]])
                cp("act", Wendsw[ws][:, hb * 4:hb * 4 + 4, 64:128], pv[:, :, 0:64], [PB[pb]], [B_W[ws]])

        def tables(g, part=0):
            sl8 = g % 8
            tS = tabS[:, sl8, :]; tC = tabC[:, sl8, :]
            bt_ = [B_tab[sl8]]
            if part in (0, 1):
                act(tS, cf[:, 256:512], AF.Identity, P5, bt_, scale=fr16[:, g:g + 1])
                act(tC.bitcast(I32), cf[:, 256:512], AF.Identity, P5, bt_, scale=fr16[:, g:g + 1])
            if part in (0, 2):
                tt("dve", tS, tS, tC.bitcast(I32), ALU.subtract, bt_, bt_)
                act(tC, tS, AF.Abs, bt_, bt_)
                act(tC, tC, AF.Sin, bt_ + [B_const], bt_, scale=-C2PI, bias=cf[:, 548:549])
                act(tS, tS, AF.Sin, bt_ + [B_const], bt_, scale=cf[:, 551:552])

        def s0(g):
            qb, gl = divmod(g, 8)
            ws = qb % 2; sl = g % 2
            mm(psum[sl][:, 0:256], Wend[ws][:, gl, :], Uu[ws][:, gl, :], True, True, [B_W[ws], B_U[ws][gl]], [PB[sl]], inc=False)
            mm(psum[sl][:, 256:512], Wendsw[ws][:, gl, :], Uu[ws][:, gl, :], True, True, [B_W[ws], B_U[ws][gl]], [PB[sl]])

        def s1(g):
            sl = g % 2; sl8 = g % 8
            tt("dve", st1[sl], psum[sl][:, 0:256], tabC[:, sl8, :], ALU.mult, [PB[sl], B_tab[sl8]], [B_st1[sl]])
            tt("dve", st2[sl], psum[sl][:, 256:512], tabS[:, sl8, :], ALU.mult, [PB[sl], B_tab[sl8]], [B_st2[sl]])
            tt("dve", sxr[sl], st1[sl], st2[sl], ALU.add, [B_st1[sl], B_st2[sl]], [B_sxr[sl]])
            S.op("dve", lambda e: e.tensor_tensor_scan(out=Gb[sl], data0=mag8[:, g:g + 1].to_broadcast([128, 256]),
                                                       data1=sxr[sl], initial=0.0, op0=ALU.mult, op1=ALU.add),
                 [B_sxr[sl], B_s5], [B_G[sl]])

        def s2(g):
            qb, gl = divmod(g, 8)
            sl = g % 2; sl8 = g % 8
            mm(psG[sl], pswap_b, Gb[sl], True, True, [B_G[sl], B_const], [PBG[sl]])
            tt("pool", st3[sl], Gb[sl], tabC[:, sl8, :], ALU.mult, [B_G[sl], B_tab[sl8]], [B_st3[sl]])
            tt("dve", st4[sl], psG[sl], tabS[:, sl8, :], ALU.mult, [PBG[sl], B_tab[sl8]], [B_st4[sl]])
            tt("pool", Hprev[:, gl, 1:256], st3[sl][:, 0:255], st4[sl][:, 0:255], ALU.subtract,
               [B_st3[sl], B_st4[sl]], [B_H[gl]])

        def s3(g):
            qb, gl = divmod(g, 8)
            ws = qb % 2; sl = g % 2
            mm(psY[sl], Wtoep[ws][:, gl, :], Uu[ws][:, gl, :], True, False, [B_W[ws], B_U[ws][gl]], [PBY[sl]], inc=False)
            mm(psY[sl], Wc[ws][:, gl].rearrange("p a b -> p (a b)"), Hprev[:, gl, :], False, True,
               [B_Wc[ws], B_H[gl]], [PBY[sl]])
            S.op("dve", lambda e: e.scalar_tensor_tensor(out=Uu[ws][:, gl, :], in0=Uu[ws][:, gl, :], scalar=Dst[:, g:g + 1],
                                                          in1=psY[sl], op0=ALU.mult, op1=ALU.add),
                 [PBY[sl], B_dst], [B_U[ws][gl]])

        def writeback(qb):
            ws = qb % 2
            for t_ in range(8):
                S.dma("sp", y_scr_v[t_][:, 8 * qb:8 * qb + 8, :], Uu[ws][t_ * 16:(t_ + 1) * 16, :, :],
                      r=B_U[ws], w=[B_yscr[qb]])
            S.dma("sp", xsT[:, qb, :], y_scr[qb * 128:(qb + 1) * 128, :], r=[B_yscr[qb]], w=[B_xs[qb]])

        for gl_ in range(8):
            S.op("pool", lambda e, gl_=gl_: e.memset(Hprev[:, gl_, 0:1], 0.0), [], [B_H[gl_]])
        NG = 32
        uload(0); gen_pool_xyz(0); gen_pool_wc(0)
        for g in range(4):
            tables(g)
        tables(4, part=1)
        gen_pe(0)
        for step in range(NG + 4):
            if 0 <= step - 3 < NG:
                s3(step - 3)
                if (step - 3) % 8 == 7:
                    writeback((step - 3) // 8)
            if 0 <= step - 2 < NG:
                s2(step - 2)
            if 4 <= step + 3 < NG:
                tables(step + 3, part=2)
            if 4 <= step + 4 < NG:
                tables(step + 4, part=1)
            if 0 <= step - 1 < NG:
                s1(step - 1)
            nb = (step + 7) // 8
            if step + 7 == 8 * nb and nb < 4:
                gen_pool_xyz(nb)
            if step + 5 == 8 * nb and nb < 4 and nb >= 1:
                uload(nb)
                gen_pool_wc(nb)
            if step + 2 == 8 * ((step + 2) // 8) and 1 <= (step + 2) // 8 < 4:
                gen_pe((step + 2) // 8)
            if step < NG:
                s0(step)


        hbglu = small[:, 400:404]
        B_hb = Buf("hbglu")

        def glu_pre():
            for q in range(4):
                for cc in range(4):
                    cs_ = slice(cc * 512, (cc + 1) * 512)
                    act(xsT[:, q, cs_], xsT[:, q, cs_], AF.Gelu, [], [B_xs[q]])
            ts("dve", hbglu, bglu, 0.5, None, ALU.mult, None, [B_bglu], [B_hb])

        def glu_units():
            for m in range(4):
                for cc in range(4):
                    c0 = cc * 64
                    pb = 7
                    ncs = slice(cc * 512, (cc + 1) * 512)
                    yv = lambda k_: xsT[:, k_, :].rearrange("p (t c) -> p t c", t=8)[:, :, c0:c0 + 64]
                    for k in range(4):
                        mm(psum[pb][:, :], wglu[:, k, m * 128:(m + 1) * 128], yv(k), k == 0, k == 3,
                           [B_wglu, B_xs[k]], [PB[pb]])
                    act(tmp[0], psum[pb][:, :], AF.Tanh, [PB[pb], B_hb], [TB[0]], scale=0.5, bias=hbglu[:, m:m + 1])
                    t0v = tmp[0].rearrange("p (t c) -> p t c", t=8)
                    t1v = tmp[1].rearrange("p (t c) -> p t c", t=8)
                    ym_ = yv(m)
                    S.op("dve", lambda e, t0v=t0v, t1v=t1v, ym_=ym_: e.scalar_tensor_tensor(
                        out=t1v, in0=t0v, scalar=1.0, in1=ym_, op0=ALU.add, op1=ALU.mult), [B_xs[m], TB[0]], [TB[1]])
                    mv2 = mixT[:, m, ncs].rearrange("p (c t) -> p c t", t=8)
                    t1c = tmp[1].rearrange("p (t c) -> p c t", t=8)
                    S.op("dve", lambda e, mv2=mv2, t1c=t1c: e.scalar_tensor_tensor(
                        out=mv2, in0=t1c, scalar=0.5, in1=mv2, op0=ALU.mult, op1=ALU.mult), [TB[1]], [B_mix[m][cc]])

        def glu_all():
            glu_pre()
            glu_units()

        if debug and stage == 4:
            glu_all()
            S.barrier()
            dv = dbg_d.rearrange("p (a b) -> p a b", a=8)
            B_dbg = Buf("dbg")
            srcs = [mixT[:, 0, :], mixT[:, 1, :], mixT[:, 2, :], mixT[:, 3, :], xsT[:, 0, :], xsT[:, 1, :], xsT[:, 2, :], xsT[:, 3, :]]
            for n_, sv in enumerate(srcs):
                for cc in range(4):
                    cs = slice(cc * 512, (cc + 1) * 512)
                    cp("dve", tmp[cc % 4], sv[:, cs], [], [TB[cc % 4]])
                    S.dma("sp", dv[:, n_, cs], tmp[cc % 4], r=[TB[cc % 4]], w=[B_dbg])
            S._wait("sp", B_dbg.w)
            return nc


        S.barrier()
        if debug and stage == 40:
            glu_all()
            return nc
        cur[0] = WO
        qnT = alloc([4, L], BF16); knT = alloc([4, L], BF16); Vv = alloc([16, 512], BF16); qrT = alloc([2, L], BF16)
        wuq_f = alloc([2, 768], F32); wuq = alloc([2, 768], BF16); wqr = alloc([2, 4, 128], BF16); qg = alloc([2], F32)
        wukv_f = alloc([1024], F32); wukv = alloc([1024], BF16); kvg = alloc([1], F32)
        sq = [alloc([3, 512], BF16) for _ in range(2)]
        rstdq = [alloc([512], F32) for _ in range(2)]
        rstdk = [alloc([512], F32) for _ in range(2)]
        rkt = [alloc([4], F32) for _ in range(2)]
        B_wq = Buf("wuq"); B_wkv = Buf("wukv")
        B_sq = [Buf("sq0"), Buf("sq1")]
        B_rq = [Buf("rq0"), Buf("rq1")]
        B_rk = [Buf("rk0"), Buf("rk1")]
        B_rkt = [Buf("rkt0"), Buf("rkt1")]
        B_qn = [[Buf("qn%d_%d" % (h, c)) for c in range(4)] for h in range(4)]
        B_kn = [[Buf("kn%d_%d" % (h, c)) for c in range(4)] for h in range(4)]
        B_qr = [[Buf("qr%d_%d" % (h, c)) for c in range(4)] for h in range(2)]
        B_V = [Buf("V%d" % i) for i in range(16)]

        S.dma("sp", wuq_f, wuq_d.rearrange("(k p) n -> p k n", p=128), w=[B_wq])
        S.dma("sp", qg, qg_d.rearrange("(k p) o -> p (k o)", p=128), w=[B_wq], allow_slow_non_contiguous=True)
        S.dma("sp", wukv_f, wukv_d, w=[B_wkv])
        S.dma("sp", kvg, kvg_d, w=[B_wkv])
        for k in range(2):
            ts("dve", wuq[:, k, :], wuq_f[:, k, :], qg[:, k:k + 1], None, ALU.mult, None, [B_wq], [B_wq])
        for pair in range(2):
            for hh in range(2):
                base = (2 * pair + hh) * 192 + 128
                cp("pool", wqr[:, :, 2 * pair, hh * 64:hh * 64 + 64], wuq[:, :, base:base + 64], [B_wq], [B_wq])
                cp("pool", wqr[:, :, 2 * pair + 1, hh * 64:hh * 64 + 32], wuq[:, :, base + 32:base + 64], [B_wq], [B_wq])
                cp("pool", wqr[:, :, 2 * pair + 1, hh * 64 + 32:hh * 64 + 64], wuq[:, :, base:base + 32], [B_wq], [B_wq])
        ts("dve", wukv, wukv_f, kvg[:, 0:1], None, ALU.mult, None, [B_wkv], [B_wkv])
        wukv_v = wukv.rearrange("p (h x) -> p h x", h=4)[:, :, 128:256]

        def rsqrt_from(out, src_ps, scale_, r, w):
            act(out, src_ps, AF.Ln, r + [B_const], w, scale=scale_, bias=cf[:, 554:555])
            act(out, out, AF.Exp, w, w, scale=-0.5)

        rotc = [0]

        def nb_():
            pb = rotc[0] % 8
            rotc[0] += 1
            return pb

        def prep0(cc):
            cs = slice(cc * 512, (cc + 1) * 512)
            sl = cc % 2
            for j in range(2):
                act(sq[sl][:, j, :], cqT[:, j, cs], AF.Square, [B_cq[cc]], [B_sq[sl]])
            act(sq[sl][:, 2, :], ckvT[:, cs], AF.Square, [B_ckv[cc]], [B_sq[sl]])
            pbq = nb_()
            mm(psum[pbq][:, :], ones_b, sq[sl][:, 0, :], True, False, [B_const, B_sq[sl]], [PB[pbq]])
            mm(psum[pbq][:, :], ones_b, sq[sl][:, 1, :], False, True, [B_const, B_sq[sl]], [PB[pbq]])
            pbk = nb_()
            mm(psum[pbk][:, :], ones_b, sq[sl][:, 2, :], True, True, [B_const, B_sq[sl]], [PB[pbk]])
            pbt = nb_()
            for ii in range(4):
                mm(psum[pbt][:, ii:ii + 1], sq[sl][:, 2, ii * 128:(ii + 1) * 128], ones_b[:, 0:1], True, True,
                   [B_const, B_sq[sl]], [PB[pbt]], inc=(ii == 3))
            act(rstdq[sl], psum[pbq][:, :], AF.Ln, [PB[pbq], B_const], [B_rq[sl]], scale=1.0 / 256.0, bias=cf[:, 554:555])
            act(rstdk[sl], psum[pbk][:, :], AF.Ln, [PB[pbk], B_const], [B_rk[sl]], scale=1.0 / 128.0, bias=cf[:, 554:555])
            act(rkt[sl], psum[pbt][:, 0:4], AF.Ln, [PB[pbt], B_const], [B_rkt[sl]], scale=1.0 / 128.0, bias=cf[:, 554:555])
            act(rstdq[sl], rstdq[sl], AF.Exp, [], [B_rq[sl]], scale=-0.5)
            act(rstdk[sl], rstdk[sl], AF.Exp, [], [B_rk[sl]], scale=-0.5)
            act(rkt[sl], rkt[sl], AF.Exp, [], [B_rkt[sl]], scale=-0.5)

        def prep1(cc):
            cs = slice(cc * 512, (cc + 1) * 512)
            sl = cc % 2
            for h in range(4):
                pb = nb_()
                for k in range(2):
                    mm(psum[pb][:, :], wuq[:, k, h * 192:h * 192 + 128], cqT[:, k, cs], k == 0, k == 1,
                       [B_wq, B_cq[cc]], [PB[pb]])
                tt("dve", qnT[:, h, cs], psum[pb][:, :], rstdq[sl], ALU.mult, [PB[pb], B_rq[sl]], [B_qn[h][cc]])
            for pair in range(2):
                pb1 = nb_()
                pb2 = nb_()
                for k in range(2):
                    mm(psum[pb1][:, :], wqr[:, k, 2 * pair, :], cqT[:, k, cs], k == 0, k == 1, [B_wq, B_cq[cc]], [PB[pb1]])
                for k in range(2):
                    mm(psum[pb2][:, :], wqr[:, k, 2 * pair + 1, :], cqT[:, k, cs], k == 0, k == 1, [B_wq, B_cq[cc]], [PB[pb2]])
                ta = tmp[2 * pair]; tb = tmp[2 * pair + 1]
                tt("dve", ta, psum[pb1][:, :], COSr[:, cs], ALU.mult, [PB[pb1], B_rope], [TB[2 * pair]])
                tt("dve", tb, psum[pb2][:, :], SINr[:, cs], ALU.mult, [PB[pb2], B_rope], [TB[2 * pair + 1]])
                tt("dve", ta, ta, tb, ALU.add, [TB[2 * pair + 1]], [TB[2 * pair]])
                tt("dve", qrT[:, pair, cs], ta, rstdq[sl], ALU.mult, [TB[2 * pair], B_rq[sl]], [B_qr[pair][cc]])
            for h in range(4):
                pb = nb_()
                mm(psum[pb][:, :], wukv[:, h * 256:h * 256 + 128], ckvT[:, cs], True, True, [B_wkv, B_ckv[cc]], [PB[pb]])
                tt("dve", knT[:, h, cs], psum[pb][:, :], rstdk[sl], ALU.mult, [PB[pb], B_rk[sl]], [B_kn[h][cc]])
            for ii in range(4):
                i = cc * 4 + ii
                pb = nb_()
                mm(psum[pb][:, :], ckvT[:, i * 128:(i + 1) * 128], wukv_v, True, True, [B_wkv, B_ckv[cc]], [PB[pb]])
                S.op("act", lambda e: e.activation(out=Vv[:, i, :], in_=psum[pb][:, :], func=AF.Copy,
                                                   scale=rkt[sl][:, ii:ii + 1]),
                     [PB[pb], B_rkt[sl]], [B_V[i]])

        prep0(0); prep0(1); prep1(0); prep0(2); prep1(1); prep0(3); prep1(2)
        glu_pre()
        prep1(3)

        S.barrier()
        if debug and stage == 45:
            return nc
        wout = view(WO + 57344, [8, D], BF16)
        pTall = view(WO + 73728, [2, L], BF16)
        pstage = [small[:, 448:704], small[:, 704:960]]
        B_pst = [Buf("pst0"), Buf("pst1")]; B_pTall = Buf("pTall")
        wpg = view(ZO + 20480, [8, D], BF16)
        g2 = view(ZO + 36864, [D], F32); b2 = view(ZO + 40960, [D], F32)
        g1 = view(ARENA - 8192, [D], F32); b1 = view(ARENA - 4096, [D], F32)
        B_wout = Buf("wout"); B_wpg = Buf("wpg"); B_g2 = Buf("g2b2"); B_g1x = Buf("g1b1x")
        S.dma("pool", wout, wout_d.rearrange("(k p) n -> p k n", p=128), w=[B_wout])
        S.dma("pool", wpg, wpg_d.rearrange("(k p) n -> p k n", p=128), w=[B_wpg])
        S.dma("sp", g2, g2_d.partition_broadcast(128)[:, 0, :], w=[B_g2])
        S.dma("sp", b2, b2_d.partition_broadcast(128)[:, 0, :], w=[B_g2])

        krB = view(WO + 81920, [L], BF16)
        B_krB = Buf("krB")
        cp("pool", krB[64:128, :], krT[64:128, :], B_kr, [B_krB])
        S.op("pool", lambda e: e.memset(krB[0:64, :], 0.0), [], [B_krB])
        S.op("pool", lambda e: e.memset(krT[64:128, :], 0.0), [B_krB], B_kr)
        attops = []
        S.rec = attops
        items = []
        for h in range(4):
            for Q in range(4):
                for j in range(4 * Q + 4):
                    items.append((h, Q, j))

        def emit_S(idx):
            h, Q, j = items[idx]
            a = j - 4 * Q
            c0 = 128 * a if a > 0 else 0
            pb = idx % 3
            ptb = idx % 3
            hb = 64 * (h % 2)
            pair = h // 2
            qs = slice(Q * 512 + c0, (Q + 1) * 512)
            ks = slice(j * 128, (j + 1) * 128)
            mm(psum[pb][:, c0:512], knT[:, h, ks], qnT[:, h, qs], True, False, [B_kn[h][j // 4], B_qn[h][Q]], [PB[pb]], inc=False)
            mm(psum[pb][:, c0:512], (krT if h % 2 == 0 else krB)[:, ks], qrT[:, pair, qs], False, a < 0,
               [B_kr[j // 4], B_krB, B_qr[pair][Q]], [PB[pb]], inc=(a < 0))
            if a >= 0:
                mm(psum[pb][:, c0:c0 + 128], ident_b, trimask, False, True, [B_const], [PB[pb]])
            act(PT[ptb][:, c0:512], psum[pb][:, c0:512], AF.Exp, [PB[pb]], [PTB[ptb]], scale=SCALE)

        def emit_PV(idx):
            h, Q, j = items[idx]
            a = j - 4 * Q
            c0 = 128 * a if a > 0 else 0
            ptb = idx % 3
            hq = h * 4 + Q
            po = 3 + hq % 2
            pl = 5 + hq % 2
            last = (j == 4 * Q + 3)
            mm(psum[po][:, c0:512], Vv[:, j, h * 128:(h + 1) * 128], PT[ptb][:, c0:512], j == 0, last,
               [B_V[j], PTB[ptb]], [PB[po]], inc=last)
            mm(psum[pl][:, c0:512], ones_b, PT[ptb][:, c0:512], j == 0, last, [B_const, PTB[ptb]], [PB[pl]], inc=True)
            if last:
                Qs = slice(Q * 512, (Q + 1) * 512)
                S.op("dve", lambda e: e.reciprocal(out=tmp[2], in_=psum[pl][:, :]), [PB[pl]], [TB[2]])
                tt("dve", tmp[3], psum[po][:, :], tmp[2], ALU.mult, [PB[po], TB[2]], [TB[3]])
                tt("pool", mixT[:, 4 + h, Qs], tmp[3], mixT[:, 4 + h, Qs], ALU.mult, [TB[3]], [B_mix[4 + h][Q]])

        for idx in range(len(items)):
            emit_S(idx)
            if idx > 1:
                emit_PV(idx - 2)
        emit_PV(len(items) - 2)
        emit_PV(len(items) - 1)
        gluops = []
        S.rec = gluops
        for i in range(16):
            s_ = i % 2
            S.dma("sp", pstage[s_], p_d[i * 128:(i + 1) * 128, :], w=[B_pst[s_]])
            for k in range(2):
                tr(psum[7][:, k * 128:(k + 1) * 128], pstage[s_][:, k * 128:(k + 1) * 128], [B_pst[s_], B_const], [PB[7]], inc=(k == 1))
            cp("act", pTall[:, :, i * 128:(i + 1) * 128], psum[7][:, 0:256].rearrange("p (a b) -> p a b", a=2),
               [PB[7]], [B_pTall])
        glu_units()
        S.rec = None
        na_, ng_ = len(attops), len(gluops)
        ia = ig = 0
        while ia < na_ or ig < ng_:
            if ig >= ng_ or (ia < na_ and ia * ng_ <= ig * na_):
                attops[ia](); ia += 1
            else:
                gluops[ig](); ig += 1
        S.dma("sp", g1, g1_d.partition_broadcast(128)[:, 0, :], w=[B_g1x, B_wglu, B_bglu])
        S.dma("sp", b1, b1_d.partition_broadcast(128)[:, 0, :], w=[B_g1x, B_wglu, B_bglu])

        if debug and stage == 5:
            S.barrier()
            dv = dbg_d.rearrange("p (a b) -> p a b", a=8)
            B_dbg = Buf("dbg")
            srcs = [mixT[:, 4, :], mixT[:, 5, :], mixT[:, 6, :], mixT[:, 7, :], qnT[:, 0, :], knT[:, 0, :], qrT[:, 0, :], Vv.rearrange("p a b -> p (a b)")[:, 0:2048]]
            for n_, sv in enumerate(srcs):
                for cc in range(4):
                    cs = slice(cc * 512, (cc + 1) * 512)
                    cp("dve", tmp[cc % 4], sv[:, cs], [], [TB[cc % 4]])
                    S.dma("sp", dv[:, n_, cs], tmp[cc % 4], r=[TB[cc % 4]], w=[B_dbg])
            S._wait("sp", B_dbg.w)
            return nc

        S.barrier()
        xt = [view(ZO + 4096 * i, [D], F32) for i in range(4)]
        xn = [view(WO + 4096 * i, [D], F32) for i in range(5)]
        uu = [view(WO + 20480 + 4096 * i, [D], F32) for i in range(6)]
        uTb = [view(WO + 45056 + 2048 * i, [8, 128], BF16) for i in range(2)]
        gate = [view(WO + 81920 + 4096 * i, [D], F32) for i in range(5)] + [view(WO + 49152 + 4096 * i, [D], F32) for i in range(2)]
        assert WO + 81920 + 5 * 4096 <= ARENA - 8192
        wpp = view(36096, [2, D], BF16)
        B_xt = [Buf("fxt%d" % i) for i in range(4)]; B_xn = [Buf("fxn%d" % i) for i in range(5)]
        B_u = [Buf("u%d" % i) for i in range(6)]; B_gate = [Buf("gate%d" % i) for i in range(7)]
        B_uT = [Buf("uT0"), Buf("uT1")]; B_wpp = Buf("wpp")
        B_out = [Buf("out%d" % i) for i in range(7)]
        st1_ = [small[:, 20 * i:20 * i + 20] for i in range(4)]
        st2_ = [small[:, 80 + 20 * i:100 + 20 * i] for i in range(4)]
        B_s1 = [Buf("fs1_%d" % i) for i in range(4)]; B_s2 = [Buf("fs2_%d" % i) for i in range(4)]
        S.dma("pool", wpp, wpp_d.rearrange("(k p) n -> p k n", p=128), w=[B_wpp])

        def tsl(i):
            return slice(i * 128, (i + 1) * 128)

        def bn(xin, stat, r, rb):
            st = stat[:, 0:12].rearrange("p (a b) -> p a b", a=2)
            S.op("dve", lambda e: e.bn_stats(out=st[:, 0, :], in_=xin[:, 0:512]), r, [rb])
            S.op("dve", lambda e: e.bn_stats(out=st[:, 1, :], in_=xin[:, 512:1024]), r, [rb])
            S.op("dve", lambda e: e.bn_aggr(out=stat[:, 12:14], in_=stat[:, 0:12]), [rb], [rb])

        def lnexp(stat, rb):
            act(stat[:, 14:15], stat[:, 13:14], AF.Ln, [rb, B_const], [rb], bias=cf[:, 553:554])
            act(stat[:, 15:16], stat[:, 14:15], AF.Exp, [rb], [rb], scale=-0.5)

        def nmr(stat, rb):
            ts("dve", stat[:, 16:17], stat[:, 12:13], stat[:, 15:16], -1.0, ALU.mult, ALU.mult, [rb], [rb])

        def m0(i):
            S.dma("sp", xt[i % 4], x_d[tsl(i), :], w=[B_xt[i % 4]])
            bn(xt[i % 4], st1_[i % 4], [B_xt[i % 4]], B_s1[i % 4])

        def m1(i):
            lnexp(st1_[i % 4], B_s1[i % 4])

        def m2(i):
            nmr(st1_[i % 4], B_s1[i % 4])

        def m3(i):
            st = st1_[i % 4]
            act(xn[i % 5], xt[i % 4], AF.Identity, [B_xt[i % 4], B_s1[i % 4]], [B_xn[i % 5]], scale=st[:, 15:16], bias=st[:, 16:17])

        def m4(i):
            tt("dve", xn[i % 5], xn[i % 5], g1, ALU.mult, [B_g1x], [B_xn[i % 5]])

        def m5(i):
            tt("pool", xn[i % 5], xn[i % 5], b1, ALU.add, [B_g1x], [B_xn[i % 5]])

        def m6(i):
            for hf in range(2):
                hs = slice(hf * 512, (hf + 1) * 512)
                for k in range(8):
                    mm(psum[hf][:, :], mixT[:, k, tsl(i)], wout[:, k, hs], k == 0, k == 7,
                       [B_mix[k][i // 4], B_wout], [PB[hf]])

        def m7(i):
            for hf in range(2):
                hs = slice(hf * 512, (hf + 1) * 512)
                S.op("dve", lambda e: e.scalar_tensor_tensor(out=uu[i % 6][:, hs], in0=xn[i % 5][:, hs], scalar=ALPHA,
                                                              in1=psum[hf][:, :], op0=ALU.mult, op1=ALU.add),
                     [B_xn[i % 5], PB[hf]], [B_u[i % 6]])

        def m8(i):
            for hb in range(2):
                pb = 2 + hb
                for j in range(4):
                    k = hb * 4 + j
                    tr(psum[pb][:, j * 128:(j + 1) * 128], uu[i % 6][:, k * 128:(k + 1) * 128], [B_u[i % 6], B_const], [PB[pb]], inc=(j == 3))

        def m9(i):
            for hb in range(2):
                pb = 2 + hb
                cp("act", uTb[i % 2][:, hb * 4:hb * 4 + 4, :], psum[pb].rearrange("p (a b) -> p a b", a=4),
                   [PB[pb]], [B_uT[i % 2]])

        def m10(i):
            for hf in range(2):
                hs = slice(hf * 512, (hf + 1) * 512)
                pb = 4 + hf
                for k in range(8):
                    mm(psum[pb][:, :], uTb[i % 2][:, k, :], wpg[:, k, hs], k == 0, k == 7, [B_uT[i % 2], B_wpg], [PB[pb]])

        def m11(i):
            for hf in range(2):
                hs = slice(hf * 512, (hf + 1) * 512)
                act(gate[i % 7][:, hs], psum[4 + hf][:, :], AF.Sigmoid, [PB[4 + hf]], [B_gate[i % 7]])
            for hf in range(2):
                hs = slice(hf * 512, (hf + 1) * 512)
                pb = 6 + hf
                for k in range(2):
                    mm(psum[pb][:, :], pTall[:, k, tsl(i)], wpp[:, k, hs], k == 0, k == 1, [B_pTall, B_wpp], [PB[pb]])

        def m12(i):
            gt_ = gate[i % 7]
            for hf in range(2):
                hs = slice(hf * 512, (hf + 1) * 512)
                tt("dve", gt_[:, hs], gt_[:, hs], psum[6 + hf][:, :], ALU.mult, [PB[6 + hf]], [B_gate[i % 7]])
            tt("dve", gt_, gt_, uu[i % 6], ALU.add, [B_u[i % 6]], [B_gate[i % 7]])
            bn(gt_, st2_[i % 4], [B_gate[i % 7]], B_s2[i % 4])

        def m13(i):
            lnexp(st2_[i % 4], B_s2[i % 4])

        def m14(i):
            nmr(st2_[i % 4], B_s2[i % 4])

        def m15(i):
            st = st2_[i % 4]
            act(gate[i % 7], gate[i % 7], AF.Identity, [B_s2[i % 4]], [B_gate[i % 7]], scale=st[:, 15:16], bias=st[:, 16:17])

        def m16(i):
            tt("dve", gate[i % 7], gate[i % 7], g2, ALU.mult, [B_g2], [B_gate[i % 7]])

        def m17(i):
            tt("pool", gate[i % 7], gate[i % 7], b2, ALU.add, [B_g2], [B_gate[i % 7]])
            S.dma("sp", out_d[tsl(i), :], gate[i % 7], r=[B_gate[i % 7]], w=[B_out[i % 7]])

        stages = [m0, m1, m2, m3, m4, m5, m6, m7, m8, m9, m10, m11, m12, m13, m14, m15, m16, m17]
        for step in range(16 + len(stages) - 1):
            for si in range(len(stages) - 1, -1, -1):
                i = step - si
                if 0 <= i < 16:
                    stages[si](i)
        for bo in B_out:
            S._wait("sp", bo.w)
    return nc


def make_in_maps(inputs):
    cf, cb = host_consts()
    maps = []
    f = lambda a: np.ascontiguousarray(np.asarray(a, dtype=np.float32))
    for b in range(NCORES):
        m = {
            "x": f(inputs["x"][b]), "p": f(inputs["p"][0, b]),
            "pos": np.ascontiguousarray(np.asarray(inputs["positions"][b], dtype=np.int32).reshape(1, L)),
            "cf": cf, "cb": cb,
            "ln_emb_g": f(inputs["ln_emb_g"]).reshape(1, D), "ln_emb_b": f(inputs["ln_emb_b"]).reshape(1, D),
            "ln_g": f(inputs["ln_g"][0]).reshape(1, D), "ln_b": f(inputs["ln_b"][0]).reshape(1, D),
            "w_in": f(inputs["w_in"][0]),
            "a_re": f(inputs["a_re"][0]), "a_im": f(inputs["a_im"][0]), "log_dt": f(inputs["log_dt"][0]).reshape(1, 32),
            "b_re": f(inputs["b_re"][0]), "b_im": f(inputs["b_im"][0]),
            "c_re": f(inputs["c_re"][0]).reshape(512, 64), "c_im": f(inputs["c_im"][0]).reshape(512, 64),
            "d_skip": f(inputs["d_skip"][0]).reshape(32, 16),
            "w_glu": f(inputs["w_glu"][0]), "b_glu": f(inputs["b_glu"][0]).reshape(512, 1),
            "q_norm_g": f(inputs["q_norm_g"][0]).reshape(256, 1), "w_uq": f(inputs["w_uq"][0]),
            "kv_norm_g": f(inputs["kv_norm_g"][0]).reshape(128, 1), "w_ukv": f(inputs["w_ukv"][0]),
            "w_out": f(inputs["w_out"][0]), "w_pg": f(inputs["w_pg"][0]), "w_pp": f(inputs["w_pp"][0]),
        }
        maps.append(m)
    return maps


def kernel(**inputs):
    nc = build()
    in_maps = make_in_maps(inputs)
    res = run_bass_kernel_spmd(nc, in_maps, core_ids=list(range(NCORES)))
    out = np.stack([np.asarray(r["out"], dtype=np.float32) for r in res.results], axis=0)
    return out
```
